# Optimizing a Trainium2 kernel written in Bass

```python
import math
import jax
import jax.numpy as jnp
from jax import lax
import numpy as np

D_MODEL = 1024
BATCH = 4
SEQ = 8192
DEPTH = 1

D_MIX = D_MODEL
D_CONV = D_MIX // 2
CONV_GROUPS = 8
CONV_GROUP_DIM = D_CONV // CONV_GROUPS
CONV_K_MIX = 3
DN_HEADS = 4
DN_HEAD_DIM = (D_MIX - D_CONV) // DN_HEADS
DN_QK = DN_HEADS * DN_HEAD_DIM
DN_V = DN_HEADS * DN_HEAD_DIM
CONV_K_QKV = 4
CHUNK = 64
FFN_HIDDEN = ((8 * D_MODEL // 3 + 255) // 256) * 256
N_MOD = 6
EPS = 1e-6
P_IN = 3 * D_CONV + 2 * DN_QK + 2 * DN_V + 2 * DN_HEADS

kernel_name = 'hybrid_conv_deltanet_adaln_layer'


def rms_norm(x, gain, eps=EPS):
    x32 = x.astype(jnp.float32)
    y = x32 * lax.rsqrt(jnp.mean(x32 * x32, axis=-1, keepdims=True) + eps)
    return y.astype(x.dtype) * gain


def l2_normalize(x, eps=EPS):
    x32 = x.astype(jnp.float32)
    return (x32 * lax.rsqrt(jnp.sum(x32 * x32, axis=-1, keepdims=True) + eps)).astype(x.dtype)


def adaln(x, gain, shift, scale):
    return rms_norm(x, gain) * (1 + scale[:, None, :]) + shift[:, None, :]


def causal_depthwise_conv(u, w):
    k = w.shape[0]
    s = u.shape[1]
    up = jnp.pad(u, ((0, 0), (k - 1, 0), (0, 0)))
    out = up[:, 0:s] * w[0]
    for j in range(1, k):
        out = out + up[:, j:j + s] * w[j]
    return out


def chunked_gated_delta_rule(q, k, v, g, beta):
    out_dtype = v.dtype
    f32 = jnp.float32
    b, h, t, dk = q.shape
    dv = v.shape[-1]
    n = t // CHUNK

    def blocks(a):
        return a.astype(f32).reshape(b, h, n, CHUNK, *a.shape[3:])

    q = blocks(q) * (dk ** -0.5)
    k = blocks(k)
    v = blocks(v)
    beta = blocks(beta)
    g = jnp.cumsum(blocks(g), axis=-1)
    idx = jnp.arange(CHUNK)
    causal = idx[:, None] >= idx[None, :]
    strict = idx[:, None] > idx[None, :]
    decay = jnp.exp(jnp.where(causal, g[..., :, None] - g[..., None, :], -jnp.inf))
    k_beta = k * beta[..., None]
    lower = jnp.where(strict, jnp.einsum('bhncd,bhnsd->bhncs', k_beta, k) * decay, 0.0)
    tri = lower + jnp.eye(CHUNK, dtype=f32)
    rhs = jnp.concatenate([v * beta[..., None], k_beta * jnp.exp(g)[..., None]], axis=-1)
    sol = lax.linalg.triangular_solve(tri, rhs, left_side=True, lower=True, unit_diagonal=True)
    u = sol[..., :dv]
    w = sol[..., dv:]
    attn = jnp.where(causal, jnp.einsum('bhncd,bhnsd->bhncs', q, k) * decay, 0.0)
    g_last = g[..., -1]
    q_dec = q * jnp.exp(g)[..., None]
    k_dec = k * jnp.exp(g_last[..., None] - g)[..., None]

    def step(state, inp):
        attn_c, u_c, w_c, q_c, k_c, gl_c = inp
        v_new = u_c - jnp.einsum('bhck,bhkv->bhcv', w_c, state)
        o_c = jnp.einsum('bhck,bhkv->bhcv', q_c, state) + jnp.einsum('bhcs,bhsv->bhcv', attn_c, v_new)
        state = state * jnp.exp(gl_c)[..., None, None] + jnp.einsum('bhck,bhcv->bhkv', k_c, v_new)
        return state, o_c

    xs = tuple(jnp.moveaxis(a, 2, 0) for a in (attn, u, w, q_dec, k_dec, g_last))
    s0 = jnp.zeros((b, h, dk, dv), f32)
    _, o = lax.scan(step, s0, xs)
    o = jnp.moveaxis(o, 0, 2).reshape(b, h, t, dv)
    return o.astype(out_dtype)


def setup_inputs(seed: int = 0) -> dict:
    key = jax.random.key(seed)
    ks = jax.random.split(key, 20)
    f32 = jnp.float32
    L, D = DEPTH, D_MODEL

    def normal(k, shape, scale):
        return jax.random.normal(k, shape, f32) * scale

    dt = jnp.exp(jax.random.uniform(ks[12], (L, DN_HEADS), f32) * (math.log(0.1) - math.log(0.001)) + math.log(0.001))
    return {
        'x': normal(ks[0], (BATCH, SEQ, D), 1.0),
        'c': normal(ks[1], (BATCH, D), 1.0),
        'w_ada': normal(ks[2], (L, D, N_MOD * D), 0.5 * D ** -0.5),
        'b_ada': normal(ks[3], (L, N_MOD * D), 0.02),
        'w_ada_final': normal(ks[4], (D, 2 * D), 0.5 * D ** -0.5),
        'b_ada_final': normal(ks[5], (2 * D,), 0.02),
        'g_norm_mix': 1.0 + normal(ks[6], (L, D), 0.1),
        'g_norm_ffn': 1.0 + normal(ks[7], (L, D), 0.1),
        'g_norm_final': 1.0 + normal(ks[8], (D,), 0.1),
        'w_in': normal(ks[9], (L, D, P_IN), D ** -0.5),
        'conv_w_mix': normal(ks[10], (L, CONV_K_MIX, D_CONV), CONV_K_MIX ** -0.5),
        'conv_w_qkv': normal(ks[11], (L, CONV_K_QKV, 2 * DN_QK + DN_V), CONV_K_QKV ** -0.5),
        'a_log': jnp.log(jax.random.uniform(ks[13], (L, DN_HEADS), f32, minval=1.0, maxval=16.0)),
        'dt_bias': dt + jnp.log(-jnp.expm1(-dt)),
        'g_conv_out': 1.0 + normal(ks[14], (L, D_CONV), 0.1),
        'g_dn_out': 1.0 + normal(ks[15], (L, DN_HEAD_DIM), 0.1),
        'w_out': normal(ks[16], (L, D_MIX, D), D_MIX ** -0.5),
        'w_gate_up': normal(ks[17], (L, D, 2 * FFN_HIDDEN), D ** -0.5),
        'w_down': normal(ks[18], (L, FFN_HIDDEN, D), FFN_HIDDEN ** -0.5),
    }


def reference(x, c, w_ada, b_ada, w_ada_final, b_ada_final, g_norm_mix, g_norm_ffn, g_norm_final,
              w_in, conv_w_mix, conv_w_qkv, a_log, dt_bias, g_conv_out, g_dn_out, w_out, w_gate_up, w_down):
    b, s, _ = x.shape
    c_act = jax.nn.silu(c)
    split_at = [D_CONV, 2 * D_CONV, 3 * D_CONV,
                3 * D_CONV + DN_QK, 3 * D_CONV + 2 * DN_QK,
                3 * D_CONV + 2 * DN_QK + DN_V, 3 * D_CONV + 2 * DN_QK + 2 * DN_V,
                3 * D_CONV + 2 * DN_QK + 2 * DN_V + DN_HEADS]
    for layer in range(DEPTH):
        mod = c_act @ w_ada[layer] + b_ada[layer]
        shift_m, scale_m, gate_m, shift_f, scale_f, gate_f = jnp.split(mod, N_MOD, axis=-1)

        h = adaln(x, g_norm_mix[layer], shift_m, scale_m)
        p = h @ w_in[layer]
        gate_b, gate_c, h_conv, q, k, v, z, a, bt = jnp.split(p, split_at, axis=-1)

        ya = gate_b * causal_depthwise_conv(gate_c * h_conv, conv_w_mix[layer])
        ya = rms_norm(ya.reshape(b, s, CONV_GROUPS, CONV_GROUP_DIM),
                      g_conv_out[layer].reshape(CONV_GROUPS, CONV_GROUP_DIM)).reshape(b, s, D_CONV)

        qkv = jax.nn.silu(causal_depthwise_conv(jnp.concatenate([q, k, v], axis=-1), conv_w_qkv[layer]))
        q, k, v = jnp.split(qkv, [DN_QK, 2 * DN_QK], axis=-1)

        def heads(t):
            return t.reshape(b, s, DN_HEADS, DN_HEAD_DIM).transpose(0, 2, 1, 3)

        q = l2_normalize(heads(q))
        k = l2_normalize(heads(k))
        v = heads(v)
        beta = jax.nn.sigmoid(bt.astype(jnp.float32)).transpose(0, 2, 1)
        g_dec = (-jnp.exp(a_log[layer].astype(jnp.float32))
                 * jax.nn.softplus(a.astype(jnp.float32) + dt_bias[layer].astype(jnp.float32))).transpose(0, 2, 1)
        o = chunked_gated_delta_rule(q, k, v, g_dec, beta).transpose(0, 2, 1, 3)
        o = rms_norm(o, g_dn_out[layer]) * jax.nn.silu(z.reshape(b, s, DN_HEADS, DN_HEAD_DIM))
        yb = o.reshape(b, s, DN_V)

        y = jnp.concatenate([ya, yb], axis=-1) @ w_out[layer]
        x = x + gate_m[:, None, :] * y

        h = adaln(x, g_norm_ffn[layer], shift_f, scale_f)
        gt, up = jnp.split(h @ w_gate_up[layer], 2, axis=-1)
        x = x + gate_f[:, None, :] * ((jax.nn.silu(gt) * up) @ w_down[layer])

    shift_o, scale_o = jnp.split(c_act @ w_ada_final + b_ada_final, 2, axis=-1)
    return adaln(x, g_norm_final, shift_o, scale_o)
```

```python
import numpy as np
from contextlib import ExitStack
import concourse.bass as bass
import concourse.mybir as mybir
from concourse.bass_utils import run_bass_kernel_spmd

F32 = mybir.dt.float32
BF = mybir.dt.bfloat16
AF = mybir.ActivationFunctionType
ALU = mybir.AluOpType
AX = mybir.AxisListType

P = 128
D = 1024
TB = 512
NT = 4
PIN = 3592
HID = 2816
NHC = 22
EPS = 1e-6
NSLOT = 4
MODEL_MARKS = False
EXPERIMENT = None
EXP_PRE = None
DK_SCALE = 128 ** -0.5


def PRIO(st, bl):
    return st - 0.03 * bl


class _Cap:
    def __init__(self):
        self.call = None

    def __getattr__(self, name):
        def f(*a, **k):
            self.call = (name, a, k)
            return None
        return f


def _free_elems(ap):
    n = 1
    for d in list(ap.shape)[1:]:
        n *= int(d)
    return n


class Sched:
    LAT = 0.25

    def __init__(self, nc, es):
        self.nc = nc
        self.es = es
        self.eng = {'pe': nc.tensor, 'dve': nc.vector, 'act': nc.scalar, 'pool': nc.gpsimd, 'sp': nc.sync}
        self.sem = {}
        self.cnt = {}
        for k in self.eng:
            self._newsem(k)
        self.seen = {e: {} for e in self.eng}
        self.buf = {}
        self.recs = []
        self.labels = []
        self.ticket = {}
        self.fin = {}
        self.free = {e: 0.0 for e in self.eng}
        self.n_emitted = 0
        self.crit = {}
        self.start = {}
        self.last_on = {}
        self.nwaits = 0
        self.nops = 0

    def _newsem(self, k):
        if k not in self.sem:
            self.sem[k] = self.es.enter_context(self.nc.semaphore("s_" + k))
            self.cnt[k] = 0

    def _dur(self, e, call, dsem):
        name, a, k = call
        try:
            if dsem is not None:
                out = k.get('out')
                nbytes = _free_elems(out) * int(out.shape[0]) * (2 if out.dtype == BF else 4)
                return 2.0 + nbytes / 150e3
            if e == 'pe':
                if name == 'transpose':
                    return 0.11
                rhs = k.get('rhs')
                cols = max(64, _free_elems(rhs))
                f = 4.0 if rhs.dtype == F32 else 1.0
                return max(0.1, cols * f / 1920.0)
            out = k.get('out', None)
            if out is None:
                out = k.get('ap', a[0] if a else None)
            n = _free_elems(out)
            if e == 'pool':
                return 0.2 + n / 450.0
            return 0.07 + n / 960.0
        except Exception:
            return 0.3

    REN = {}
    REN2 = {}
    ALIAS = {}

    def op(self, e, fn, reads=(), writes=(), dsem=None):
        if self.REN2:
            reads = [self.REN2.get(b, b) for b in reads]
            writes = [self.REN2.get(b, b) for b in writes]
        if self.REN:
            reads = [self.REN.get(b, b) for b in reads]
            writes = [self.REN.get(b, b) for b in writes]
        if self.ALIAS:
            reads = list(reads) + [p_ for b in reads for p_ in self.ALIAS.get(b, ())]
            writes = list(writes) + [p_ for b in writes for p_ in self.ALIAS.get(b, ())]
        cap = _Cap()
        fn(cap)
        call = cap.call
        assert call is not None
        rid = len(self.recs)
        deps = set()
        recs = self.recs
        for b in reads:
            st = self.buf.get(b)
            if st:
                deps.update(st[0])
                if b.startswith('ps'):
                    for r in st[1]:
                        if recs[r][0] != e:
                            deps.add(r)
        for b in writes:
            st = self.buf.get(b)
            if st:
                deps.update(st[0])
                deps.update(st[1])
        ws = set(writes)
        for b in writes:
            st = self.buf.setdefault(b, [[], []])
            if st[1]:
                st[0] = [rid]
                st[1] = []
            else:
                if len(st[0]) > 64:
                    st[0] = st[0][-64:]
                st[0].append(rid)
        for b in reads:
            if b in ws:
                continue
            st = self.buf.setdefault(b, [[], []])
            if len(st[1]) > 256:
                keep = {}
                for r in st[1]:
                    keep[recs[r][0]] = r
                st[1] = sorted(keep.values())
            st[1].append(rid)
        if dsem is not None:
            self._newsem(dsem)
        self.recs.append((e, call, deps, dsem, self._dur(e, call, dsem)))
        self.labels.append((list(writes) + ['-'])[0])
        return None

    def flush(self):
        recs = self.recs
        i0 = self.n_emitted
        n = len(recs)
        if i0 >= n:
            return
        LAT = self.LAT
        pend = {}
        users = {}
        for rid in range(i0, n):
            c = 0
            for d in recs[rid][2]:
                if d >= i0:
                    c += 1
                    users.setdefault(d, []).append(rid)
            pend[rid] = c
        bl = {}
        for rid in range(n - 1, i0 - 1, -1):
            m = 0.0
            e = recs[rid][0]
            for u_ in users.get(rid, ()):
                v = bl[u_] + (LAT if recs[u_][0] != e else 0.03)
                if v > m:
                    m = v
            bl[rid] = recs[rid][4] + m
        fin = self.fin
        free = self.free

        def tdep_of(rid):
            e = recs[rid][0]
            t = 0.0
            cr = None
            for d in recs[rid][2]:
                fd = fin.get(d, 0.0) + (LAT if recs[d][0] != e else 0.03)
                if fd > t:
                    t = fd
                    cr = d
            return t, cr

        ready = {e: {} for e in self.eng}
        for rid in range(i0, n):
            if pend[rid] == 0:
                ready[recs[rid][0]][rid] = tdep_of(rid)
        left = n - i0
        while left:
            best = None
            for e, rd in ready.items():
                if not rd:
                    continue
                fe = free[e]
                cand = None
                for rid, (td, cr) in rd.items():
                    st = td if td > fe else fe
                    key = (PRIO(st, bl[rid]), -bl[rid], rid)
                    if cand is None or key < cand[0]:
                        cand = (key, rid, td, cr, st)
                if best is None or cand[0] < best[0]:
                    best = (cand[0], cand[1], e, cand[2], cand[3], cand[4])
            _, rid, e, td, cr, t = best
            del ready[e][rid]
            self.crit[rid] = ('dep', cr) if (cr is not None and td >= free[e]) else ('eng', self.last_on.get(e))
            self.start[rid] = t
            self.last_on[e] = rid
            self._emit(rid)
            dur = recs[rid][4]
            if recs[rid][3] is not None:
                free[e] = t + 0.15
                fin[rid] = t + dur
            else:
                free[e] = t + dur
                fin[rid] = t + dur
            for u_ in users.get(rid, ()):
                pend[u_] -= 1
                if pend[u_] == 0:
                    ready[recs[u_][0]][u_] = tdep_of(u_)
            left -= 1
        self.n_emitted = n

    def _emit(self, rid):
        e, call, deps, dsem, _ = self.recs[rid]
        eng = self.eng[e]
        need = {}
        for d in deps:
            k, v = self.ticket[d]
            if e == 'pe' and k == 'pe':
                continue
            if need.get(k, 0) < v:
                need[k] = v
        seen = self.seen[e]
        for k, v in need.items():
            if seen.get(k, 0) >= v:
                continue
            eng.wait_ge(self.sem[k], v)
            seen[k] = v
            self.nwaits += 1
        name, a, kw = call
        ins = getattr(eng, name)(*a, **kw)
        self.nops += 1
        if dsem is not None:
            self.cnt[dsem] += 16
            ins.then_inc(self.sem[dsem], 16)
            self.ticket[rid] = (dsem, self.cnt[dsem])
        else:
            self.cnt[e] += 1
            ins.then_inc(self.sem[e], 1)
            self.ticket[rid] = (e, self.cnt[e])

    def mark(self, label):
        self.flush()
        if not hasattr(self, 'marks'):
            self.marks = []
        self.marks.append((label, max(self.free.values()), dict(self.free)))

    def fence(self):
        self.flush()
        ce = ['pe', 'dve', 'act', 'pool']
        for e in ce:
            for k in ce:
                v = self.cnt[k]
                if v > 0 and self.seen[e].get(k, 0) < v:
                    self.eng[e].wait_ge(self.sem[k], v)
                    self.seen[e][k] = v
                    self.nwaits += 1
        t = max(self.free[e] for e in ce)
        t = max([t] + [self.fin.get(r, 0.0) for r in range(max(0, self.n_emitted - 400), self.n_emitted)
                       if self.recs[r][3] is None])
        for e in ce:
            self.free[e] = t

    def finish(self, e='sp'):
        self.flush()
        eng = self.eng[e]
        for k, v in self.cnt.items():
            if v > 0 and self.seen[e].get(k, 0) < v:
                eng.wait_ge(self.sem[k], v)
                self.seen[e][k] = v
        self.sim_time = max(self.free.values())
        self.busy = {}
        for (e, call, deps, dsem, dur) in self.recs:
            self.busy[e] = self.busy.get(e, 0.0) + (0.15 if dsem is not None else dur)


def bc(ap, n):
    return bass.AP(ap.tensor, ap.offset, [list(x) for x in ap.ap] + [[0, n]])


def build_nc(nblk_own=8, nblk_pre=8, dbg=(), stop=99):
    nc = bass.Bass("TRN2", target_bir_lowering=False)
    NTOK_OWN = nblk_own * TB
    NTOK_PRE = max(nblk_pre, 1) * TB

    def din(name, shape, dt=F32):
        return nc.dram_tensor(name, list(shape), dt, kind="ExternalInput").ap()

    x_own = din("x_own", [NTOK_OWN, D])
    x_pre = din("x_pre", [NTOK_PRE, D])
    pmask_d = din("pmask", [P, 1])
    cT_d = din("cT", [P, 8])
    w_ada_d = din("w_ada", [D, 6 * D])
    w_adaf_d = din("w_ada_final", [D, 2 * D])
    bpp_d = din("b_pp", [P, 4, 8])
    bbc_d = din("b_bc", [P, 4, D])
    gfin_d = din("gfin_bc", [P, D])
    gpp_d = din("g_pp", [P, 2, 8])
    w_in_d = din("w_in", [D, PIN])
    w_out_d = din("w_out", [D, D])
    w_gu_d = din("w_gu", [D, 2 * HID])
    w_dn_d = din("w_down", [HID, D])
    cwm_d = din("cw_mix", [P, 4, 3])
    cwq_d = din("cw_qkv", [P, 12, 4])
    gconv_d = din("g_conv", [P, 4])
    gdn_d = din("g_dn", [P, 1])
    alog_d = din("alog_bc", [P, 4])
    dtb_d = din("dtb_bc", [P, 4])
    cmask_d = din("cmask", [P, 4, P])
    cst_d = din("consts", [P, 5, P])
    out_d = nc.dram_tensor("out", [NTOK_OWN, D], F32, kind="ExternalOutput").ap()
    dbg_d = {}
    for name, shape in dbg:
        dbg_d[name] = nc.dram_tensor("dbg_" + name, list(shape), F32, kind="ExternalOutput").ap()

    win_bf = nc.dram_tensor("win_bf", [D, PIN], BF, kind="Internal").ap()
    wout_bf = nc.dram_tensor("wout_bf", [D, D], BF, kind="Internal").ap()
    wgu_bf = nc.dram_tensor("wgu_bf", [D, 2 * HID], BF, kind="Internal").ap()
    wdn_bf = nc.dram_tensor("wdn_bf", [HID, D], BF, kind="Internal").ap()

    with ExitStack() as es:
        S = Sched(nc, es)
        op = S.op

        def sb(name, shape, dt=F32):
            return es.enter_context(nc.sbuf_tensor(name, list(shape), dt))

        psb = [es.enter_context(nc.psum_tensor("ps%d" % i, [P, 512], F32)) for i in range(8)]
        psi = [0]

        class PS:
            def __init__(self, i):
                self.name = "ps%d" % i
                self.t = psb[i]
                self.b = psb[i].bitcast(BF)

            def v3(self, a=4):
                return self.t[:].rearrange("p (a b) -> p a b", a=a)

        def newps():
            i = psi[0]
            psi[0] = (i + 1) % 8
            return PS(i)

        cst = sb("cst", [P, 5, P])
        identf = cst[:, 0, :]
        onesf = cst[:, 1, :]
        Uf = cst[:, 2, :]
        identb_t = sb("identb", [P, P], BF)
        onesb_t = sb("onesb", [P, P], BF)
        blk1_t = sb("blk1", [P, P], BF)
        m_cs = sb("m_cs", [P, 4, P])
        m_sc = sb("m_sc", [P, 4, P])
        m_scn = sb("m_scn", [P, 4, P])
        diagq = sb("diagq", [P, 12, 4, P], BF)
        diagm = sb("diagm", [P, 4, 3, P], BF)
        prm = sb("prm", [P, 64])
        c_gsm, c_shm, c_gsf, c_shf = 0, 8, 16, 24
        c_gconv, c_gdn, c_negA, c_dtb, c_pm = 32, 36, 37, 41, 45
        modbc = sb("modbc", [P, 4, D])
        wab = sb("wab", [P, 8, 8], BF)
        cmb = sb("cmb", [P, 4, P], BF)

        X = sb("X", [P, NT, D])
        xsb = sb("xsb", [P, NT, D], BF)
        ssx = sb("ssx", [P, 8])
        hT = sb("hT", [P, 8, TB], BF)
        mixT = sb("mixT", [P, 8, TB], BF)
        ring = [sb("ring%d" % i, [P, 8, 512], BF) for i in range(NSLOT)]
        ucv = sb("ucv", [P, 4, TB + 8], BF)
        yaj = sb("yaj", [P, TB])
        yaj2 = sb("yaj2", [P, TB])
        sqb = sb("sqb", [P, TB], BF)
        rst = sb("rst", [P, TB])
        qkvp = sb("qkvp", [P, 12, TB + 8], BF)
        zs = sb("zs", [P, 4, TB], BF)
        S32 = sb("S32", [P, 4, P])
        Sbf = sb("Sbf", [P, 4, P], BF)
        absb = sb("absb", [P, NT, 8])
        sc = sb("sc", [P, 24, 16])
        ARENA_W = 15360
        arena = sb("arena", [P, ARENA_W])

        aoff = [0]

        def carve(nwords, dt=F32, shape3=None):
            a = arena[:, aoff[0]:aoff[0] + nwords]
            aoff[0] += nwords
            if dt == BF:
                a = a.bitcast(BF)
            if shape3 is not None:
                a = a.rearrange("p (a b) -> p a b", a=shape3)
            return a

        cs = carve(1536, F32, 3)
        e1t = carve(1024, F32, 2)
        sqj = e1t[:, 0, :].rearrange("p (a b) -> p a b", a=4)
        Erow = carve(512, F32, 4)
        dm0 = carve(512, F32, 4)
        dm1 = carve(512, F32, 4)
        dm2 = carve(512, F32, 4)
        qh = carve(512, F32, 4)
        kh = carve(512, F32, 4)
        u_sb = carve(512, F32, 4)
        oT = carve(2048, F32, 4)
        hcb = oT
        dm1b = carve(256, BF, 4)
        khT = carve(256, BF, 4)
        qhT = carve(256, BF, 4)
        Lt = [carve(256, BF, 4), carve(256, BF, 4)]
        Mt = [carve(256, BF, 4), carve(256, BF, 4)]
        Qb = carve(256, BF, 4)
        wT = carve(256, BF, 4)
        vnew = carve(256, BF, 4)
        Nf = [carve(256, BF, 4), carve(256, BF, 4)]
        Lbd = [carve(256, BF, 4), carve(256, BF, 4)]
        Mbd = [carve(256, BF, 4), carve(256, BF, 4)]
        kbg = [carve(256, BF, 4), carve(256, BF, 4)]
        kdec = [carve(256, BF, 4), carve(256, BF, 4)]
        vb = [carve(256, BF, 4), carve(256, BF, 4)]
        qdT = [carve(256, BF, 4), carve(256, BF, 4)]
        attnT = [carve(256, BF, 4), carve(256, BF, 4)]
        assert aoff[0] <= ARENA_W, aoff[0]
        actT = arena[:, 0:5632].bitcast(BF).rearrange("p (a b) -> p a b", a=NHC)
        _pg = {'cs0': (0, 1), 'cs1': (2, 3), 'cs2': (4, 5), 'e1t0': (6, 7), 'e1t1': (8, 9), 'Erow': (10, 11),
               'dm0': (12, 13), 'dm1': (14, 15), 'dm2': (16, 17), 'qh': (18, 19), 'kh': (20, 21)}
        S.ALIAS = {k_: ['aw%d' % p_ for p_ in v_] for k_, v_ in _pg.items()}
        for J_ in range(NHC):
            S.ALIAS['actT%d' % J_] = ['aw%d' % J_]

        def scv(i, w=4):
            return sc[:, i, 0:w]

        def dma(e, out, in_, reads, writes, dsem, **kw):
            return op(e, lambda en: en.dma_start(out=out, in_=in_, **kw), reads=reads, writes=writes, dsem=dsem)

        dma('sp', cst[:], cst_d, [], ['cst'], 'd_c0')
        dma('sp', prm[:, c_gconv:c_gconv + 4], gconv_d, [], ['prm'], 'd_c1')
        dma('sp', prm[:, c_gdn:c_gdn + 1], gdn_d, [], ['prm'], 'd_c1')
        dma('sp', prm[:, c_dtb:c_dtb + 4], dtb_d, [], ['prm'], 'd_c1')
        dma('sp', prm[:, c_pm:c_pm + 1], pmask_d, [], ['prm'], 'd_c1')
        dma('sp', prm[:, c_negA:c_negA + 4], alog_d, [], ['prm'], 'd_c1')
        cT = sb("cTs", [P, 8])
        bpp = sb("bpp", [P, 4, 8])
        gpp = sb("gpp", [P, 2, 8])
        cwm = sb("cwm", [P, 4, 3])
        cwq = sb("cwq", [P, 12, 4])
        dma('sp', cT[:], cT_d, [], ['cT'], 'd_cT')
        dma('sp', bpp[:], bpp_d, [], ['bpp'], 'd_bpp')
        dma('sp', gpp[:], gpp_d, [], ['gpp'], 'd_gpp')
        dma('sp', cwm[:], cwm_d, [], ['cwm'], 'd_cwm')
        dma('sp', cwq[:], cwq_d, [], ['cwq'], 'd_cwq')
        dma('sp', modbc[:], bbc_d, [], ['modbc'], 'd_c3')
        dma('pool', cmb[:], cmask_d, [], ['cmb'], 'd_cmb')

        def cast_rows(dst, src, nrows, c0, c1, name, sem):
            for r0 in range(0, nrows, 128):
                r1 = min(nrows, r0 + 128)
                for a in range(c0, c1, 2048):
                    b_ = min(c1, a + 2048)
                    dma('pool', dst[r0:r1, a:b_], src[r0:r1, a:b_], [], [name], sem)

        cast_rows(win_bf, w_in_d, D, 1536, PIN, 'win_bf_b', 'd_w0')
        cast_rows(win_bf, w_in_d, D, 0, 1536, 'win_bf_a', 'd_w1')

        op('dve', lambda e: e.tensor_copy(out=identb_t[:], in_=identf), ['cst'], ['identb'])
        op('dve', lambda e: e.tensor_copy(out=onesb_t[:], in_=onesf), ['cst'], ['onesb'])
        op('dve', lambda e: e.tensor_copy(out=blk1_t[:], in_=cst[:, 3, :]), ['cst'], ['blk1'])
        for h in range(4):
            op('dve', lambda e: e.tensor_copy(out=m_cs[:, h, :], in_=cst[:, 4, :]), ['cst'], ['m_cs'])
            op('dve', lambda e: e.tensor_copy(out=m_sc[:, h, :], in_=Uf), ['cst'], ['m_sc'])
            op('dve', lambda e: e.tensor_tensor(out=m_scn[:, h, :], in0=identf, in1=Uf, op=ALU.subtract),
               ['cst'], ['m_scn'])
            op('dve', lambda e: e.tensor_tensor(out=m_scn[:, h, :], in0=m_scn[:, h, :], in1=cmb[:, 0, :], op=ALU.mult),
               ['m_scn', 'cmb'], ['m_scn'])
        op('act', lambda e: e.activation(out=prm[:, c_negA:c_negA + 4], in_=prm[:, c_negA:c_negA + 4], func=AF.Exp),
           ['prm'], ['prm'])
        op('dve', lambda e: e.tensor_scalar(out=prm[:, c_negA:c_negA + 4], in0=prm[:, c_negA:c_negA + 4],
                                            scalar1=-1.0, scalar2=None, op0=ALU.mult), ['prm'], ['prm'])
        for cc in range(12):
            for j in range(4):
                eng = 'dve' if (cc + j) % 2 == 0 else 'pool'
                op(eng, lambda e: e.tensor_scalar(out=diagq[:, cc, j, :], in0=identf, scalar1=cwq[:, cc, j:j + 1],
                                                  scalar2=0.0, op0=ALU.mult, op1=ALU.add), ['cst', 'cwq'], ['diagq'])
        for jj in range(4):
            for j in range(3):
                eng = 'dve' if (jj + j) % 2 == 0 else 'pool'
                op(eng, lambda e: e.tensor_scalar(out=diagm[:, jj, j, :], in0=identf, scalar1=cwm[:, jj, j:j + 1],
                                                  scalar2=0.0, op0=ALU.mult, op1=ALU.add), ['cst', 'cwm'], ['diagm'])
        op('pool', lambda e: e.memset(S32[:], 0.0), [], ['S32'])
        op('pool', lambda e: e.memset(Sbf[:], 0.0), [], ['Sbf'])
        op('pool', lambda e: e.memset(qkvp[:], 0.0), [], ['qkvp%d' % i for i in range(12)])
        op('pool', lambda e: e.memset(ucv[:], 0.0), [], ['ucv'])

        cact = sb("cact", [P, 8])
        ctmp = sb("ctmp", [P, 8])
        op('act', lambda e: e.activation(out=ctmp[:], in_=cT[:], func=AF.Exp, scale=-1.0), ['cT'], ['ctmp'])
        op('act', lambda e: e.activation(out=ctmp[:], in_=ctmp[:], func=AF.Ln, bias=1.0), ['ctmp'], ['ctmp'])
        op('act', lambda e: e.activation(out=ctmp[:], in_=ctmp[:], func=AF.Exp, scale=-1.0), ['ctmp'], ['ctmp'])
        op('dve', lambda e: e.tensor_tensor(out=cact[:], in0=cT[:], in1=ctmp[:], op=ALU.mult), ['cT', 'ctmp'], ['cact'])
        MIXN = ['mixT%d' % i_ for i_ in range(8)]
        mixf = mixT[:].rearrange("p a b -> p (a b)").bitcast(F32)
        crep = mixf[:, 0:1024].rearrange("p (a b) -> p a b", a=8)
        gfin = mixf[:, 1024:2048]
        wst = arena[:, 1024:1024 + 8192].rearrange("p (a b) -> p a b", a=8)
        for kc in range(8):
            op('dve', lambda e: e.tensor_scalar(out=crep[:, kc, :], in0=onesf, scalar1=cact[:, kc:kc + 1],
                                                scalar2=None, op0=ALU.mult), ['cst', 'cact'], MIXN)
        dma('sp', gfin, gfin_d, [], MIXN, 'd_gfin')
        pp_cols = [0, 1, 3, 4]
        ppres = sb("ppres", [P, 4, 8])
        prmf = sb("prmf", [P, 16])
        for vi, vcol in list(enumerate(pp_cols))[:2]:
            dma('sp', wst, w_ada_d[:, vcol * D:(vcol + 1) * D].rearrange("(k p) c -> p k c", p=P),
                [], ['wst'], 'd_wst')
            ps = newps()
            for j in range(8):
                for kc in range(8):
                    op('pe', lambda e: e.matmul(ps.t[:, j:j + 1], lhsT=wst[:, kc, j * P:(j + 1) * P],
                                                rhs=cact[:, kc:kc + 1], start=(kc == 0), stop=(kc == 7)),
                       ['wst', 'cact'], [ps.name])
            op('dve', lambda e: e.tensor_tensor(out=ppres[:, vi, :], in0=ps.t[:, 0:8], in1=bpp[:, vi, :], op=ALU.add),
               [ps.name, 'bpp'], ['ppres'])
        op('dve', lambda e: e.scalar_tensor_tensor(out=prm[:, c_gsm:c_gsm + 8], in0=ppres[:, 1, :], scalar=1.0,
                                                   in1=gpp[:, 0, :], op0=ALU.add, op1=ALU.mult),
           ['ppres', 'gpp'], ['prm'])
        op('dve', lambda e: e.tensor_copy(out=prm[:, c_shm:c_shm + 8], in_=ppres[:, 0, :]), ['ppres'], ['prm'])
        bsrc = [(w_ada_d, 2), (w_ada_d, 5), (w_adaf_d, 0), (w_adaf_d, 1)]
        XN = ['X%d' % t_ for t_ in range(NT)]
        xst = X[:].rearrange("p a b -> p (a b)").rearrange("p (a b) -> p a b", a=8)

        def bc_job(jid):
            vi, half = jid // 2, jid % 2
            wd, vcol = bsrc[vi]
            dma('sp', xst, wd[:, vcol * D + half * 512:vcol * D + (half + 1) * 512].rearrange("(k p) c -> p k c", p=P),
                [], XN, 'd_x0')
            ps = newps()
            for kc in range(8):
                op('pe', lambda e: e.matmul(ps.t[:], lhsT=crep[:, kc, :], rhs=xst[:, kc, :],
                                            start=(kc == 0), stop=(kc == 7)), XN + MIXN, [ps.name])
            sl = modbc[:, vi, half * 512:(half + 1) * 512]
            op('dve', lambda e: e.tensor_tensor(out=sl, in0=ps.t[:], in1=sl, op=ALU.add), [ps.name, 'modbc'], ['modbc'])
            if vi == 3:
                op('dve', lambda e: e.scalar_tensor_tensor(out=sl, in0=sl, scalar=1.0,
                                                           in1=gfin[:, half * 512:(half + 1) * 512],
                                                           op0=ALU.add, op1=ALU.mult), ['modbc'] + MIXN, ['modbc'])
        def pp_job(jid):
            vi, half = 2 + jid // 2, jid % 2
            vcol = pp_cols[vi]
            dma('sp', xst, w_ada_d[:, vcol * D + half * 512:vcol * D + (half + 1) * 512].rearrange("(k p) c -> p k c", p=P),
                [], XN, 'd_x0')
            ps = newps()
            for jj in range(4):
                for kc in range(8):
                    op('pe', lambda e: e.matmul(ps.t[:, jj:jj + 1], lhsT=xst[:, kc, jj * P:(jj + 1) * P],
                                                rhs=cact[:, kc:kc + 1], start=(kc == 0), stop=(kc == 7)),
                       XN + ['cact'], [ps.name])
            pn = 'ppres%d' % vi
            op('dve', lambda e: e.tensor_tensor(out=ppres[:, vi, half * 4:(half + 1) * 4], in0=ps.t[:, 0:4],
                                                in1=bpp[:, vi, half * 4:(half + 1) * 4], op=ALU.add),
               [ps.name, 'bpp'], [pn])
            if jid == 3:
                op('dve', lambda e: e.scalar_tensor_tensor(out=prmf[:, 0:8], in0=ppres[:, 3, :], scalar=1.0,
                                                           in1=gpp[:, 1, :], op0=ALU.add, op1=ALU.mult),
                   ['ppres3', 'gpp'], ['prmf'])
                op('dve', lambda e: e.tensor_copy(out=prmf[:, 8:16], in_=ppres[:, 2, :]), ['ppres2'], ['prmf'])
        bc_jobs = [('pp', j_) for j_ in range(4)] + [('bc', j_) for j_ in range(8)]

        def run_job(job):
            (pp_job if job[0] == 'pp' else bc_job)(job[1])
        gate_m_bc = modbc[:, 0, :]
        gate_f_bc = modbc[:, 1, :]
        sh_o_bc = modbc[:, 2, :]
        gs_o_bc = modbc[:, 3, :]

        cast_rows(wout_bf, w_out_d, D, 0, D, 'wout_bf', 'd_w2')
        cast_rows(wgu_bf, w_gu_d, D, 0, 2 * HID, 'wgu_bf', 'd_w3')
        cast_rows(wdn_bf, w_dn_d, HID, 0, D, 'wdn_bf', 'd_w4')
        dma('sp', wab[:], win_bf[:, 3584:3592].rearrange("(k p) c -> p k c", p=P), ['win_bf_b'], ['wab'], 'd_c4')
        S.fence()
        if stop <= 1:
            nblk_pre = 0
            nblk_own = 0
        if 'modbc' in dbg_d:
            dma('pool', dbg_d['modbc'], modbc[:], ['modbc'], [], 'd_dbg')
            dma('pool', dbg_d['prm'], prm[:, 0:46], ['prm'], [], 'd_dbg')

        def win_src(g):
            return ('win_bf_a' if g < 3 else 'win_bf_b',
                    win_bf[:, g * 512:(g + 1) * 512].rearrange("(k p) c -> p k c", p=P), 8, 512)

        def wout_src(hf):
            return ('wout_bf', wout_bf[:, hf * 512:(hf + 1) * 512].rearrange("(k p) c -> p k c", p=P), 8, 512)

        def wgu_src(g, up):
            c0 = (HID if up else 0) + g * 512
            w = min(512, HID - g * 512)
            return ('wgu_bf', wgu_bf[:, c0:c0 + w].rearrange("(k p) c -> p k c", p=P), 8, w)

        def wdn_src(hf, G):
            r0 = G * 8 * P
            n = min(8, NHC - G * 8)
            return ('wdn_bf', wdn_bf[r0:r0 + n * P, hf * 512:(hf + 1) * 512].rearrange("(k p) c -> p k c", p=P), n, 512)

        plan = []
        for blk in range(nblk_pre):
            if blk == nblk_pre - 1:
                plan += [win_src(2), win_src(1), win_src(3)]
            plan += [win_src(4), win_src(5)]
        for blk in range(nblk_own):
            plan += [win_src(2), win_src(1), win_src(0), win_src(6), win_src(3), win_src(4), win_src(5)]
            plan += [wout_src(0), wout_src(1)]
            for g in range(6):
                plan += [wgu_src(g, False), wgu_src(g, True)]
            for hf in range(2):
                for G in range(3):
                    plan += [wdn_src(hf, G)]
        rstate = {'issued': 0, 'used': 0}

        def ring_issue(upto):
            while rstate['issued'] < min(upto, len(plan)):
                i = rstate['issued']
                name, src, nk, w = plan[i]
                slot = i % NSLOT
                dma('sp', ring[slot][:, 0:nk, 0:w], src, [name], ['ring%d' % slot], 'd_ring%d' % slot)
                rstate['issued'] += 1

        def ring_get_n(k):
            i = rstate['used']
            ring_issue(i + NSLOT)
            rstate['used'] += k
            res = []
            for q_ in range(k):
                slot = (i + q_) % NSLOT
                res += [ring[slot], 'ring%d' % slot]
            return res

        def ring_get():
            return ring_get_n(1)

        def load_x(src_blk_ap):
            for tt in range(NT):
                dma('sp', X[:, tt, :], src_blk_ap[:, tt, :], [], ['X%d' % tt], 'd_x%d' % tt)

        def norm_T(c_gs, c_sh, pt=None, pn='prm'):
            pt = prm if pt is None else pt
            for tt in range(NT):
                op('act', lambda e: e.activation(out=xsb[:, tt, :], in_=X[:, tt, :], func=AF.Square,
                                                 accum_out=ssx[:, tt:tt + 1]), ['X%d' % tt], ['xsb%d' % tt, 'ssx'])
            op('act', lambda e: e.activation(out=ssx[:, 0:4], in_=ssx[:, 0:4], func=AF.Ln, scale=1.0 / D, bias=EPS),
               ['ssx'], ['ssx'])
            op('act', lambda e: e.activation(out=ssx[:, 0:4], in_=ssx[:, 0:4], func=AF.Exp, scale=-0.5),
               ['ssx'], ['ssx'])
            for tt in range(NT):
                if tt % 2 == 0:
                    op('dve', lambda e: e.tensor_scalar(out=xsb[:, tt, :], in0=X[:, tt, :], scalar1=ssx[:, tt:tt + 1],
                                                        scalar2=None, op0=ALU.mult), ['X%d' % tt, 'ssx'], ['xsb%d' % tt])
                else:
                    op('act', lambda e: e.activation(out=xsb[:, tt, :], in_=X[:, tt, :], func=AF.Copy,
                                                     scale=ssx[:, tt:tt + 1]), ['X%d' % tt, 'ssx'], ['xsb%d' % tt])
            for kp in range(4):
                ps = newps()
                for k2 in range(2):
                    kc = kp * 2 + k2
                    for tt in range(NT):
                        op('pe', lambda e: e.transpose(out=ps.b[:, k2 * 512 + tt * P:k2 * 512 + (tt + 1) * P],
                                                       in_=xsb[:, tt, kc * P:(kc + 1) * P], identity=identb_t[:]),
                           ['xsb%d' % tt, 'identb'], [ps.name])
                for k2 in range(2):
                    kc = kp * 2 + k2
                    src = ps.b[:, k2 * 512:(k2 + 1) * 512]
                    if kp % 2 == 0:
                        op('act', lambda e: e.activation(out=hT[:, kc, :], in_=src, func=AF.Identity,
                                                         scale=pt[:, c_gs + kc:c_gs + kc + 1],
                                                         bias=pt[:, c_sh + kc:c_sh + kc + 1]),
                           [ps.name, pn], ['hT%d' % kc])
                    else:
                        op('dve', lambda e: e.tensor_scalar(out=hT[:, kc, :], in0=src,
                                                            scalar1=pt[:, c_gs + kc:c_gs + kc + 1],
                                                            scalar2=pt[:, c_sh + kc:c_sh + kc + 1],
                                                            op0=ALU.mult, op1=ALU.add), [ps.name, pn], ['hT%d' % kc])

        def inproj(wt, wname, j):
            ps = newps()
            for kc in range(8):
                op('pe', lambda e: e.matmul(ps.t[:], lhsT=wt[:, kc, j * P:(j + 1) * P], rhs=hT[:, kc, :],
                                            start=(kc == 0), stop=(kc == 7)), [wname, 'hT%d' % kc], [ps.name])
            return ps

        def silu_from_ps(ps, out_ap, tmp_ap, rd, wr, wr_tmp, mul_eng='dve'):
            op('act', lambda e: e.activation(out=tmp_ap, in_=ps, func=AF.Exp, scale=-1.0), rd, wr_tmp)
            op('act', lambda e: e.activation(out=tmp_ap, in_=tmp_ap, func=AF.Ln, bias=1.0), wr_tmp, wr_tmp)
            op('act', lambda e: e.activation(out=tmp_ap, in_=tmp_ap, func=AF.Exp, scale=-1.0), wr_tmp, wr_tmp)
            op('dve', lambda e: e.tensor_tensor(out=out_ap, in0=ps, in1=tmp_ap, op=ALU.mult), rd + wr_tmp, wr)

        def conv_mixer(full, last_pre):
            wt, wn = ring_get()
            for j in range(4):
                ps = inproj(wt, wn, j)
                op('act', lambda e: e.activation(out=hcb[:, j, :], in_=ps.t[:], func=AF.Copy), [ps.name], ['oT%d' % j])
            wt, wn = ring_get()
            for j in range(4):
                ps = inproj(wt, wn, j)
                op('dve', lambda e: e.tensor_tensor(out=ucv[:, j, 8:8 + TB], in0=ps.t[:], in1=hcb[:, j, :], op=ALU.mult),
                   [ps.name, 'oT%d' % j], ['ucv'])
            if full:
                for j in range(4):
                    ps = newps()
                    for tap in range(3):
                        op('pe', lambda e: e.matmul(ps.t[:], lhsT=diagm[:, j, tap, :], rhs=ucv[:, j, 6 + tap:6 + tap + TB],
                                                    start=(tap == 0), stop=(tap == 2)), ['diagm', 'ucv'], [ps.name])
                    op('act', lambda e: e.activation(out=hcb[:, j, :], in_=ps.t[:], func=AF.Copy), [ps.name], ['oT%d' % j])
            if last_pre:
                op('pool', lambda e: e.tensor_scalar(out=ucv[:, :, 6:8], in0=ucv[:, :, TB + 6:TB + 8],
                                                     scalar1=prm[:, c_pm:c_pm + 1], scalar2=0.0, op0=ALU.mult, op1=ALU.add),
                   ['ucv', 'prm'], ['ucv'])
            else:
                op('pool', lambda e: e.tensor_copy(out=ucv[:, :, 6:8], in_=ucv[:, :, TB + 6:TB + 8]), ['ucv'], ['ucv'])
            if not full:
                return
            wt, wn = ring_get()
            for j in range(4):
                ps = inproj(wt, wn, j)
                op('dve', lambda e: e.tensor_tensor(out=yaj[:], in0=ps.t[:], in1=hcb[:, j, :], op=ALU.mult),
                   [ps.name, 'oT%d' % j], ['yaj'])
                op('act', lambda e: e.activation(out=sqb[:], in_=yaj[:], func=AF.Square), ['yaj'], ['sqb'])
                ps2 = newps()
                op('pe', lambda e: e.matmul(ps2.t[:], lhsT=blk1_t[:], rhs=sqb[:], start=True, stop=True),
                   ['blk1', 'sqb'], [ps2.name])
                op('act', lambda e: e.activation(out=rst[:], in_=ps2.t[:], func=AF.Ln, scale=1.0 / 64, bias=EPS),
                   [ps2.name], ['rst'])
                op('act', lambda e: e.activation(out=rst[:], in_=rst[:], func=AF.Exp, scale=-0.5), ['rst'], ['rst'])
                op('dve', lambda e: e.scalar_tensor_tensor(out=mixT[:, j, :], in0=yaj[:],
                                                           scalar=prm[:, c_gconv + j:c_gconv + j + 1], in1=rst[:],
                                                           op0=ALU.mult, op1=ALU.mult), ['yaj', 'rst', 'prm'], ['mixT%d' % j])

        def z_gate():
            wt, wn = ring_get()
            for h in range(4):
                ps = inproj(wt, wn, h)
                silu_from_ps(ps.t[:], zs[:, h, :], rst[:], [ps.name], ['zs'], ['rst'])

        def qkv_in(need_q):
            for t in range(3):
                if t == 0 and not need_q:
                    continue
                wt, wn = ring_get()
                for h in range(4):
                    cc = t * 4 + h
                    ps = inproj(wt, wn, h)
                    if h % 2 == 0:
                        op('act', lambda e: e.activation(out=qkvp[:, cc, 8:8 + TB], in_=ps.t[:], func=AF.Copy),
                           [ps.name], ['qkvp%d' % cc])
                    else:
                        op('dve', lambda e: e.tensor_copy(out=qkvp[:, cc, 8:8 + TB], in_=ps.t[:]),
                           [ps.name], ['qkvp%d' % cc])

        def ab_in():
            ps = newps()
            for tt in range(NT):
                for kc in range(8):
                    op('pe', lambda e: e.matmul(ps.t[:, tt * 8:(tt + 1) * 8], lhsT=hT[:, kc, tt * P:(tt + 1) * P],
                                                rhs=wab[:, kc, :], start=(kc == 0), stop=(kc == 7)),
                       ['hT%d' % kc, 'wab'], [ps.name])
            op('dve', lambda e: e.tensor_copy(out=absb[:].rearrange("p a b -> p (a b)"), in_=ps.t[:, 0:32]),
               [ps.name], ['absb'])

        (I_XA, I_ABS, I_E1, I_L1, I_G, I_E2, I_BETA, I_NBETA, I_GC, I_GL, I_EGL, I_ECOL, I_KDS) = range(13)
        J_SSQ, J_SSK, J_RQ, J_RK, J_SQ, J_SKBG, J_SKD = range(13, 20)

        def sl16(i):
            return sc[:, i, :]

        def sl4(i, c):
            return sc[:, i, c * 4:(c + 1) * 4]

        def bch(a2, n=4):
            return bass.AP(a2.tensor, a2.offset, [list(a2.ap[0]), [0, n], list(a2.ap[1])])

        PSA = [PS(i) for i in range(4)]
        psB_i = [0]

        def newpsB():
            i = psB_i[0]
            psB_i[0] = (i + 1) % 3
            return PS(4 + i)

        def dn_scalars():
            dv = lambda f, r, w: op('dve', f, r, w)
            ac = lambda f, r, w: op('act', f, r, w)
            v3 = lambda i: sc[:, i, :].rearrange("p (a b) -> p a b", a=4)
            dv(lambda e: e.tensor_tensor(out=v3(I_XA), in0=absb[:, :, 0:4], in1=bch(prm[:, c_dtb:c_dtb + 4]), op=ALU.add),
               ['absb', 'prm'], ['sc_xa'])
            ac(lambda e: e.activation(out=sl16(I_ABS), in_=sl16(I_XA), func=AF.Abs), ['sc_xa'], ['sc_abs'])
            ac(lambda e: e.activation(out=sl16(I_E1), in_=sl16(I_ABS), func=AF.Exp, scale=-1.0), ['sc_abs'], ['sc_e1'])
            ac(lambda e: e.activation(out=sl16(I_L1), in_=sl16(I_E1), func=AF.Ln, bias=1.0), ['sc_e1'], ['sc_l1'])
            dv(lambda e: e.scalar_tensor_tensor(out=sl16(I_G), in0=sl16(I_XA), scalar=0.0, in1=sl16(I_L1),
                                                op0=ALU.max, op1=ALU.add), ['sc_xa', 'sc_l1'], ['sc_g'])
            dv(lambda e: e.tensor_tensor(out=v3(I_G), in0=v3(I_G), in1=bch(prm[:, c_negA:c_negA + 4]), op=ALU.mult),
               ['sc_g', 'prm'], ['sc_g'])
            ac(lambda e: e.activation(out=v3(I_E2), in_=absb[:, :, 4:8], func=AF.Exp, scale=-1.0), ['absb'], ['sc_e2'])
            dv(lambda e: e.tensor_scalar(out=sl16(I_E2), in0=sl16(I_E2), scalar1=1.0, scalar2=None, op0=ALU.add),
               ['sc_e2'], ['sc_e2'])
            dv(lambda e: e.reciprocal(out=sl16(I_BETA), in_=sl16(I_E2)), ['sc_e2'], ['sc_beta'])
            dv(lambda e: e.tensor_scalar(out=sl16(I_NBETA), in0=sl16(I_BETA), scalar1=-1.0, scalar2=None, op0=ALU.mult),
               ['sc_beta'], ['sc_nbeta'])
            ps = newps()
            op('pe', lambda e: e.matmul(ps.t[:, 0:16], lhsT=Uf, rhs=sl16(I_G), start=True, stop=True),
               ['cst', 'sc_g'], [ps.name])
            op('pe', lambda e: e.matmul(ps.t[:, 16:32], lhsT=onesf, rhs=sl16(I_G), start=True, stop=True),
               ['cst', 'sc_g'], [ps.name])
            dv(lambda e: e.tensor_copy(out=sl16(I_GC), in_=ps.t[:, 0:16]), [ps.name], ['sc_gc'])
            dv(lambda e: e.tensor_copy(out=sl16(I_GL), in_=ps.t[:, 16:32]), [ps.name], ['sc_gl'])
            ac(lambda e: e.activation(out=sl16(I_EGL), in_=sl16(I_GL), func=AF.Exp), ['sc_gl'], ['sc_egl'])
            ac(lambda e: e.activation(out=sl16(I_ECOL), in_=sl16(I_GC), func=AF.Exp), ['sc_gc'], ['sc_ecol'])
            dv(lambda e: e.tensor_tensor(out=sl16(I_KDS), in0=sl16(I_GL), in1=sl16(I_GC), op=ALU.subtract),
               ['sc_gl', 'sc_gc'], ['sc_kds'])
            ac(lambda e: e.activation(out=sl16(I_KDS), in_=sl16(I_KDS), func=AF.Exp), ['sc_kds'], ['sc_kds'])

        def prep_gen(c, need_q):
            ob = c % 2
            sfx = '_%d' % ob
            dv = lambda f, r, w: op('dve', f, r, w)
            ac = lambda f, r, w: op('act', f, r, w)
            po = lambda f, r, w: op('pool', f, r, w)
            types = [0, 1, 2] if need_q else [1, 2]
            gcb = bc(sl4(I_GC, c), P)
            def colbc(slot, h):
                a1 = sc[:, slot, c * 4 + h:c * 4 + h + 1]
                return bass.AP(a1.tensor, a1.offset, [list(a1.ap[0]), [0, P]])
            ps_gr = PSA[0]
            for h in range(4):
                op('pe', lambda e: e.transpose(out=ps_gr.t[:, h * P:(h + 1) * P], in_=colbc(I_GC, h), identity=identf),
                   ['sc_gc', 'cst'], [ps_gr.name])
            gr3 = ps_gr.v3()
            yield
            if need_q:
                ac(lambda e: e.activation(out=Erow, in_=gr3, func=AF.Exp), [ps_gr.name], ['Erow'])
            dv(lambda e: e.tensor_tensor(out=dm0, in0=gr3, in1=gcb, op=ALU.subtract), [ps_gr.name, 'sc_gc'], ['dm0'])
            ps_br = PSA[1]
            for h in range(4):
                op('pe', lambda e: e.transpose(out=ps_br.t[:, h * P:(h + 1) * P], in_=colbc(I_BETA, h), identity=identf),
                   ['sc_beta', 'cst'], [ps_br.name])
            yield
            dv(lambda e: e.tensor_scalar(out=dm1, in0=dm0, scalar1=0.0, scalar2=None, op0=ALU.max), ['dm0'], ['dm1'])
            po(lambda e: e.tensor_scalar(out=dm2, in0=dm0, scalar1=0.0, scalar2=-3.0e38, op0=ALU.min, op1=ALU.max), ['dm0'], ['dm2'])
            ac(lambda e: e.activation(out=dm1, in_=dm1, func=AF.Exp, scale=-1.0), ['dm1'], ['dm1'])
            ac(lambda e: e.activation(out=dm2, in_=dm2, func=AF.Exp), ['dm2'], ['dm2'])
            yield
            f2 = lambda a: a.rearrange("p a b -> p (a b)")
            po(lambda e: e.tensor_tensor(out=f2(dm1), in0=f2(dm1), in1=f2(m_cs[:]), op=ALU.mult), ['dm1', 'm_cs'], ['dm1'])
            po(lambda e: e.tensor_tensor(out=dm1, in0=dm1, in1=bc(sl4(I_NBETA, c), P), op=ALU.mult), ['dm1', 'sc_nbeta'], ['dm1'])
            po(lambda e: e.tensor_tensor(out=dm1b, in0=dm1, in1=bch(cmb[:, 0, :]), op=ALU.mult), ['dm1', 'cmb'], ['dm1b'])
            po(lambda e: e.tensor_tensor(out=f2(dm2), in0=f2(dm2), in1=f2(m_sc[:]), op=ALU.mult), ['dm2', 'm_sc'], ['dm2'])
            dv(lambda e: e.tensor_tensor(out=f2(dm0), in0=ps_br.t[:], in1=f2(m_scn[:]), op=ALU.mult),
               [ps_br.name, 'm_scn'], ['dm0'])
            po(lambda e: e.tensor_tensor(out=f2(dm0), in0=f2(dm0), in1=f2(dm2), op=ALU.mult), ['dm0', 'dm2'], ['dm0'])
            yield
            Tps = {}
            for ti, t in enumerate(types):
                ps = PSA[t]
                for h in range(4):
                    cc = t * 4 + h
                    for tap in range(4):
                        op('pe', lambda e: e.matmul(ps.t[:, h * P:(h + 1) * P], lhsT=diagq[:, cc, tap, :],
                                                    rhs=qkvp[:, cc, 5 + tap + c * P:5 + tap + (c + 1) * P],
                                                    start=(tap == 0), stop=(tap == 3)),
                           ['diagq', 'qkvp%d' % cc], [ps.name])
                silu_from_ps(ps.t[:], cs[:, t, :], e1t[:, ti % 2, :], [ps.name], ['cs%d' % t], ['e1t%d' % (ti % 2)])
                yield
            for t in types:
                ps = PSA[t]
                Tps[t] = ps
                for h in range(4):
                    op('pe', lambda e: e.transpose(out=ps.t[:, h * P:(h + 1) * P], in_=cs[:, t, h * P:(h + 1) * P],
                                                   identity=identf), ['cs%d' % t, 'cst'], [ps.name])
            yield
            for t, islot, rslot in ((0, J_SSQ, J_RQ), (1, J_SSK, J_RK)):
                if t not in types:
                    continue
                ac(lambda e: e.activation(out=sqj, in_=Tps[t].v3(), func=AF.Square), [Tps[t].name], ['e1t0'])
                dv(lambda e: e.tensor_reduce(out=sl4(islot, ob), in_=sqj, axis=AX.X, op=ALU.add), ['e1t0'], ['sc_ss%d' % t + sfx])
                ac(lambda e: e.activation(out=sl4(rslot, ob), in_=sl4(islot, ob), func=AF.Ln, bias=EPS),
                   ['sc_ss%d' % t + sfx], ['sc_r%d' % t + sfx])
                ac(lambda e: e.activation(out=sl4(rslot, ob), in_=sl4(rslot, ob), func=AF.Exp, scale=-0.5),
                   ['sc_r%d' % t + sfx], ['sc_r%d' % t + sfx])
                yield
            dv(lambda e: e.tensor_tensor(out=sl4(J_SKBG, ob), in0=sl4(J_RK, ob), in1=sl4(I_BETA, c), op=ALU.mult),
               ['sc_r1' + sfx, 'sc_beta'], ['sc_skbg' + sfx])
            dv(lambda e: e.tensor_tensor(out=sl4(J_SKBG, ob), in0=sl4(J_SKBG, ob), in1=sl4(I_ECOL, c), op=ALU.mult),
               ['sc_skbg' + sfx, 'sc_ecol'], ['sc_skbg' + sfx])
            dv(lambda e: e.tensor_tensor(out=sl4(J_SKD, ob), in0=sl4(J_RK, ob), in1=sl4(I_KDS, c), op=ALU.mult),
               ['sc_r1' + sfx, 'sc_kds'], ['sc_skd' + sfx])
            Tk = Tps[1].v3()
            Tv = Tps[2].v3()
            dv(lambda e: e.tensor_tensor(out=kh, in0=Tk, in1=bc(sl4(J_RK, ob), P), op=ALU.mult),
               [Tps[1].name, 'sc_r1' + sfx], ['kh'])
            dv(lambda e: e.tensor_tensor(out=kbg[ob], in0=Tk, in1=bc(sl4(J_SKBG, ob), P), op=ALU.mult),
               [Tps[1].name, 'sc_skbg' + sfx], ['kbg' + sfx])
            yield
            dv(lambda e: e.tensor_tensor(out=kdec[ob], in0=Tk, in1=bc(sl4(J_SKD, ob), P), op=ALU.mult),
               [Tps[1].name, 'sc_skd' + sfx], ['kdec' + sfx])
            dv(lambda e: e.tensor_tensor(out=vb[ob], in0=Tv, in1=bc(sl4(I_BETA, c), P), op=ALU.mult),
               [Tps[2].name, 'sc_beta'], ['vb' + sfx])
            if need_q:
                dv(lambda e: e.tensor_scalar(out=sl4(J_SQ, ob), in0=sl4(J_RQ, ob), scalar1=DK_SCALE, scalar2=None, op0=ALU.mult),
                   ['sc_r0' + sfx], ['sc_sq' + sfx])
                dv(lambda e: e.tensor_tensor(out=qh, in0=Tps[0].v3(), in1=bc(sl4(J_SQ, ob), P), op=ALU.mult),
                   [Tps[0].name, 'sc_sq' + sfx], ['qh'])
            yield
            ps_k = PSA[3]
            for h in range(4):
                op('pe', lambda e: e.transpose(out=ps_k.t[:, h * P:(h + 1) * P], in_=kh[:, h, :], identity=identf),
                   ['kh', 'cst'], [ps_k.name])
            ac(lambda e: e.activation(out=khT, in_=ps_k.v3(), func=AF.Copy), [ps_k.name], ['khT'])
            if need_q:
                ps_q = PSA[0]
                for h in range(4):
                    op('pe', lambda e: e.transpose(out=ps_q.t[:, h * P:(h + 1) * P], in_=qh[:, h, :], identity=identf),
                       ['qh', 'cst'], [ps_q.name])
                ac(lambda e: e.activation(out=qhT, in_=ps_q.v3(), func=AF.Copy), [ps_q.name], ['qhT'])
                dv(lambda e: e.tensor_tensor(out=qdT[ob], in0=ps_q.v3(), in1=Erow, op=ALU.mult), [ps_q.name, 'Erow'], ['qdT' + sfx])
            yield
            ps_G = PSA[1]
            for h in range(4):
                op('pe', lambda e: e.matmul(ps_G.t[:, h * P:(h + 1) * P], lhsT=khT[:, h, :], rhs=khT[:, h, :],
                                            start=True, stop=True), ['khT'], [ps_G.name])
            if need_q:
                ps_A = PSA[2]
                for h in range(4):
                    op('pe', lambda e: e.matmul(ps_A.t[:, h * P:(h + 1) * P], lhsT=khT[:, h, :], rhs=qhT[:, h, :],
                                                start=True, stop=True), ['khT', 'qhT'], [ps_A.name])
            dv(lambda e: e.tensor_tensor(out=Lbd[ob], in0=ps_G.v3(), in1=dm1b, op=ALU.mult), [ps_G.name, 'dm1b'], ['Lbd' + sfx])
            dv(lambda e: e.tensor_tensor(out=Mbd[ob], in0=ps_G.v3(), in1=dm0, op=ALU.mult), [ps_G.name, 'dm0'], ['Mbd' + sfx])
            dv(lambda e: e.tensor_tensor(out=Nf[ob], in0=ps_G.v3(), in1=dm1, op=ALU.mult), [ps_G.name, 'dm1'], ['Nf' + sfx])
            if need_q:
                dv(lambda e: e.tensor_tensor(out=attnT[ob], in0=ps_A.v3(), in1=dm2, op=ALU.mult), [ps_A.name, 'dm2'], ['attnT' + sfx])
            yield

        def inv_gen(c, need_q):
            ob = c % 2
            sfx = '_%d' % ob
            dv = lambda f, r, w: op('dve', f, r, w)
            ac = lambda f, r, w: op('act', f, r, w)
            idb3 = bch(identb_t[:])
            dv(lambda e: e.tensor_tensor(out=Qb, in0=Mbd[ob], in1=idb3, op=ALU.add), ['Mbd' + sfx, 'identb'], ['Q'])
            Lc, Mc, Lcn, Mcn = Lbd[ob], Mbd[ob], 'Lbd' + sfx, 'Mbd' + sfx
            for ki, k in enumerate((1, 2, 4, 8)):
                nx = ki % 2
                if k > 1:
                    psQ = newpsB()
                    for h in range(4):
                        op('pe', lambda e: e.matmul(psQ.t[:, h * P:(h + 1) * P], lhsT=Lc[:, h, :], rhs=Qb[:, h, :],
                                                    start=True, stop=True), [Lcn, 'Q'], [psQ.name])
                if k < 8:
                    psA = newpsB()
                    psB = newpsB()
                    for h in range(4):
                        op('pe', lambda e: e.matmul(psA.t[:, h * P:(h + 1) * P], lhsT=Mc[:, h, :], rhs=Lc[:, h, :],
                                                    start=True, stop=True), [Lcn, Mcn], [psA.name])
                    for h in range(4):
                        op('pe', lambda e: e.matmul(psB.t[:, h * P:(h + 1) * P], lhsT=Lc[:, h, :], rhs=Mc[:, h, :],
                                                    start=True, stop=True), [Lcn, Mcn], [psB.name])
                yield
                if k > 1:
                    dv(lambda e: e.tensor_tensor(out=Qb, in0=psQ.v3(), in1=Qb, op=ALU.add), [psQ.name, 'Q'], ['Q'])
                if k < 8:
                    ac(lambda e: e.activation(out=Lt[nx], in_=psA.v3(), func=AF.Copy), [psA.name], ['Lt%d' % nx])
                    ac(lambda e: e.activation(out=Mt[nx], in_=psB.v3(), func=AF.Copy), [psB.name], ['Mt%d' % nx])
                    Lc, Mc, Lcn, Mcn = Lt[nx], Mt[nx], 'Lt%d' % nx, 'Mt%d' % nx
                yield
            Xn, Wp, Xnn, Wpn = Lt[1], Mt[1], 'Lt1', 'Mt1'
            for lvl in (1, 2, 3):
                psT = newpsB()
                for h in range(4):
                    op('pe', lambda e: e.transpose(out=psT.b[:, h * P:(h + 1) * P], in_=Qb[:, h, :], identity=identb_t[:]),
                       ['Q', 'identb'], [psT.name])
                psW = newpsB()
                for h in range(4):
                    op('pe', lambda e: e.matmul(psW.t[:, h * P:(h + 1) * P], lhsT=Nf[ob][:, h, :], rhs=Qb[:, h, :],
                                                start=True, stop=True), ['Nf' + sfx, 'Q'], [psW.name])
                yield
                ac(lambda e: e.activation(out=Xn, in_=psT.b[:, 0:512].rearrange("p (a b) -> p a b", a=4), func=AF.Copy),
                   [psT.name], [Xnn])
                dv(lambda e: e.tensor_tensor(out=Wp, in0=psW.v3(), in1=bch(cmb[:, lvl, :]), op=ALU.mult),
                   [psW.name, 'cmb'], [Wpn])
                psZ = newpsB()
                for h in range(4):
                    op('pe', lambda e: e.matmul(psZ.t[:, h * P:(h + 1) * P], lhsT=Xn[:, h, :], rhs=Wp[:, h, :],
                                                start=True, stop=True), [Xnn, Wpn], [psZ.name])
                yield
                dv(lambda e: e.tensor_tensor(out=Qb, in0=psZ.v3(), in1=Qb, op=ALU.add), [psZ.name, 'Q'], ['Q'])
            ps_w = newpsB()
            ps_u = newpsB()
            for h in range(4):
                op('pe', lambda e: e.matmul(ps_w.t[:, h * P:(h + 1) * P], lhsT=kbg[ob][:, h, :], rhs=Qb[:, h, :],
                                            start=True, stop=True), ['kbg' + sfx, 'Q'], [ps_w.name])
            for h in range(4):
                op('pe', lambda e: e.matmul(ps_u.t[:, h * P:(h + 1) * P], lhsT=Qb[:, h, :], rhs=vb[ob][:, h, :],
                                            start=True, stop=True), ['vb' + sfx, 'Q'], [ps_u.name])
            yield
            ac(lambda e: e.activation(out=wT, in_=ps_w.v3(), func=AF.Copy), [ps_w.name], ['wT'])
            ac(lambda e: e.activation(out=u_sb, in_=ps_u.v3(), func=AF.Copy), [ps_u.name], ['u'])
            ps_p = PS(7)
            for h in range(4):
                op('pe', lambda e: e.matmul(ps_p.t[:, h * P:(h + 1) * P], lhsT=wT[:, h, :], rhs=Sbf[:, h, :],
                                            start=True, stop=True), ['wT', 'Sbf'], [ps_p.name])
            yield
            dv(lambda e: e.tensor_tensor(out=vnew, in0=u_sb, in1=ps_p.v3(), op=ALU.subtract), ['u', ps_p.name], ['vnew'])
            if need_q:
                ps_o = PS(7)
                for h in range(4):
                    op('pe', lambda e: e.matmul(ps_o.t[:, h * P:(h + 1) * P], lhsT=Sbf[:, h, :], rhs=qdT[ob][:, h, :],
                                                start=True, stop=False), ['Sbf', 'qdT' + sfx], [ps_o.name])
                    op('pe', lambda e: e.matmul(ps_o.t[:, h * P:(h + 1) * P], lhsT=vnew[:, h, :], rhs=attnT[ob][:, h, :],
                                                start=False, stop=True), ['vnew', 'attnT' + sfx], [ps_o.name])
            ps_s = PS(3)
            for h in range(4):
                op('pe', lambda e: e.matmul(ps_s.t[:, h * P:(h + 1) * P], lhsT=kdec[ob][:, h, :], rhs=vnew[:, h, :],
                                            start=True, stop=True), ['kdec' + sfx, 'vnew'], [ps_s.name])
            yield
            if need_q:
                ac(lambda e: e.activation(out=oT[:, :, c * P:(c + 1) * P], in_=ps_o.v3(), func=AF.Copy), [ps_o.name], ['oT0', 'oT1', 'oT2', 'oT3'])
            for h in range(4):
                dv(lambda e: e.scalar_tensor_tensor(out=S32[:, h, :], in0=S32[:, h, :],
                                                    scalar=sc[:, I_EGL, c * 4 + h:c * 4 + h + 1],
                                                    in1=ps_s.t[:, h * P:(h + 1) * P], op0=ALU.mult, op1=ALU.add),
                   ['S32', 'sc_egl', ps_s.name], ['S32'])
            ac(lambda e: e.activation(out=Sbf[:], in_=S32[:], func=AF.Copy), ['S32'], ['Sbf'])
            yield

        def run_gens(*gens):
            gens = [g for g in gens if g is not None]
            while gens:
                for g in list(gens):
                    try:
                        next(g)
                    except StopIteration:
                        gens.remove(g)

        def dn_block(need_q):
            dn_scalars()
            run_gens(prep_gen(0, need_q))
            for c in range(4):
                run_gens(inv_gen(c, need_q), prep_gen(c + 1, need_q) if c < 3 else None)

        def halo_qkv(last_pre, need_q):
            for cc in range(12):
                if cc < 4 and not need_q and not last_pre:
                    continue
                nm = 'qkvp%d' % cc
                eng = 'pool' if cc % 2 == 0 else 'dve'
                if last_pre:
                    op(eng, lambda e: e.tensor_scalar(out=qkvp[:, cc, 5:8], in0=qkvp[:, cc, TB + 5:TB + 8],
                                                      scalar1=prm[:, c_pm:c_pm + 1], scalar2=0.0, op0=ALU.mult, op1=ALU.add),
                       [nm, 'prm'], [nm])
                else:
                    op(eng, lambda e: e.tensor_copy(out=qkvp[:, cc, 5:8], in_=qkvp[:, cc, TB + 5:TB + 8]), [nm], [nm])

        def dn_out():
            rtmp = [cs[:, 0, :], cs[:, 1, :], cs[:, 2, :], e1t[:, 0, :]]
            rnm = ['cs0', 'cs1', 'cs2', 'e1t0']
            stmp = [t_.rearrange("p a b -> p (a b)") for t_ in (Lt[0], Lt[1], Mt[0], Mt[1])]
            snm = ['Lt0', 'Lt1', 'Mt0', 'Mt1']
            for h in range(4):
                r_, rn, q_, qn = rtmp[h], rnm[h], stmp[h], snm[h]
                op('act', lambda e: e.activation(out=q_, in_=oT[:, h, :], func=AF.Square), ['oT%d' % h], [qn])
                ps = newps()
                op('pe', lambda e: e.matmul(ps.t[:], lhsT=onesb_t[:], rhs=q_, start=True, stop=True),
                   ['onesb', qn], [ps.name])
                op('act', lambda e: e.activation(out=r_, in_=ps.t[:], func=AF.Ln, scale=1.0 / P, bias=EPS),
                   [ps.name], [rn])
                op('act', lambda e: e.activation(out=r_, in_=r_, func=AF.Exp, scale=-0.5), [rn], [rn])
                op('dve', lambda e: e.scalar_tensor_tensor(out=r_, in0=oT[:, h, :],
                                                           scalar=prm[:, c_gdn:c_gdn + 1], in1=r_,
                                                           op0=ALU.mult, op1=ALU.mult), ['oT%d' % h, rn, 'prm'], [rn])
                op('pool', lambda e: e.tensor_tensor(out=mixT[:, 4 + h, :], in0=r_, in1=zs[:, h, :], op=ALU.mult),
                   [rn, 'zs'], ['mixT%d' % (4 + h)])

        def out_proj():
            w0, n0, w1, n1 = ring_get_n(2)
            for tt in range(NT):
                for hf, (wt, wn) in enumerate(((w0, n0), (w1, n1))):
                    ps = newps()
                    for kc in range(8):
                        op('pe', lambda e: e.matmul(ps.t[:], lhsT=mixT[:, kc, tt * P:(tt + 1) * P], rhs=wt[:, kc, :],
                                                    start=(kc == 0), stop=(kc == 7)), ['mixT%d' % kc, wn], [ps.name])
                    yt, yn = (yaj, 'yaj') if hf == 0 else (yaj2, 'yaj2')
                    op('dve', lambda e: e.tensor_tensor(out=yt[:], in0=ps.t[:], in1=gate_m_bc[:, hf * 512:(hf + 1) * 512],
                                                        op=ALU.mult), [ps.name, 'modbc'], [yn])
                    xs_ = X[:, tt, hf * 512:(hf + 1) * 512]
                    op('pool' if hf == 0 else 'dve', lambda e: e.tensor_tensor(out=xs_, in0=xs_, in1=yt[:], op=ALU.add),
                       ['X%d' % tt, yn], ['X%d' % tt])

        def ffn():
            for g in range(6):
                wg, ng, wu, nu = ring_get_n(2)
                nj = min(4, NHC - g * 4)
                for j in range(nj):
                    J = g * 4 + j
                    psg = newps()
                    psu = newps()
                    for kc in range(8):
                        op('pe', lambda e: e.matmul(psg.t[:], lhsT=wg[:, kc, j * P:(j + 1) * P], rhs=hT[:, kc, :],
                                                    start=(kc == 0), stop=(kc == 7)), [ng, 'hT%d' % kc], [psg.name])
                    for kc in range(8):
                        op('pe', lambda e: e.matmul(psu.t[:], lhsT=wu[:, kc, j * P:(j + 1) * P], rhs=hT[:, kc, :],
                                                    start=(kc == 0), stop=(kc == 7)), [nu, 'hT%d' % kc], [psu.name])
                    op('act', lambda e: e.activation(out=rst[:], in_=psg.t[:], func=AF.Silu), [psg.name], ['rst'])
                    op('dve', lambda e: e.tensor_tensor(out=actT[:, J, :], in0=psu.t[:], in1=rst[:], op=ALU.mult),
                       [psu.name, 'rst'], ['actT%d' % J])
            for hf in range(2):
                accs = [newps() for _ in range(NT)]
                for G in range(3):
                    wd, nd = ring_get()
                    n = min(8, NHC - G * 8)
                    for tt in range(NT):
                        for j in range(n):
                            J = G * 8 + j
                            op('pe', lambda e: e.matmul(accs[tt].t[:], lhsT=actT[:, J, tt * P:(tt + 1) * P], rhs=wd[:, j, :],
                                                        start=(J == 0), stop=(J == NHC - 1)), ['actT%d' % J, nd], [accs[tt].name])
                for tt in range(NT):
                    yt, yn = (yaj, 'yaj') if tt % 2 == 0 else (yaj2, 'yaj2')
                    op('dve', lambda e: e.tensor_tensor(out=yt[:], in0=accs[tt].t[:],
                                                        in1=gate_f_bc[:, hf * 512:(hf + 1) * 512], op=ALU.mult),
                       [accs[tt].name, 'modbc'], [yn])
                    xs_ = X[:, tt, hf * 512:(hf + 1) * 512]
                    op('pool' if tt % 2 == 0 else 'dve', lambda e: e.tensor_tensor(out=xs_, in0=xs_, in1=yt[:], op=ALU.add),
                       ['X%d' % tt, yn], ['X%d' % tt])

        def final_norm():
            for tt in range(NT):
                op('act', lambda e: e.activation(out=xsb[:, tt, :], in_=X[:, tt, :], func=AF.Square,
                                                 accum_out=ssx[:, 4 + tt:5 + tt]), ['X%d' % tt], ['xsb%d' % tt, 'ssxf'])
            op('act', lambda e: e.activation(out=ssx[:, 4:8], in_=ssx[:, 4:8], func=AF.Ln, scale=1.0 / D, bias=EPS),
               ['ssxf'], ['ssxf'])
            op('act', lambda e: e.activation(out=ssx[:, 4:8], in_=ssx[:, 4:8], func=AF.Exp, scale=-0.5), ['ssxf'], ['ssxf'])
            for tt in range(NT):
                op('dve', lambda e: e.scalar_tensor_tensor(out=X[:, tt, :], in0=X[:, tt, :], scalar=ssx[:, 4 + tt:5 + tt],
                                                           in1=gs_o_bc, op0=ALU.mult, op1=ALU.mult),
                   ['X%d' % tt, 'ssxf', 'modbc'], ['X%d' % tt])
                op('pool' if tt % 2 == 0 else 'dve', lambda e: e.tensor_tensor(out=X[:, tt, :], in0=X[:, tt, :], in1=sh_o_bc, op=ALU.add),
                   ['X%d' % tt, 'modbc'], ['X%d' % tt])

        def dump(name, ap_, rd):
            if name in dbg_d:
                dma('pool', dbg_d[name], ap_, rd, [], 'd_dbg')

        xpre_v = x_pre.rearrange("(n t p) d -> n p t d", t=NT, p=P)
        xown_v = x_own.rearrange("(n t p) d -> n p t d", t=NT, p=P)
        out_v = out_d.rearrange("(n t p) d -> n p t d", t=NT, p=P)

        S.mark('prologue')
        for blk in range(nblk_pre):
            last = (blk == nblk_pre - 1)
            if MODEL_MARKS:
                S.mark('pre%d' % blk)
            if EXP_PRE:
                S.REN = {n_: n_ + '_%d' % (blk % 2) for n_ in EXP_PRE}
            load_x(xpre_v[blk])
            norm_T(c_gsm, c_shm)
            njob = -(-len(bc_jobs) // (nblk_pre - blk))
            for _ in range(njob):
                run_job(bc_jobs.pop(0))
            if last:
                conv_mixer(False, True)
            qkv_in(last)
            ab_in()
            dn_block(False)
            halo_qkv(last, False)
            if last:
                op('dve', lambda e: e.tensor_scalar(out=S32[:], in0=S32[:], scalar1=prm[:, c_pm:c_pm + 1], scalar2=None,
                                                    op0=ALU.mult), ['S32', 'prm'], ['S32'])
                op('act', lambda e: e.activation(out=Sbf[:], in_=S32[:], func=AF.Copy), ['S32'], ['Sbf'])

        while bc_jobs:
            run_job(bc_jobs.pop(0))
        if stop <= 2:
            nblk_own = 0
        S.REN = {}
        for blk in range(nblk_own):
            if EXPERIMENT:
                S.REN = {n_: n_ + '_%d' % (blk % 2) for n_ in EXPERIMENT}
            if MODEL_MARKS:
                S.mark('own%d' % blk)
            load_x(xown_v[blk])
            norm_T(c_gsm, c_shm)
            conv_mixer(True, False)
            z_gate()
            qkv_in(True)
            ab_in()
            if MODEL_MARKS:
                S.mark('  inproj')
            if stop <= 3:
                break
            dn_block(True)
            halo_qkv(False, True)
            if MODEL_MARKS:
                S.mark('  dn')
            if stop <= 4:
                break
            dn_out()
            if blk == 0:
                dump('mixT', mixT[:], MIXN)
            out_proj()
            if blk == 0:
                dump('x1', X[:], XN)
            if MODEL_MARKS:
                S.mark('  outproj')
            if stop <= 5:
                break
            norm_T(0, 8, prmf, 'prmf')
            if stop <= 6:
                break
            ffn()
            if MODEL_MARKS:
                S.mark('  ffn')
            if stop <= 7:
                break
            final_norm()
            for tt in range(NT):
                dma('sp', out_v[blk][:, tt, :], X[:, tt, :], ['X%d' % tt], [], 'd_out%d' % tt)
        S.finish('sp')
        build_nc.stats = (S.nops, S.nwaits)
        build_nc.sim_time = S.sim_time
        build_nc.busy = S.busy
        build_nc.marks = getattr(S, 'marks', [])
        build_nc.S = S
    return nc


def _pp(v):
    return np.ascontiguousarray(np.asarray(v, np.float32).reshape(8, 128).T)


def _consts():
    p = np.arange(128)
    ident = np.eye(128, dtype=np.float32)
    ones = np.ones((128, 128), np.float32)
    U = (p[:, None] <= p[None, :]).astype(np.float32)
    blk = ((p[:, None] // 64) == (p[None, :] // 64)).astype(np.float32)
    strict = (p[:, None] > p[None, :]).astype(np.float32)
    return np.ascontiguousarray(np.stack([ident, ones, U, blk, strict], axis=1))


def _cmask():
    p = np.arange(128)
    r, c = p[:, None], p[None, :]
    ms = [((r // 16) == (c // 16)).astype(np.float32)]
    for b in (16, 32, 64):
        ms.append((((r // (2 * b)) == (c // (2 * b))) & ((r // b) % 2 == 0) & ((c // b) % 2 == 1)).astype(np.float32))
    return np.ascontiguousarray(np.stack(ms, axis=1))


def make_in_maps(inp, nblk_own=8, nblk_pre=8):
    f = lambda a: np.ascontiguousarray(np.asarray(a, np.float32))
    x = f(inp['x'])
    c = f(inp['c'])
    w_ada = f(inp['w_ada'][0])
    b_ada = f(inp['b_ada'][0])
    w_adaf = f(inp['w_ada_final'])
    b_adaf = f(inp['b_ada_final'])
    bsl = lambda i: b_ada[i * D:(i + 1) * D]
    b_pp = np.ascontiguousarray(np.stack([_pp(bsl(0)), _pp(bsl(1)), _pp(bsl(3)), _pp(bsl(4))], axis=1))
    bb = np.stack([bsl(2), bsl(5), b_adaf[0:D], b_adaf[D:2 * D]], axis=0)
    b_bc = np.ascontiguousarray(np.broadcast_to(bb[None], (P, 4, D)))
    gfin_bc = np.ascontiguousarray(np.broadcast_to(f(inp['g_norm_final'])[None], (P, D)))
    g_pp = np.ascontiguousarray(np.stack([_pp(inp['g_norm_mix'][0]), _pp(inp['g_norm_ffn'][0])], axis=1))
    cwm = f(inp['conv_w_mix'][0])
    cw_mix = np.ascontiguousarray(cwm.T.reshape(4, 128, 3).transpose(1, 0, 2))
    cwq = f(inp['conv_w_qkv'][0])
    cw_qkv = np.ascontiguousarray(cwq.T.reshape(12, 128, 4).transpose(1, 0, 2))
    g_conv = np.ascontiguousarray(f(inp['g_conv_out'][0]).reshape(4, 128).T)
    g_dn = np.ascontiguousarray(f(inp['g_dn_out'][0]).reshape(128, 1))
    alog_bc = np.ascontiguousarray(np.broadcast_to(f(inp['a_log'][0])[None], (P, 4)))
    dtb_bc = np.ascontiguousarray(np.broadcast_to(f(inp['dt_bias'][0])[None], (P, 4)))
    shared = {
        "w_ada": w_ada, "w_ada_final": w_adaf, "b_pp": b_pp, "b_bc": b_bc, "gfin_bc": gfin_bc, "g_pp": g_pp,
        "w_in": f(inp['w_in'][0]), "w_out": f(inp['w_out'][0]), "w_gu": f(inp['w_gate_up'][0]),
        "w_down": f(inp['w_down'][0]), "cw_mix": cw_mix, "cw_qkv": cw_qkv, "g_conv": g_conv, "g_dn": g_dn,
        "alog_bc": alog_bc, "dtb_bc": dtb_bc, "consts": _consts(), "cmask": _cmask(),
    }
    n_own = nblk_own * TB
    n_pre = max(nblk_pre, 1) * TB
    S_ = x.shape[1]
    halfS = S_ // 2
    maps = []
    for core in range(8):
        b, half = core // 2, core % 2
        m = dict(shared)
        s0 = half * halfS
        m["x_own"] = np.ascontiguousarray(x[b, s0:s0 + n_own])
        p0 = s0 - n_pre if half == 1 else 0
        m["x_pre"] = np.ascontiguousarray(x[b, p0:p0 + n_pre])
        m["pmask"] = np.full((P, 1), float(half), np.float32)
        m["cT"] = _pp(c[b])
        maps.append(m)
    return maps


_NC_CACHE = {}


def kernel(**inputs):
    if 'nc' not in _NC_CACHE:
        _NC_CACHE['nc'] = build_nc()
    nc = _NC_CACHE['nc']
    maps = make_in_maps(inputs)
    res = run_bass_kernel_spmd(nc, maps, core_ids=list(range(8)))
    B, S_, _ = inputs['x'].shape
    out = np.empty((B, S_, D), np.float32)
    halfS = S_ // 2
    for core in range(8):
        b, half = core // 2, core % 2
        out[b, half * halfS:(half + 1) * halfS] = res.results[core]["out"]
    return out
```

```python
import numpy as np
from contextlib import ExitStack
import concourse.bass as bass
import concourse.mybir as mybir
from concourse.bass_utils import run_bass_kernel_spmd

F32 = mybir.dt.float32
BF = mybir.dt.bfloat16
AF = mybir.ActivationFunctionType
ALU = mybir.AluOpType
AX = mybir.AxisListType

P = 128
D = 1024
TB = 512
NT = 4
PIN = 3592
HID = 2816
NHC = 22
EPS = 1e-6
NSLOT = 4
KEEPWARM_GAP = 1.0
KEEPWARM_STEP = 0.8
MODEL_MARKS = False
EXPERIMENT = None
EXP_PRE = None
DK_SCALE = 128 ** -0.5


def PRIO(st, bl):
    return st - 0.03 * bl


class _Cap:
    def __init__(self):
        self.call = None

    def __getattr__(self, name):
        def f(*a, **k):
            self.call = (name, a, k)
            return None
        return f


def _free_elems(ap):
    n = 1
    for d in list(ap.shape)[1:]:
        n *= int(d)
    return n


class Sched:
    LAT = 0.25

    def __init__(self, nc, es):
        self.nc = nc
        self.es = es
        self.eng = {'pe': nc.tensor, 'dve': nc.vector, 'act': nc.scalar, 'pool': nc.gpsimd, 'sp': nc.sync}
        self.sem = {}
        self.cnt = {}
        for k in self.eng:
            self._newsem(k)
        self.seen = {e: {} for e in self.eng}
        self.buf = {}
        self.recs = []
        self.labels = []
        self.ticket = {}
        self.fin = {}
        self.free = {e: 0.0 for e in self.eng}
        self.n_emitted = 0
        self.dummy_w = None
        self.n_dummy = 0
        self.emit_order = []
        self.crit = {}
        self.start = {}
        self.last_on = {}
        self.nwaits = 0
        self.nops = 0

    def _newsem(self, k):
        if k not in self.sem:
            self.sem[k] = self.es.enter_context(self.nc.semaphore("s_" + k))
            self.cnt[k] = 0

    def _dur(self, e, call, dsem):
        name, a, k = call
        try:
            if dsem is not None:
                out = k.get('out')
                nbytes = _free_elems(out) * int(out.shape[0]) * (2 if out.dtype == BF else 4)
                return 2.0 + nbytes / 150e3
            if e == 'pe':
                if name == 'transpose':
                    return 0.11
                rhs = k.get('rhs')
                cols = max(64, _free_elems(rhs))
                f = 4.0 if rhs.dtype == F32 else 1.0
                return max(0.1, cols * f / 1920.0)
            out = k.get('out', None)
            if out is None:
                out = k.get('ap', a[0] if a else None)
            n = _free_elems(out)
            if e == 'pool':
                return 0.2 + n / 450.0
            return 0.07 + n / 960.0
        except Exception:
            return 0.3

    REN = {}
    REN2 = {}
    ALIAS = {}

    def op(self, e, fn, reads=(), writes=(), dsem=None):
        if self.REN2:
            reads = [self.REN2.get(b, b) for b in reads]
            writes = [self.REN2.get(b, b) for b in writes]
        if self.REN:
            reads = [self.REN.get(b, b) for b in reads]
            writes = [self.REN.get(b, b) for b in writes]
        if self.ALIAS:
            reads = list(reads) + [p_ for b in reads for p_ in self.ALIAS.get(b, ())]
            writes = list(writes) + [p_ for b in writes for p_ in self.ALIAS.get(b, ())]
        cap = _Cap()
        fn(cap)
        call = cap.call
        assert call is not None
        rid = len(self.recs)
        deps = set()
        recs = self.recs
        for b in reads:
            st = self.buf.get(b)
            if st:
                deps.update(st[0])
                if b.startswith('ps'):
                    for r in st[1]:
                        if recs[r][0] != e:
                            deps.add(r)
        for b in writes:
            st = self.buf.get(b)
            if st:
                deps.update(st[0])
                deps.update(st[1])
        ws = set(writes)
        for b in writes:
            st = self.buf.setdefault(b, [[], []])
            if st[1]:
                st[0] = [rid]
                st[1] = []
            else:
                if len(st[0]) > 64:
                    st[0] = st[0][-64:]
                st[0].append(rid)
        for b in reads:
            if b in ws:
                continue
            st = self.buf.setdefault(b, [[], []])
            if len(st[1]) > 256:
                keep = {}
                for r in st[1]:
                    keep[recs[r][0]] = r
                st[1] = sorted(keep.values())
            st[1].append(rid)
        if dsem is not None:
            self._newsem(dsem)
        self.recs.append((e, call, deps, dsem, self._dur(e, call, dsem)))
        self.labels.append((list(writes) + ['-'])[0])
        return None

    def flush(self):
        recs = self.recs
        i0 = self.n_emitted
        n = len(recs)
        if i0 >= n:
            return
        LAT = self.LAT
        pend = {}
        users = {}
        for rid in range(i0, n):
            c = 0
            for d in recs[rid][2]:
                if d >= i0:
                    c += 1
                    users.setdefault(d, []).append(rid)
            pend[rid] = c
        bl = {}
        for rid in range(n - 1, i0 - 1, -1):
            m = 0.0
            e = recs[rid][0]
            for u_ in users.get(rid, ()):
                v = bl[u_] + (LAT if recs[u_][0] != e else 0.03)
                if v > m:
                    m = v
            bl[rid] = recs[rid][4] + m
        fin = self.fin
        free = self.free

        def tdep_of(rid):
            e = recs[rid][0]
            t = 0.0
            cr = None
            for d in recs[rid][2]:
                fd = fin.get(d, 0.0) + (LAT if recs[d][0] != e else 0.03)
                if fd > t:
                    t = fd
                    cr = d
            return t, cr

        ready = {e: {} for e in self.eng}
        for rid in range(i0, n):
            if pend[rid] == 0:
                ready[recs[rid][0]][rid] = tdep_of(rid)
        left = n - i0
        while left:
            best = None
            for e, rd in ready.items():
                if not rd:
                    continue
                fe = free[e]
                cand = None
                for rid, (td, cr) in rd.items():
                    st = td if td > fe else fe
                    key = (PRIO(st, bl[rid]), -bl[rid], rid)
                    if cand is None or key < cand[0]:
                        cand = (key, rid, td, cr, st)
                if best is None or cand[0] < best[0]:
                    best = (cand[0], cand[1], e, cand[2], cand[3], cand[4])
            _, rid, e, td, cr, t = best
            del ready[e][rid]
            self.crit[rid] = ('dep', cr) if (cr is not None and td >= free[e]) else ('eng', self.last_on.get(e))
            self.start[rid] = t
            self.last_on[e] = rid
            if e == 'pe' and self.dummy_w is not None and t - free[e] > KEEPWARM_GAP:
                self._keepwarm(free[e], t)
            self._emit(rid)
            dur = recs[rid][4]
            if recs[rid][3] is not None:
                free[e] = t + 0.15
                fin[rid] = t + dur
            else:
                free[e] = t + dur
                fin[rid] = t + dur
            for u_ in users.get(rid, ()):
                pend[u_] -= 1
                if pend[u_] == 0:
                    ready[recs[u_][0]][u_] = tdep_of(u_)
            left -= 1
        self.n_emitted = n

    def _keepwarm(self, t0, t1):
        recent = [r for r in self.emit_order[-400:] if self.recs[r][0] in ('dve', 'act', 'pool') and self.recs[r][3] is None]
        tt = t0 + KEEPWARM_STEP
        pe = self.eng['pe']
        while tt < t1 - 0.3:
            best = None
            for r in recent:
                f = self.fin.get(r, 0.0)
                if f <= tt and (best is None or f > self.fin[best]):
                    best = r
            if best is not None:
                k, v = self.ticket[best]
                if self.seen['pe'].get(k, 0) < v:
                    pe.wait_ge(self.sem[k], v)
                    self.seen['pe'][k] = v
                    self.nwaits += 1
            pe.ldweights(self.dummy_w)
            self.n_dummy += 1
            tt += KEEPWARM_STEP

    def _emit(self, rid):
        e, call, deps, dsem, _ = self.recs[rid]
        eng = self.eng[e]
        need = {}
        for d in deps:
            k, v = self.ticket[d]
            if e == 'pe' and k == 'pe':
                continue
            if need.get(k, 0) < v:
                need[k] = v
        seen = self.seen[e]
        for k, v in need.items():
            if seen.get(k, 0) >= v:
                continue
            eng.wait_ge(self.sem[k], v)
            seen[k] = v
            self.nwaits += 1
        name, a, kw = call
        ins = getattr(eng, name)(*a, **kw)
        self.emit_order.append(rid)
        self.nops += 1
        if dsem is not None:
            self.cnt[dsem] += 16
            ins.then_inc(self.sem[dsem], 16)
            self.ticket[rid] = (dsem, self.cnt[dsem])
        else:
            self.cnt[e] += 1
            ins.then_inc(self.sem[e], 1)
            self.ticket[rid] = (e, self.cnt[e])

    def mark(self, label):
        self.flush()
        if not hasattr(self, 'marks'):
            self.marks = []
        self.marks.append((label, max(self.free.values()), dict(self.free)))

    def fence(self):
        self.flush()
        ce = ['pe', 'dve', 'act', 'pool']
        for e in ce:
            for k in ce:
                v = self.cnt[k]
                if v > 0 and self.seen[e].get(k, 0) < v:
                    self.eng[e].wait_ge(self.sem[k], v)
                    self.seen[e][k] = v
                    self.nwaits += 1
        t = max(self.free[e] for e in ce)
        t = max([t] + [self.fin.get(r, 0.0) for r in range(max(0, self.n_emitted - 400), self.n_emitted)
                       if self.recs[r][3] is None])
        for e in ce:
            self.free[e] = t

    def finish(self, e='sp'):
        self.flush()
        eng = self.eng[e]
        for k, v in self.cnt.items():
            if v > 0 and self.seen[e].get(k, 0) < v:
                eng.wait_ge(self.sem[k], v)
                self.seen[e][k] = v
        self.sim_time = max(self.free.values())
        self.busy = {}
        for (e, call, deps, dsem, dur) in self.recs:
            self.busy[e] = self.busy.get(e, 0.0) + (0.15 if dsem is not None else dur)


def bc(ap, n):
    return bass.AP(ap.tensor, ap.offset, [list(x) for x in ap.ap] + [[0, n]])


def build_nc(nblk_own=8, nblk_pre=8, dbg=(), stop=99):
    nc = bass.Bass("TRN2", target_bir_lowering=False)
    NTOK_OWN = nblk_own * TB
    NTOK_PRE = max(nblk_pre, 1) * TB

    def din(name, shape, dt=F32):
        return nc.dram_tensor(name, list(shape), dt, kind="ExternalInput").ap()

    x_own = din("x_own", [NTOK_OWN, D])
    x_pre = din("x_pre", [NTOK_PRE, D])
    pmask_d = din("pmask", [P, 1])
    cT_d = din("cT", [P, 8])
    w_ada_d = din("w_ada", [D, 6 * D])
    w_adaf_d = din("w_ada_final", [D, 2 * D])
    bpp_d = din("b_pp", [P, 4, 8])
    bbc_d = din("b_bc", [P, 4, D])
    gfin_d = din("gfin_bc", [P, D])
    gpp_d = din("g_pp", [P, 2, 8])
    w_in_d = din("w_in", [D, PIN])
    w_out_d = din("w_out", [D, D])
    w_gu_d = din("w_gu", [D, 2 * HID])
    w_dn_d = din("w_down", [HID, D])
    cwm_d = din("cw_mix", [P, 4, 3])
    cwq_d = din("cw_qkv", [P, 12, 4])
    gconv_d = din("g_conv", [P, 4])
    gdn_d = din("g_dn", [P, 1])
    alog_d = din("alog_bc", [P, 4])
    dtb_d = din("dtb_bc", [P, 4])
    cmask_d = din("cmask", [P, 4, P])
    cst_d = din("consts", [P, 5, P])
    out_d = nc.dram_tensor("out", [NTOK_OWN, D], F32, kind="ExternalOutput").ap()
    dbg_d = {}
    for name, shape in dbg:
        dbg_d[name] = nc.dram_tensor("dbg_" + name, list(shape), F32, kind="ExternalOutput").ap()

    win_bf = nc.dram_tensor("win_bf", [D, PIN], BF, kind="Internal").ap()
    wout_bf = nc.dram_tensor("wout_bf", [D, D], BF, kind="Internal").ap()
    wgu_bf = nc.dram_tensor("wgu_bf", [D, 2 * HID], BF, kind="Internal").ap()
    wdn_bf = nc.dram_tensor("wdn_bf", [HID, D], BF, kind="Internal").ap()

    with ExitStack() as es:
        S = Sched(nc, es)
        op = S.op

        def sb(name, shape, dt=F32):
            return es.enter_context(nc.sbuf_tensor(name, list(shape), dt))

        psb = [es.enter_context(nc.psum_tensor("ps%d" % i, [P, 512], F32)) for i in range(8)]
        psi = [0]

        class PS:
            def __init__(self, i):
                self.name = "ps%d" % i
                self.t = psb[i]
                self.b = psb[i].bitcast(BF)

            def v3(self, a=4):
                return self.t[:].rearrange("p (a b) -> p a b", a=a)

        def newps():
            i = psi[0]
            psi[0] = (i + 1) % 8
            return PS(i)

        cst = sb("cst", [P, 5, P])
        identf = cst[:, 0, :]
        onesf = cst[:, 1, :]
        Uf = cst[:, 2, :]
        identb_t = sb("identb", [P, P], BF)
        onesb_t = sb("onesb", [P, P], BF)
        blk1_t = sb("blk1", [P, P], BF)
        m_cs = sb("m_cs", [P, 4, P])
        m_sc = sb("m_sc", [P, 4, P])
        m_scn = sb("m_scn", [P, 4, P])
        diagq = sb("diagq", [P, 12, 4, P], BF)
        diagm = sb("diagm", [P, 4, 3, P], BF)
        prm = sb("prm", [P, 64])
        c_gsm, c_shm, c_gsf, c_shf = 0, 8, 16, 24
        c_gconv, c_gdn, c_negA, c_dtb, c_pm = 32, 36, 37, 41, 45
        modbc = sb("modbc", [P, 4, D])
        wab = sb("wab", [P, 8, 8], BF)
        cmb = sb("cmb", [P, 4, P], BF)

        X = sb("X", [P, NT, D])
        xsb = sb("xsb", [P, NT, D], BF)
        ssx = sb("ssx", [P, 8])
        hT = sb("hT", [P, 8, TB], BF)
        mixT = sb("mixT", [P, 8, TB], BF)
        ring = [sb("ring%d" % i, [P, 8, 512], BF) for i in range(NSLOT)]
        ucv = sb("ucv", [P, 4, TB + 8], BF)
        yaj = sb("yaj", [P, TB])
        yaj2 = sb("yaj2", [P, TB])
        sqb = sb("sqb", [P, TB], BF)
        rst = sb("rst", [P, TB])
        qkvp = sb("qkvp", [P, 12, TB + 8], BF)
        zs = sb("zs", [P, 4, TB], BF)
        S32 = sb("S32", [P, 4, P])
        Sbf = sb("Sbf", [P, 4, P], BF)
        absb = sb("absb", [P, NT, 8])
        sc = sb("sc", [P, 24, 16])
        ARENA_W = 15360
        arena = sb("arena", [P, ARENA_W])

        aoff = [0]

        def carve(nwords, dt=F32, shape3=None):
            a = arena[:, aoff[0]:aoff[0] + nwords]
            aoff[0] += nwords
            if dt == BF:
                a = a.bitcast(BF)
            if shape3 is not None:
                a = a.rearrange("p (a b) -> p a b", a=shape3)
            return a

        cs = carve(1536, F32, 3)
        e1t = carve(1024, F32, 2)
        sqj = e1t[:, 0, :].rearrange("p (a b) -> p a b", a=4)
        Erow = carve(512, F32, 4)
        dm0 = carve(512, F32, 4)
        dm1 = carve(512, F32, 4)
        dm2 = carve(512, F32, 4)
        qh = carve(512, F32, 4)
        kh = carve(512, F32, 4)
        u_sb = carve(512, F32, 4)
        oT = carve(2048, F32, 4)
        hcb = oT
        dm1b = carve(256, BF, 4)
        khT = carve(256, BF, 4)
        qhT = carve(256, BF, 4)
        Lt = [carve(256, BF, 4), carve(256, BF, 4)]
        Mt = [carve(256, BF, 4), carve(256, BF, 4)]
        Qb = carve(256, BF, 4)
        wT = carve(256, BF, 4)
        vnew = carve(256, BF, 4)
        Nf = [carve(256, BF, 4), carve(256, BF, 4)]
        Lbd = [carve(256, BF, 4), carve(256, BF, 4)]
        Mbd = [carve(256, BF, 4), carve(256, BF, 4)]
        kbg = [carve(256, BF, 4), carve(256, BF, 4)]
        kdec = [carve(256, BF, 4), carve(256, BF, 4)]
        vb = [carve(256, BF, 4), carve(256, BF, 4)]
        qdT = [carve(256, BF, 4), carve(256, BF, 4)]
        attnT = [carve(256, BF, 4), carve(256, BF, 4)]
        assert aoff[0] <= ARENA_W, aoff[0]
        actT = arena[:, 0:5632].bitcast(BF).rearrange("p (a b) -> p a b", a=NHC)
        _pg = {'cs0': (0, 1), 'cs1': (2, 3), 'cs2': (4, 5), 'e1t0': (6, 7), 'e1t1': (8, 9), 'Erow': (10, 11),
               'dm0': (12, 13), 'dm1': (14, 15), 'dm2': (16, 17), 'qh': (18, 19), 'kh': (20, 21)}
        S.ALIAS = {k_: ['aw%d' % p_ for p_ in v_] for k_, v_ in _pg.items()}
        for J_ in range(NHC):
            S.ALIAS['actT%d' % J_] = ['aw%d' % J_]

        def scv(i, w=4):
            return sc[:, i, 0:w]

        def dma(e, out, in_, reads, writes, dsem, **kw):
            return op(e, lambda en: en.dma_start(out=out, in_=in_, **kw), reads=reads, writes=writes, dsem=dsem)

        dma('sp', cst[:], cst_d, [], ['cst'], 'd_c0')
        dma('sp', prm[:, c_gconv:c_gconv + 4], gconv_d, [], ['prm'], 'd_c1')
        dma('sp', prm[:, c_gdn:c_gdn + 1], gdn_d, [], ['prm'], 'd_c1')
        dma('sp', prm[:, c_dtb:c_dtb + 4], dtb_d, [], ['prm'], 'd_c1')
        dma('sp', prm[:, c_pm:c_pm + 1], pmask_d, [], ['prm'], 'd_c1')
        dma('sp', prm[:, c_negA:c_negA + 4], alog_d, [], ['prm'], 'd_c1')
        cT = sb("cTs", [P, 8])
        bpp = sb("bpp", [P, 4, 8])
        gpp = sb("gpp", [P, 2, 8])
        cwm = sb("cwm", [P, 4, 3])
        cwq = sb("cwq", [P, 12, 4])
        dma('sp', cT[:], cT_d, [], ['cT'], 'd_cT')
        dma('sp', bpp[:], bpp_d, [], ['bpp'], 'd_bpp')
        dma('sp', gpp[:], gpp_d, [], ['gpp'], 'd_gpp')
        dma('sp', cwm[:], cwm_d, [], ['cwm'], 'd_cwm')
        dma('sp', cwq[:], cwq_d, [], ['cwq'], 'd_cwq')
        dma('sp', modbc[:], bbc_d, [], ['modbc'], 'd_c3')
        dma('pool', cmb[:], cmask_d, [], ['cmb'], 'd_cmb')

        def cast_rows(dst, src, nrows, c0, c1, name, sem):
            for r0 in range(0, nrows, 128):
                r1 = min(nrows, r0 + 128)
                for a in range(c0, c1, 2048):
                    b_ = min(c1, a + 2048)
                    dma('pool', dst[r0:r1, a:b_], src[r0:r1, a:b_], [], [name], sem)

        cast_rows(win_bf, w_in_d, D, 1536, PIN, 'win_bf_b', 'd_w0')
        cast_rows(win_bf, w_in_d, D, 0, 1536, 'win_bf_a', 'd_w1')

        op('dve', lambda e: e.tensor_copy(out=identb_t[:], in_=identf), ['cst'], ['identb'])
        S.keepwarm_ap = identb_t[:]
        op('dve', lambda e: e.tensor_copy(out=onesb_t[:], in_=onesf), ['cst'], ['onesb'])
        op('dve', lambda e: e.tensor_copy(out=blk1_t[:], in_=cst[:, 3, :]), ['cst'], ['blk1'])
        for h in range(4):
            op('dve', lambda e: e.tensor_copy(out=m_cs[:, h, :], in_=cst[:, 4, :]), ['cst'], ['m_cs'])
            op('dve', lambda e: e.tensor_copy(out=m_sc[:, h, :], in_=Uf), ['cst'], ['m_sc'])
            op('dve', lambda e: e.tensor_tensor(out=m_scn[:, h, :], in0=identf, in1=Uf, op=ALU.subtract),
               ['cst'], ['m_scn'])
            op('dve', lambda e: e.tensor_tensor(out=m_scn[:, h, :], in0=m_scn[:, h, :], in1=cmb[:, 0, :], op=ALU.mult),
               ['m_scn', 'cmb'], ['m_scn'])
        op('act', lambda e: e.activation(out=prm[:, c_negA:c_negA + 4], in_=prm[:, c_negA:c_negA + 4], func=AF.Exp),
           ['prm'], ['prm'])
        op('dve', lambda e: e.tensor_scalar(out=prm[:, c_negA:c_negA + 4], in0=prm[:, c_negA:c_negA + 4],
                                            scalar1=-1.0, scalar2=None, op0=ALU.mult), ['prm'], ['prm'])
        for cc in range(12):
            for j in range(4):
                eng = 'dve' if (cc + j) % 2 == 0 else 'pool'
                op(eng, lambda e: e.tensor_scalar(out=diagq[:, cc, j, :], in0=identf, scalar1=cwq[:, cc, j:j + 1],
                                                  scalar2=0.0, op0=ALU.mult, op1=ALU.add), ['cst', 'cwq'], ['diagq'])
        for jj in range(4):
            for j in range(3):
                eng = 'dve' if (jj + j) % 2 == 0 else 'pool'
                op(eng, lambda e: e.tensor_scalar(out=diagm[:, jj, j, :], in0=identf, scalar1=cwm[:, jj, j:j + 1],
                                                  scalar2=0.0, op0=ALU.mult, op1=ALU.add), ['cst', 'cwm'], ['diagm'])
        op('pool', lambda e: e.memset(S32[:], 0.0), [], ['S32'])
        op('pool', lambda e: e.memset(Sbf[:], 0.0), [], ['Sbf'])
        op('pool', lambda e: e.memset(qkvp[:], 0.0), [], ['qkvp%d' % i for i in range(12)])
        op('pool', lambda e: e.memset(ucv[:], 0.0), [], ['ucv'])

        cact = sb("cact", [P, 8])
        ctmp = sb("ctmp", [P, 8])
        op('act', lambda e: e.activation(out=ctmp[:], in_=cT[:], func=AF.Exp, scale=-1.0), ['cT'], ['ctmp'])
        op('act', lambda e: e.activation(out=ctmp[:], in_=ctmp[:], func=AF.Ln, bias=1.0), ['ctmp'], ['ctmp'])
        op('act', lambda e: e.activation(out=ctmp[:], in_=ctmp[:], func=AF.Exp, scale=-1.0), ['ctmp'], ['ctmp'])
        op('dve', lambda e: e.tensor_tensor(out=cact[:], in0=cT[:], in1=ctmp[:], op=ALU.mult), ['cT', 'ctmp'], ['cact'])
        MIXN = ['mixT%d' % i_ for i_ in range(8)]
        mixf = mixT[:].rearrange("p a b -> p (a b)").bitcast(F32)
        crep = mixf[:, 0:1024].rearrange("p (a b) -> p a b", a=8)
        gfin = mixf[:, 1024:2048]
        wst = arena[:, 1024:1024 + 8192].rearrange("p (a b) -> p a b", a=8)
        for kc in range(8):
            op('dve', lambda e: e.tensor_scalar(out=crep[:, kc, :], in0=onesf, scalar1=cact[:, kc:kc + 1],
                                                scalar2=None, op0=ALU.mult), ['cst', 'cact'], MIXN)
        dma('sp', gfin, gfin_d, [], MIXN, 'd_gfin')
        pp_cols = [0, 1, 3, 4]
        ppres = sb("ppres", [P, 4, 8])
        for vi, vcol in enumerate(pp_cols):
            dma('sp', wst, w_ada_d[:, vcol * D:(vcol + 1) * D].rearrange("(k p) c -> p k c", p=P),
                [], ['wst'], 'd_wst')
            ps = newps()
            for j in range(8):
                for kc in range(8):
                    op('pe', lambda e: e.matmul(ps.t[:, j:j + 1], lhsT=wst[:, kc, j * P:(j + 1) * P],
                                                rhs=cact[:, kc:kc + 1], start=(kc == 0), stop=(kc == 7)),
                       ['wst', 'cact'], [ps.name])
            op('dve', lambda e: e.tensor_tensor(out=ppres[:, vi, :], in0=ps.t[:, 0:8], in1=bpp[:, vi, :], op=ALU.add),
               [ps.name, 'bpp'], ['ppres'])
        for (vs, vh, gi, cg, csf) in ((1, 0, 0, c_gsm, c_shm), (3, 2, 1, c_gsf, c_shf)):
            op('dve', lambda e: e.scalar_tensor_tensor(out=prm[:, cg:cg + 8], in0=ppres[:, vs, :], scalar=1.0,
                                                       in1=gpp[:, gi, :], op0=ALU.add, op1=ALU.mult),
               ['ppres', 'gpp'], ['prm'])
            op('dve', lambda e: e.tensor_copy(out=prm[:, csf:csf + 8], in_=ppres[:, vh, :]), ['ppres'], ['prm'])
        bsrc = [(w_ada_d, 2), (w_ada_d, 5), (w_adaf_d, 0), (w_adaf_d, 1)]
        XN = ['X%d' % t_ for t_ in range(NT)]
        xst = X[:].rearrange("p a b -> p (a b)").rearrange("p (a b) -> p a b", a=8)

        def bc_job(jid):
            vi, half = jid // 2, jid % 2
            wd, vcol = bsrc[vi]
            dma('sp', xst, wd[:, vcol * D + half * 512:vcol * D + (half + 1) * 512].rearrange("(k p) c -> p k c", p=P),
                [], XN, 'd_x0')
            ps = newps()
            for kc in range(8):
                op('pe', lambda e: e.matmul(ps.t[:], lhsT=crep[:, kc, :], rhs=xst[:, kc, :],
                                            start=(kc == 0), stop=(kc == 7)), XN + MIXN, [ps.name])
            sl = modbc[:, vi, half * 512:(half + 1) * 512]
            op('dve', lambda e: e.tensor_tensor(out=sl, in0=ps.t[:], in1=sl, op=ALU.add), [ps.name, 'modbc'], ['modbc'])
            if vi == 3:
                op('dve', lambda e: e.scalar_tensor_tensor(out=sl, in0=sl, scalar=1.0,
                                                           in1=gfin[:, half * 512:(half + 1) * 512],
                                                           op0=ALU.add, op1=ALU.mult), ['modbc'] + MIXN, ['modbc'])
        bc_jobs = list(range(8))
        gate_m_bc = modbc[:, 0, :]
        gate_f_bc = modbc[:, 1, :]
        sh_o_bc = modbc[:, 2, :]
        gs_o_bc = modbc[:, 3, :]

        cast_rows(wout_bf, w_out_d, D, 0, D, 'wout_bf', 'd_w2')
        cast_rows(wgu_bf, w_gu_d, D, 0, 2 * HID, 'wgu_bf', 'd_w3')
        cast_rows(wdn_bf, w_dn_d, HID, 0, D, 'wdn_bf', 'd_w4')
        dma('sp', wab[:], win_bf[:, 3584:3592].rearrange("(k p) c -> p k c", p=P), ['win_bf_b'], ['wab'], 'd_c4')
        S.fence()
        S.dummy_w = S.keepwarm_ap
        if stop <= 1:
            nblk_pre = 0
            nblk_own = 0
        if 'modbc' in dbg_d:
            dma('pool', dbg_d['modbc'], modbc[:], ['modbc'], [], 'd_dbg')
            dma('pool', dbg_d['prm'], prm[:, 0:46], ['prm'], [], 'd_dbg')

        def win_src(g):
            return ('win_bf_a' if g < 3 else 'win_bf_b',
                    win_bf[:, g * 512:(g + 1) * 512].rearrange("(k p) c -> p k c", p=P), 8, 512)

        def wout_src(hf):
            return ('wout_bf', wout_bf[:, hf * 512:(hf + 1) * 512].rearrange("(k p) c -> p k c", p=P), 8, 512)

        def wgu_src(g, up):
            c0 = (HID if up else 0) + g * 512
            w = min(512, HID - g * 512)
            return ('wgu_bf', wgu_bf[:, c0:c0 + w].rearrange("(k p) c -> p k c", p=P), 8, w)

        def wdn_src(hf, G):
            r0 = G * 8 * P
            n = min(8, NHC - G * 8)
            return ('wdn_bf', wdn_bf[r0:r0 + n * P, hf * 512:(hf + 1) * 512].rearrange("(k p) c -> p k c", p=P), n, 512)

        plan = []
        for blk in range(nblk_pre):
            if blk == nblk_pre - 1:
                plan += [win_src(2), win_src(1), win_src(3)]
            plan += [win_src(4), win_src(5)]
        for blk in range(nblk_own):
            plan += [win_src(2), win_src(1), win_src(0), win_src(6), win_src(3), win_src(4), win_src(5)]
            plan += [wout_src(0), wout_src(1)]
            for g in range(6):
                plan += [wgu_src(g, False), wgu_src(g, True)]
            for hf in range(2):
                for G in range(3):
                    plan += [wdn_src(hf, G)]
        rstate = {'issued': 0, 'used': 0}

        def ring_issue(upto):
            while rstate['issued'] < min(upto, len(plan)):
                i = rstate['issued']
                name, src, nk, w = plan[i]
                slot = i % NSLOT
                dma('sp', ring[slot][:, 0:nk, 0:w], src, [name], ['ring%d' % slot], 'd_ring%d' % slot)
                rstate['issued'] += 1

        def ring_get_n(k):
            i = rstate['used']
            ring_issue(i + NSLOT)
            rstate['used'] += k
            res = []
            for q_ in range(k):
                slot = (i + q_) % NSLOT
                res += [ring[slot], 'ring%d' % slot]
            return res

        def ring_get():
            return ring_get_n(1)

        def load_x(src_blk_ap):
            for tt in range(NT):
                dma('sp', X[:, tt, :], src_blk_ap[:, tt, :], [], ['X%d' % tt], 'd_x%d' % tt)

        def norm_T(c_gs, c_sh):
            for tt in range(NT):
                op('act', lambda e: e.activation(out=xsb[:, tt, :], in_=X[:, tt, :], func=AF.Square,
                                                 accum_out=ssx[:, tt:tt + 1]), ['X%d' % tt], ['xsb%d' % tt, 'ssx'])
            op('act', lambda e: e.activation(out=ssx[:, 0:4], in_=ssx[:, 0:4], func=AF.Ln, scale=1.0 / D, bias=EPS),
               ['ssx'], ['ssx'])
            op('act', lambda e: e.activation(out=ssx[:, 0:4], in_=ssx[:, 0:4], func=AF.Exp, scale=-0.5),
               ['ssx'], ['ssx'])
            for tt in range(NT):
                if tt % 2 == 0:
                    op('dve', lambda e: e.tensor_scalar(out=xsb[:, tt, :], in0=X[:, tt, :], scalar1=ssx[:, tt:tt + 1],
                                                        scalar2=None, op0=ALU.mult), ['X%d' % tt, 'ssx'], ['xsb%d' % tt])
                else:
                    op('act', lambda e: e.activation(out=xsb[:, tt, :], in_=X[:, tt, :], func=AF.Copy,
                                                     scale=ssx[:, tt:tt + 1]), ['X%d' % tt, 'ssx'], ['xsb%d' % tt])
            for kp in range(4):
                ps = newps()
                for k2 in range(2):
                    kc = kp * 2 + k2
                    for tt in range(NT):
                        op('pe', lambda e: e.transpose(out=ps.b[:, k2 * 512 + tt * P:k2 * 512 + (tt + 1) * P],
                                                       in_=xsb[:, tt, kc * P:(kc + 1) * P], identity=identb_t[:]),
                           ['xsb%d' % tt, 'identb'], [ps.name])
                for k2 in range(2):
                    kc = kp * 2 + k2
                    src = ps.b[:, k2 * 512:(k2 + 1) * 512]
                    if kp % 2 == 0:
                        op('act', lambda e: e.activation(out=hT[:, kc, :], in_=src, func=AF.Identity,
                                                         scale=prm[:, c_gs + kc:c_gs + kc + 1],
                                                         bias=prm[:, c_sh + kc:c_sh + kc + 1]),
                           [ps.name, 'prm'], ['hT%d' % kc])
                    else:
                        op('dve', lambda e: e.tensor_scalar(out=hT[:, kc, :], in0=src,
                                                            scalar1=prm[:, c_gs + kc:c_gs + kc + 1],
                                                            scalar2=prm[:, c_sh + kc:c_sh + kc + 1],
                                                            op0=ALU.mult, op1=ALU.add), [ps.name, 'prm'], ['hT%d' % kc])

        def inproj(wt, wname, j):
            ps = newps()
            for kc in range(8):
                op('pe', lambda e: e.matmul(ps.t[:], lhsT=wt[:, kc, j * P:(j + 1) * P], rhs=hT[:, kc, :],
                                            start=(kc == 0), stop=(kc == 7)), [wname, 'hT%d' % kc], [ps.name])
            return ps

        def silu_from_ps(ps, out_ap, tmp_ap, rd, wr, wr_tmp, mul_eng='dve'):
            op('act', lambda e: e.activation(out=tmp_ap, in_=ps, func=AF.Exp, scale=-1.0), rd, wr_tmp)
            op('act', lambda e: e.activation(out=tmp_ap, in_=tmp_ap, func=AF.Ln, bias=1.0), wr_tmp, wr_tmp)
            op('act', lambda e: e.activation(out=tmp_ap, in_=tmp_ap, func=AF.Exp, scale=-1.0), wr_tmp, wr_tmp)
            op('dve', lambda e: e.tensor_tensor(out=out_ap, in0=ps, in1=tmp_ap, op=ALU.mult), rd + wr_tmp, wr)

        def conv_mixer(full, last_pre):
            wt, wn = ring_get()
            for j in range(4):
                ps = inproj(wt, wn, j)
                op('act', lambda e: e.activation(out=hcb[:, j, :], in_=ps.t[:], func=AF.Copy), [ps.name], ['oT%d' % j])
            wt, wn = ring_get()
            for j in range(4):
                ps = inproj(wt, wn, j)
                op('dve', lambda e: e.tensor_tensor(out=ucv[:, j, 8:8 + TB], in0=ps.t[:], in1=hcb[:, j, :], op=ALU.mult),
                   [ps.name, 'oT%d' % j], ['ucv'])
            if full:
                for j in range(4):
                    ps = newps()
                    for tap in range(3):
                        op('pe', lambda e: e.matmul(ps.t[:], lhsT=diagm[:, j, tap, :], rhs=ucv[:, j, 6 + tap:6 + tap + TB],
                                                    start=(tap == 0), stop=(tap == 2)), ['diagm', 'ucv'], [ps.name])
                    op('act', lambda e: e.activation(out=hcb[:, j, :], in_=ps.t[:], func=AF.Copy), [ps.name], ['oT%d' % j])
            if last_pre:
                op('pool', lambda e: e.tensor_scalar(out=ucv[:, :, 6:8], in0=ucv[:, :, TB + 6:TB + 8],
                                                     scalar1=prm[:, c_pm:c_pm + 1], scalar2=0.0, op0=ALU.mult, op1=ALU.add),
                   ['ucv', 'prm'], ['ucv'])
            else:
                op('pool', lambda e: e.tensor_copy(out=ucv[:, :, 6:8], in_=ucv[:, :, TB + 6:TB + 8]), ['ucv'], ['ucv'])
            if not full:
                return
            wt, wn = ring_get()
            for j in range(4):
                ps = inproj(wt, wn, j)
                op('dve', lambda e: e.tensor_tensor(out=yaj[:], in0=ps.t[:], in1=hcb[:, j, :], op=ALU.mult),
                   [ps.name, 'oT%d' % j], ['yaj'])
                op('act', lambda e: e.activation(out=sqb[:], in_=yaj[:], func=AF.Square), ['yaj'], ['sqb'])
                ps2 = newps()
                op('pe', lambda e: e.matmul(ps2.t[:], lhsT=blk1_t[:], rhs=sqb[:], start=True, stop=True),
                   ['blk1', 'sqb'], [ps2.name])
                op('act', lambda e: e.activation(out=rst[:], in_=ps2.t[:], func=AF.Ln, scale=1.0 / 64, bias=EPS),
                   [ps2.name], ['rst'])
                op('act', lambda e: e.activation(out=rst[:], in_=rst[:], func=AF.Exp, scale=-0.5), ['rst'], ['rst'])
                op('dve', lambda e: e.scalar_tensor_tensor(out=mixT[:, j, :], in0=yaj[:],
                                                           scalar=prm[:, c_gconv + j:c_gconv + j + 1], in1=rst[:],
                                                           op0=ALU.mult, op1=ALU.mult), ['yaj', 'rst', 'prm'], ['mixT%d' % j])

        def z_gate():
            wt, wn = ring_get()
            for h in range(4):
                ps = inproj(wt, wn, h)
                silu_from_ps(ps.t[:], zs[:, h, :], rst[:], [ps.name], ['zs'], ['rst'])

        def qkv_in(need_q):
            for t in range(3):
                if t == 0 and not need_q:
                    continue
                wt, wn = ring_get()
                for h in range(4):
                    cc = t * 4 + h
                    ps = inproj(wt, wn, h)
                    if h % 2 == 0:
                        op('act', lambda e: e.activation(out=qkvp[:, cc, 8:8 + TB], in_=ps.t[:], func=AF.Copy),
                           [ps.name], ['qkvp%d' % cc])
                    else:
                        op('dve', lambda e: e.tensor_copy(out=qkvp[:, cc, 8:8 + TB], in_=ps.t[:]),
                           [ps.name], ['qkvp%d' % cc])

        def ab_in():
            ps = newps()
            for tt in range(NT):
                for kc in range(8):
                    op('pe', lambda e: e.matmul(ps.t[:, tt * 8:(tt + 1) * 8], lhsT=hT[:, kc, tt * P:(tt + 1) * P],
                                                rhs=wab[:, kc, :], start=(kc == 0), stop=(kc == 7)),
                       ['hT%d' % kc, 'wab'], [ps.name])
            op('dve', lambda e: e.tensor_copy(out=absb[:].rearrange("p a b -> p (a b)"), in_=ps.t[:, 0:32]),
               [ps.name], ['absb'])

        (I_XA, I_ABS, I_E1, I_L1, I_G, I_E2, I_BETA, I_NBETA, I_GC, I_GL, I_EGL, I_ECOL, I_KDS) = range(13)
        J_SSQ, J_SSK, J_RQ, J_RK, J_SQ, J_SKBG, J_SKD = range(13, 20)

        def sl16(i):
            return sc[:, i, :]

        def sl4(i, c):
            return sc[:, i, c * 4:(c + 1) * 4]

        def bch(a2, n=4):
            return bass.AP(a2.tensor, a2.offset, [list(a2.ap[0]), [0, n], list(a2.ap[1])])

        PSA = [PS(i) for i in range(4)]
        psB_i = [0]

        def newpsB():
            i = psB_i[0]
            psB_i[0] = (i + 1) % 3
            return PS(4 + i)

        def dn_scalars():
            dv = lambda f, r, w: op('dve', f, r, w)
            ac = lambda f, r, w: op('act', f, r, w)
            v3 = lambda i: sc[:, i, :].rearrange("p (a b) -> p a b", a=4)
            dv(lambda e: e.tensor_tensor(out=v3(I_XA), in0=absb[:, :, 0:4], in1=bch(prm[:, c_dtb:c_dtb + 4]), op=ALU.add),
               ['absb', 'prm'], ['sc_xa'])
            ac(lambda e: e.activation(out=sl16(I_ABS), in_=sl16(I_XA), func=AF.Abs), ['sc_xa'], ['sc_abs'])
            ac(lambda e: e.activation(out=sl16(I_E1), in_=sl16(I_ABS), func=AF.Exp, scale=-1.0), ['sc_abs'], ['sc_e1'])
            ac(lambda e: e.activation(out=sl16(I_L1), in_=sl16(I_E1), func=AF.Ln, bias=1.0), ['sc_e1'], ['sc_l1'])
            dv(lambda e: e.scalar_tensor_tensor(out=sl16(I_G), in0=sl16(I_XA), scalar=0.0, in1=sl16(I_L1),
                                                op0=ALU.max, op1=ALU.add), ['sc_xa', 'sc_l1'], ['sc_g'])
            dv(lambda e: e.tensor_tensor(out=v3(I_G), in0=v3(I_G), in1=bch(prm[:, c_negA:c_negA + 4]), op=ALU.mult),
               ['sc_g', 'prm'], ['sc_g'])
            ac(lambda e: e.activation(out=v3(I_E2), in_=absb[:, :, 4:8], func=AF.Exp, scale=-1.0), ['absb'], ['sc_e2'])
            dv(lambda e: e.tensor_scalar(out=sl16(I_E2), in0=sl16(I_E2), scalar1=1.0, scalar2=None, op0=ALU.add),
               ['sc_e2'], ['sc_e2'])
            dv(lambda e: e.reciprocal(out=sl16(I_BETA), in_=sl16(I_E2)), ['sc_e2'], ['sc_beta'])
            dv(lambda e: e.tensor_scalar(out=sl16(I_NBETA), in0=sl16(I_BETA), scalar1=-1.0, scalar2=None, op0=ALU.mult),
               ['sc_beta'], ['sc_nbeta'])
            ps = newps()
            op('pe', lambda e: e.matmul(ps.t[:, 0:16], lhsT=Uf, rhs=sl16(I_G), start=True, stop=True),
               ['cst', 'sc_g'], [ps.name])
            op('pe', lambda e: e.matmul(ps.t[:, 16:32], lhsT=onesf, rhs=sl16(I_G), start=True, stop=True),
               ['cst', 'sc_g'], [ps.name])
            dv(lambda e: e.tensor_copy(out=sl16(I_GC), in_=ps.t[:, 0:16]), [ps.name], ['sc_gc'])
            dv(lambda e: e.tensor_copy(out=sl16(I_GL), in_=ps.t[:, 16:32]), [ps.name], ['sc_gl'])
            ac(lambda e: e.activation(out=sl16(I_EGL), in_=sl16(I_GL), func=AF.Exp), ['sc_gl'], ['sc_egl'])
            ac(lambda e: e.activation(out=sl16(I_ECOL), in_=sl16(I_GC), func=AF.Exp), ['sc_gc'], ['sc_ecol'])
            dv(lambda e: e.tensor_tensor(out=sl16(I_KDS), in0=sl16(I_GL), in1=sl16(I_GC), op=ALU.subtract),
               ['sc_gl', 'sc_gc'], ['sc_kds'])
            ac(lambda e: e.activation(out=sl16(I_KDS), in_=sl16(I_KDS), func=AF.Exp), ['sc_kds'], ['sc_kds'])

        def prep_gen(c, need_q):
            ob = c % 2
            sfx = '_%d' % ob
            dv = lambda f, r, w: op('dve', f, r, w)
            ac = lambda f, r, w: op('act', f, r, w)
            po = lambda f, r, w: op('pool', f, r, w)
            types = [0, 1, 2] if need_q else [1, 2]
            gcb = bc(sl4(I_GC, c), P)
            def colbc(slot, h):
                a1 = sc[:, slot, c * 4 + h:c * 4 + h + 1]
                return bass.AP(a1.tensor, a1.offset, [list(a1.ap[0]), [0, P]])
            ps_gr = PSA[0]
            for h in range(4):
                op('pe', lambda e: e.transpose(out=ps_gr.t[:, h * P:(h + 1) * P], in_=colbc(I_GC, h), identity=identf),
                   ['sc_gc', 'cst'], [ps_gr.name])
            gr3 = ps_gr.v3()
            yield
            if need_q:
                ac(lambda e: e.activation(out=Erow, in_=gr3, func=AF.Exp), [ps_gr.name], ['Erow'])
            dv(lambda e: e.tensor_tensor(out=dm0, in0=gr3, in1=gcb, op=ALU.subtract), [ps_gr.name, 'sc_gc'], ['dm0'])
            ps_br = PSA[1]
            for h in range(4):
                op('pe', lambda e: e.transpose(out=ps_br.t[:, h * P:(h + 1) * P], in_=colbc(I_BETA, h), identity=identf),
                   ['sc_beta', 'cst'], [ps_br.name])
            yield
            dv(lambda e: e.tensor_scalar(out=dm1, in0=dm0, scalar1=0.0, scalar2=None, op0=ALU.max), ['dm0'], ['dm1'])
            po(lambda e: e.tensor_scalar(out=dm2, in0=dm0, scalar1=0.0, scalar2=-3.0e38, op0=ALU.min, op1=ALU.max), ['dm0'], ['dm2'])
            ac(lambda e: e.activation(out=dm1, in_=dm1, func=AF.Exp, scale=-1.0), ['dm1'], ['dm1'])
            ac(lambda e: e.activation(out=dm2, in_=dm2, func=AF.Exp), ['dm2'], ['dm2'])
            yield
            f2 = lambda a: a.rearrange("p a b -> p (a b)")
            po(lambda e: e.tensor_tensor(out=f2(dm1), in0=f2(dm1), in1=f2(m_cs[:]), op=ALU.mult), ['dm1', 'm_cs'], ['dm1'])
            po(lambda e: e.tensor_tensor(out=dm1, in0=dm1, in1=bc(sl4(I_NBETA, c), P), op=ALU.mult), ['dm1', 'sc_nbeta'], ['dm1'])
            po(lambda e: e.tensor_tensor(out=dm1b, in0=dm1, in1=bch(cmb[:, 0, :]), op=ALU.mult), ['dm1', 'cmb'], ['dm1b'])
            po(lambda e: e.tensor_tensor(out=f2(dm2), in0=f2(dm2), in1=f2(m_sc[:]), op=ALU.mult), ['dm2', 'm_sc'], ['dm2'])
            dv(lambda e: e.tensor_tensor(out=f2(dm0), in0=ps_br.t[:], in1=f2(m_scn[:]), op=ALU.mult),
               [ps_br.name, 'm_scn'], ['dm0'])
            po(lambda e: e.tensor_tensor(out=f2(dm0), in0=f2(dm0), in1=f2(dm2), op=ALU.mult), ['dm0', 'dm2'], ['dm0'])
            yield
            Tps = {}
            for ti, t in enumerate(types):
                ps = PSA[t]
                for h in range(4):
                    cc = t * 4 + h
                    for tap in range(4):
                        op('pe', lambda e: e.matmul(ps.t[:, h * P:(h + 1) * P], lhsT=diagq[:, cc, tap, :],
                                                    rhs=qkvp[:, cc, 5 + tap + c * P:5 + tap + (c + 1) * P],
                                                    start=(tap == 0), stop=(tap == 3)),
                           ['diagq', 'qkvp%d' % cc], [ps.name])
                silu_from_ps(ps.t[:], cs[:, t, :], e1t[:, ti % 2, :], [ps.name], ['cs%d' % t], ['e1t%d' % (ti % 2)])
                yield
            for t in types:
                ps = PSA[t]
                Tps[t] = ps
                for h in range(4):
                    op('pe', lambda e: e.transpose(out=ps.t[:, h * P:(h + 1) * P], in_=cs[:, t, h * P:(h + 1) * P],
                                                   identity=identf), ['cs%d' % t, 'cst'], [ps.name])
            yield
            for t, islot, rslot in ((0, J_SSQ, J_RQ), (1, J_SSK, J_RK)):
                if t not in types:
                    continue
                ac(lambda e: e.activation(out=sqj, in_=Tps[t].v3(), func=AF.Square), [Tps[t].name], ['e1t0'])
                dv(lambda e: e.tensor_reduce(out=sl4(islot, ob), in_=sqj, axis=AX.X, op=ALU.add), ['e1t0'], ['sc_ss%d' % t + sfx])
                ac(lambda e: e.activation(out=sl4(rslot, ob), in_=sl4(islot, ob), func=AF.Ln, bias=EPS),
                   ['sc_ss%d' % t + sfx], ['sc_r%d' % t + sfx])
                ac(lambda e: e.activation(out=sl4(rslot, ob), in_=sl4(rslot, ob), func=AF.Exp, scale=-0.5),
                   ['sc_r%d' % t + sfx], ['sc_r%d' % t + sfx])
                yield
            dv(lambda e: e.tensor_tensor(out=sl4(J_SKBG, ob), in0=sl4(J_RK, ob), in1=sl4(I_BETA, c), op=ALU.mult),
               ['sc_r1' + sfx, 'sc_beta'], ['sc_skbg' + sfx])
            dv(lambda e: e.tensor_tensor(out=sl4(J_SKBG, ob), in0=sl4(J_SKBG, ob), in1=sl4(I_ECOL, c), op=ALU.mult),
               ['sc_skbg' + sfx, 'sc_ecol'], ['sc_skbg' + sfx])
            dv(lambda e: e.tensor_tensor(out=sl4(J_SKD, ob), in0=sl4(J_RK, ob), in1=sl4(I_KDS, c), op=ALU.mult),
               ['sc_r1' + sfx, 'sc_kds'], ['sc_skd' + sfx])
            Tk = Tps[1].v3()
            Tv = Tps[2].v3()
            dv(lambda e: e.tensor_tensor(out=kh, in0=Tk, in1=bc(sl4(J_RK, ob), P), op=ALU.mult),
               [Tps[1].name, 'sc_r1' + sfx], ['kh'])
            dv(lambda e: e.tensor_tensor(out=kbg[ob], in0=Tk, in1=bc(sl4(J_SKBG, ob), P), op=ALU.mult),
               [Tps[1].name, 'sc_skbg' + sfx], ['kbg' + sfx])
            yield
            dv(lambda e: e.tensor_tensor(out=kdec[ob], in0=Tk, in1=bc(sl4(J_SKD, ob), P), op=ALU.mult),
               [Tps[1].name, 'sc_skd' + sfx], ['kdec' + sfx])
            dv(lambda e: e.tensor_tensor(out=vb[ob], in0=Tv, in1=bc(sl4(I_BETA, c), P), op=ALU.mult),
               [Tps[2].name, 'sc_beta'], ['vb' + sfx])
            if need_q:
                dv(lambda e: e.tensor_scalar(out=sl4(J_SQ, ob), in0=sl4(J_RQ, ob), scalar1=DK_SCALE, scalar2=None, op0=ALU.mult),
                   ['sc_r0' + sfx], ['sc_sq' + sfx])
                dv(lambda e: e.tensor_tensor(out=qh, in0=Tps[0].v3(), in1=bc(sl4(J_SQ, ob), P), op=ALU.mult),
                   [Tps[0].name, 'sc_sq' + sfx], ['qh'])
            yield
            ps_k = PSA[3]
            for h in range(4):
                op('pe', lambda e: e.transpose(out=ps_k.t[:, h * P:(h + 1) * P], in_=kh[:, h, :], identity=identf),
                   ['kh', 'cst'], [ps_k.name])
            ac(lambda e: e.activation(out=khT, in_=ps_k.v3(), func=AF.Copy), [ps_k.name], ['khT'])
            if need_q:
                ps_q = PSA[0]
                for h in range(4):
                    op('pe', lambda e: e.transpose(out=ps_q.t[:, h * P:(h + 1) * P], in_=qh[:, h, :], identity=identf),
                       ['qh', 'cst'], [ps_q.name])
                ac(lambda e: e.activation(out=qhT, in_=ps_q.v3(), func=AF.Copy), [ps_q.name], ['qhT'])
                dv(lambda e: e.tensor_tensor(out=qdT[ob], in0=ps_q.v3(), in1=Erow, op=ALU.mult), [ps_q.name, 'Erow'], ['qdT' + sfx])
            yield
            ps_G = PSA[1]
            for h in range(4):
                op('pe', lambda e: e.matmul(ps_G.t[:, h * P:(h + 1) * P], lhsT=khT[:, h, :], rhs=khT[:, h, :],
                                            start=True, stop=True), ['khT'], [ps_G.name])
            if need_q:
                ps_A = PSA[2]
                for h in range(4):
                    op('pe', lambda e: e.matmul(ps_A.t[:, h * P:(h + 1) * P], lhsT=khT[:, h, :], rhs=qhT[:, h, :],
                                                start=True, stop=True), ['khT', 'qhT'], [ps_A.name])
            dv(lambda e: e.tensor_tensor(out=Lbd[ob], in0=ps_G.v3(), in1=dm1b, op=ALU.mult), [ps_G.name, 'dm1b'], ['Lbd' + sfx])
            dv(lambda e: e.tensor_tensor(out=Mbd[ob], in0=ps_G.v3(), in1=dm0, op=ALU.mult), [ps_G.name, 'dm0'], ['Mbd' + sfx])
            dv(lambda e: e.tensor_tensor(out=Nf[ob], in0=ps_G.v3(), in1=dm1, op=ALU.mult), [ps_G.name, 'dm1'], ['Nf' + sfx])
            if need_q:
                dv(lambda e: e.tensor_tensor(out=attnT[ob], in0=ps_A.v3(), in1=dm2, op=ALU.mult), [ps_A.name, 'dm2'], ['attnT' + sfx])
            yield

        def inv_gen(c, need_q):
            ob = c % 2
            sfx = '_%d' % ob
            dv = lambda f, r, w: op('dve', f, r, w)
            ac = lambda f, r, w: op('act', f, r, w)
            idb3 = bch(identb_t[:])
            dv(lambda e: e.tensor_tensor(out=Qb, in0=Mbd[ob], in1=idb3, op=ALU.add), ['Mbd' + sfx, 'identb'], ['Q'])
            Lc, Mc, Lcn, Mcn = Lbd[ob], Mbd[ob], 'Lbd' + sfx, 'Mbd' + sfx
            for ki, k in enumerate((1, 2, 4, 8)):
                nx = ki % 2
                if k > 1:
                    psQ = newpsB()
                    for h in range(4):
                        op('pe', lambda e: e.matmul(psQ.t[:, h * P:(h + 1) * P], lhsT=Lc[:, h, :], rhs=Qb[:, h, :],
                                                    start=True, stop=True), [Lcn, 'Q'], [psQ.name])
                if k < 8:
                    psA = newpsB()
                    psB = newpsB()
                    for h in range(4):
                        op('pe', lambda e: e.matmul(psA.t[:, h * P:(h + 1) * P], lhsT=Mc[:, h, :], rhs=Lc[:, h, :],
                                                    start=True, stop=True), [Lcn, Mcn], [psA.name])
                    for h in range(4):
                        op('pe', lambda e: e.matmul(psB.t[:, h * P:(h + 1) * P], lhsT=Lc[:, h, :], rhs=Mc[:, h, :],
                                                    start=True, stop=True), [Lcn, Mcn], [psB.name])
                yield
                if k > 1:
                    dv(lambda e: e.tensor_tensor(out=Qb, in0=psQ.v3(), in1=Qb, op=ALU.add), [psQ.name, 'Q'], ['Q'])
                if k < 8:
                    ac(lambda e: e.activation(out=Lt[nx], in_=psA.v3(), func=AF.Copy), [psA.name], ['Lt%d' % nx])
                    ac(lambda e: e.activation(out=Mt[nx], in_=psB.v3(), func=AF.Copy), [psB.name], ['Mt%d' % nx])
                    Lc, Mc, Lcn, Mcn = Lt[nx], Mt[nx], 'Lt%d' % nx, 'Mt%d' % nx
                yield
            Xn, Wp, Xnn, Wpn = Lt[1], Mt[1], 'Lt1', 'Mt1'
            for lvl in (1, 2, 3):
                psT = newpsB()
                for h in range(4):
                    op('pe', lambda e: e.transpose(out=psT.b[:, h * P:(h + 1) * P], in_=Qb[:, h, :], identity=identb_t[:]),
                       ['Q', 'identb'], [psT.name])
                psW = newpsB()
                for h in range(4):
                    op('pe', lambda e: e.matmul(psW.t[:, h * P:(h + 1) * P], lhsT=Nf[ob][:, h, :], rhs=Qb[:, h, :],
                                                start=True, stop=True), ['Nf' + sfx, 'Q'], [psW.name])
                yield
                ac(lambda e: e.activation(out=Xn, in_=psT.b[:, 0:512].rearrange("p (a b) -> p a b", a=4), func=AF.Copy),
                   [psT.name], [Xnn])
                dv(lambda e: e.tensor_tensor(out=Wp, in0=psW.v3(), in1=bch(cmb[:, lvl, :]), op=ALU.mult),
                   [psW.name, 'cmb'], [Wpn])
                psZ = newpsB()
                for h in range(4):
                    op('pe', lambda e: e.matmul(psZ.t[:, h * P:(h + 1) * P], lhsT=Xn[:, h, :], rhs=Wp[:, h, :],
                                                start=True, stop=True), [Xnn, Wpn], [psZ.name])
                yield
                dv(lambda e: e.tensor_tensor(out=Qb, in0=psZ.v3(), in1=Qb, op=ALU.add), [psZ.name, 'Q'], ['Q'])
            ps_w = newpsB()
            ps_u = newpsB()
            for h in range(4):
                op('pe', lambda e: e.matmul(ps_w.t[:, h * P:(h + 1) * P], lhsT=kbg[ob][:, h, :], rhs=Qb[:, h, :],
                                            start=True, stop=True), ['kbg' + sfx, 'Q'], [ps_w.name])
            for h in range(4):
                op('pe', lambda e: e.matmul(ps_u.t[:, h * P:(h + 1) * P], lhsT=Qb[:, h, :], rhs=vb[ob][:, h, :],
                                            start=True, stop=True), ['vb' + sfx, 'Q'], [ps_u.name])
            yield
            ac(lambda e: e.activation(out=wT, in_=ps_w.v3(), func=AF.Copy), [ps_w.name], ['wT'])
            ac(lambda e: e.activation(out=u_sb, in_=ps_u.v3(), func=AF.Copy), [ps_u.name], ['u'])
            ps_p = PS(7)
            for h in range(4):
                op('pe', lambda e: e.matmul(ps_p.t[:, h * P:(h + 1) * P], lhsT=wT[:, h, :], rhs=Sbf[:, h, :],
                                            start=True, stop=True), ['wT', 'Sbf'], [ps_p.name])
            yield
            dv(lambda e: e.tensor_tensor(out=vnew, in0=u_sb, in1=ps_p.v3(), op=ALU.subtract), ['u', ps_p.name], ['vnew'])
            if need_q:
                ps_o = PS(7)
                for h in range(4):
                    op('pe', lambda e: e.matmul(ps_o.t[:, h * P:(h + 1) * P], lhsT=Sbf[:, h, :], rhs=qdT[ob][:, h, :],
                                                start=True, stop=False), ['Sbf', 'qdT' + sfx], [ps_o.name])
                    op('pe', lambda e: e.matmul(ps_o.t[:, h * P:(h + 1) * P], lhsT=vnew[:, h, :], rhs=attnT[ob][:, h, :],
                                                start=False, stop=True), ['vnew', 'attnT' + sfx], [ps_o.name])
            ps_s = PS(3)
            for h in range(4):
                op('pe', lambda e: e.matmul(ps_s.t[:, h * P:(h + 1) * P], lhsT=kdec[ob][:, h, :], rhs=vnew[:, h, :],
                                            start=True, stop=True), ['kdec' + sfx, 'vnew'], [ps_s.name])
            yield
            if need_q:
                ac(lambda e: e.activation(out=oT[:, :, c * P:(c + 1) * P], in_=ps_o.v3(), func=AF.Copy), [ps_o.name], ['oT0', 'oT1', 'oT2', 'oT3'])
            for h in range(4):
                dv(lambda e: e.scalar_tensor_tensor(out=S32[:, h, :], in0=S32[:, h, :],
                                                    scalar=sc[:, I_EGL, c * 4 + h:c * 4 + h + 1],
                                                    in1=ps_s.t[:, h * P:(h + 1) * P], op0=ALU.mult, op1=ALU.add),
                   ['S32', 'sc_egl', ps_s.name], ['S32'])
            ac(lambda e: e.activation(out=Sbf[:], in_=S32[:], func=AF.Copy), ['S32'], ['Sbf'])
            yield

        def run_gens(*gens):
            gens = [g for g in gens if g is not None]
            while gens:
                for g in list(gens):
                    try:
                        next(g)
                    except StopIteration:
                        gens.remove(g)

        def dn_block(need_q):
            dn_scalars()
            run_gens(prep_gen(0, need_q))
            for c in range(4):
                run_gens(inv_gen(c, need_q), prep_gen(c + 1, need_q) if c < 3 else None)

        def halo_qkv(last_pre, need_q):
            for cc in range(12):
                if cc < 4 and not need_q and not last_pre:
                    continue
                nm = 'qkvp%d' % cc
                eng = 'pool' if cc % 2 == 0 else 'dve'
                if last_pre:
                    op(eng, lambda e: e.tensor_scalar(out=qkvp[:, cc, 5:8], in0=qkvp[:, cc, TB + 5:TB + 8],
                                                      scalar1=prm[:, c_pm:c_pm + 1], scalar2=0.0, op0=ALU.mult, op1=ALU.add),
                       [nm, 'prm'], [nm])
                else:
                    op(eng, lambda e: e.tensor_copy(out=qkvp[:, cc, 5:8], in_=qkvp[:, cc, TB + 5:TB + 8]), [nm], [nm])

        def dn_out():
            rtmp = [cs[:, 0, :], cs[:, 1, :], cs[:, 2, :], e1t[:, 0, :]]
            rnm = ['cs0', 'cs1', 'cs2', 'e1t0']
            stmp = [t_.rearrange("p a b -> p (a b)") for t_ in (Lt[0], Lt[1], Mt[0], Mt[1])]
            snm = ['Lt0', 'Lt1', 'Mt0', 'Mt1']
            for h in range(4):
                r_, rn, q_, qn = rtmp[h], rnm[h], stmp[h], snm[h]
                op('act', lambda e: e.activation(out=q_, in_=oT[:, h, :], func=AF.Square), ['oT%d' % h], [qn])
                ps = newps()
                op('pe', lambda e: e.matmul(ps.t[:], lhsT=onesb_t[:], rhs=q_, start=True, stop=True),
                   ['onesb', qn], [ps.name])
                op('act', lambda e: e.activation(out=r_, in_=ps.t[:], func=AF.Ln, scale=1.0 / P, bias=EPS),
                   [ps.name], [rn])
                op('act', lambda e: e.activation(out=r_, in_=r_, func=AF.Exp, scale=-0.5), [rn], [rn])
                op('dve', lambda e: e.scalar_tensor_tensor(out=r_, in0=oT[:, h, :],
                                                           scalar=prm[:, c_gdn:c_gdn + 1], in1=r_,
                                                           op0=ALU.mult, op1=ALU.mult), ['oT%d' % h, rn, 'prm'], [rn])
                op('pool', lambda e: e.tensor_tensor(out=mixT[:, 4 + h, :], in0=r_, in1=zs[:, h, :], op=ALU.mult),
                   [rn, 'zs'], ['mixT%d' % (4 + h)])

        def out_proj():
            w0, n0, w1, n1 = ring_get_n(2)
            for tt in range(NT):
                for hf, (wt, wn) in enumerate(((w0, n0), (w1, n1))):
                    ps = newps()
                    for kc in range(8):
                        op('pe', lambda e: e.matmul(ps.t[:], lhsT=mixT[:, kc, tt * P:(tt + 1) * P], rhs=wt[:, kc, :],
                                                    start=(kc == 0), stop=(kc == 7)), ['mixT%d' % kc, wn], [ps.name])
                    yt, yn = (yaj, 'yaj') if hf == 0 else (yaj2, 'yaj2')
                    op('dve', lambda e: e.tensor_tensor(out=yt[:], in0=ps.t[:], in1=gate_m_bc[:, hf * 512:(hf + 1) * 512],
                                                        op=ALU.mult), [ps.name, 'modbc'], [yn])
                    xs_ = X[:, tt, hf * 512:(hf + 1) * 512]
                    op('pool' if hf == 0 else 'dve', lambda e: e.tensor_tensor(out=xs_, in0=xs_, in1=yt[:], op=ALU.add),
                       ['X%d' % tt, yn], ['X%d' % tt])

        def ffn():
            for g in range(6):
                wg, ng, wu, nu = ring_get_n(2)
                nj = min(4, NHC - g * 4)
                for j in range(nj):
                    J = g * 4 + j
                    psg = newps()
                    psu = newps()
                    for kc in range(8):
                        op('pe', lambda e: e.matmul(psg.t[:], lhsT=wg[:, kc, j * P:(j + 1) * P], rhs=hT[:, kc, :],
                                                    start=(kc == 0), stop=(kc == 7)), [ng, 'hT%d' % kc], [psg.name])
                    for kc in range(8):
                        op('pe', lambda e: e.matmul(psu.t[:], lhsT=wu[:, kc, j * P:(j + 1) * P], rhs=hT[:, kc, :],
                                                    start=(kc == 0), stop=(kc == 7)), [nu, 'hT%d' % kc], [psu.name])
                    op('act', lambda e: e.activation(out=rst[:], in_=psg.t[:], func=AF.Silu), [psg.name], ['rst'])
                    op('dve', lambda e: e.tensor_tensor(out=actT[:, J, :], in0=psu.t[:], in1=rst[:], op=ALU.mult),
                       [psu.name, 'rst'], ['actT%d' % J])
            for hf in range(2):
                accs = [newps() for _ in range(NT)]
                for G in range(3):
                    wd, nd = ring_get()
                    n = min(8, NHC - G * 8)
                    for tt in range(NT):
                        for j in range(n):
                            J = G * 8 + j
                            op('pe', lambda e: e.matmul(accs[tt].t[:], lhsT=actT[:, J, tt * P:(tt + 1) * P], rhs=wd[:, j, :],
                                                        start=(J == 0), stop=(J == NHC - 1)), ['actT%d' % J, nd], [accs[tt].name])
                for tt in range(NT):
                    yt, yn = (yaj, 'yaj') if tt % 2 == 0 else (yaj2, 'yaj2')
                    op('dve', lambda e: e.tensor_tensor(out=yt[:], in0=accs[tt].t[:],
                                                        in1=gate_f_bc[:, hf * 512:(hf + 1) * 512], op=ALU.mult),
                       [accs[tt].name, 'modbc'], [yn])
                    xs_ = X[:, tt, hf * 512:(hf + 1) * 512]
                    op('pool' if tt % 2 == 0 else 'dve', lambda e: e.tensor_tensor(out=xs_, in0=xs_, in1=yt[:], op=ALU.add),
                       ['X%d' % tt, yn], ['X%d' % tt])

        def final_norm():
            for tt in range(NT):
                op('act', lambda e: e.activation(out=xsb[:, tt, :], in_=X[:, tt, :], func=AF.Square,
                                                 accum_out=ssx[:, 4 + tt:5 + tt]), ['X%d' % tt], ['xsb%d' % tt, 'ssxf'])
            op('act', lambda e: e.activation(out=ssx[:, 4:8], in_=ssx[:, 4:8], func=AF.Ln, scale=1.0 / D, bias=EPS),
               ['ssxf'], ['ssxf'])
            op('act', lambda e: e.activation(out=ssx[:, 4:8], in_=ssx[:, 4:8], func=AF.Exp, scale=-0.5), ['ssxf'], ['ssxf'])
            for tt in range(NT):
                op('dve', lambda e: e.scalar_tensor_tensor(out=X[:, tt, :], in0=X[:, tt, :], scalar=ssx[:, 4 + tt:5 + tt],
                                                           in1=gs_o_bc, op0=ALU.mult, op1=ALU.mult),
                   ['X%d' % tt, 'ssxf', 'modbc'], ['X%d' % tt])
                op('pool' if tt % 2 == 0 else 'dve', lambda e: e.tensor_tensor(out=X[:, tt, :], in0=X[:, tt, :], in1=sh_o_bc, op=ALU.add),
                   ['X%d' % tt, 'modbc'], ['X%d' % tt])

        def dump(name, ap_, rd):
            if name in dbg_d:
                dma('pool', dbg_d[name], ap_, rd, [], 'd_dbg')

        xpre_v = x_pre.rearrange("(n t p) d -> n p t d", t=NT, p=P)
        xown_v = x_own.rearrange("(n t p) d -> n p t d", t=NT, p=P)
        out_v = out_d.rearrange("(n t p) d -> n p t d", t=NT, p=P)

        S.mark('prologue')
        for blk in range(nblk_pre):
            last = (blk == nblk_pre - 1)
            if MODEL_MARKS:
                S.mark('pre%d' % blk)
            if EXP_PRE:
                S.REN = {n_: n_ + '_%d' % (blk % 2) for n_ in EXP_PRE}
            load_x(xpre_v[blk])
            norm_T(c_gsm, c_shm)
            njob = -(-len(bc_jobs) // (nblk_pre - blk))
            for _ in range(njob):
                bc_job(bc_jobs.pop(0))
            if last:
                conv_mixer(False, True)
            qkv_in(last)
            ab_in()
            dn_block(False)
            halo_qkv(last, False)
            if last:
                op('dve', lambda e: e.tensor_scalar(out=S32[:], in0=S32[:], scalar1=prm[:, c_pm:c_pm + 1], scalar2=None,
                                                    op0=ALU.mult), ['S32', 'prm'], ['S32'])
                op('act', lambda e: e.activation(out=Sbf[:], in_=S32[:], func=AF.Copy), ['S32'], ['Sbf'])

        while bc_jobs:
            bc_job(bc_jobs.pop(0))
        if stop <= 2:
            nblk_own = 0
        S.REN = {}
        for blk in range(nblk_own):
            if EXPERIMENT:
                S.REN = {n_: n_ + '_%d' % (blk % 2) for n_ in EXPERIMENT}
            if MODEL_MARKS:
                S.mark('own%d' % blk)
            load_x(xown_v[blk])
            norm_T(c_gsm, c_shm)
            conv_mixer(True, False)
            z_gate()
            qkv_in(True)
            ab_in()
            if MODEL_MARKS:
                S.mark('  inproj')
            if stop <= 3:
                break
            dn_block(True)
            halo_qkv(False, True)
            if MODEL_MARKS:
                S.mark('  dn')
            if stop <= 4:
                break
            dn_out()
            if blk == 0:
                dump('mixT', mixT[:], MIXN)
            out_proj()
            if blk == 0:
                dump('x1', X[:], XN)
            if MODEL_MARKS:
                S.mark('  outproj')
            if stop <= 5:
                break
            norm_T(c_gsf, c_shf)
            if stop <= 6:
                break
            ffn()
            if MODEL_MARKS:
                S.mark('  ffn')
            if stop <= 7:
                break
            final_norm()
            for tt in range(NT):
                dma('sp', out_v[blk][:, tt, :], X[:, tt, :], ['X%d' % tt], [], 'd_out%d' % tt)
        S.finish('sp')
        build_nc.stats = (S.nops, S.nwaits)
        build_nc.n_dummy = S.n_dummy
        build_nc.sim_time = S.sim_time
        build_nc.busy = S.busy
        build_nc.marks = getattr(S, 'marks', [])
        build_nc.S = S
    return nc


def _pp(v):
    return np.ascontiguousarray(np.asarray(v, np.float32).reshape(8, 128).T)


def _consts():
    p = np.arange(128)
    ident = np.eye(128, dtype=np.float32)
    ones = np.ones((128, 128), np.float32)
    U = (p[:, None] <= p[None, :]).astype(np.float32)
    blk = ((p[:, None] // 64) == (p[None, :] // 64)).astype(np.float32)
    strict = (p[:, None] > p[None, :]).astype(np.float32)
    return np.ascontiguousarray(np.stack([ident, ones, U, blk, strict], axis=1))


def _cmask():
    p = np.arange(128)
    r, c = p[:, None], p[None, :]
    ms = [((r // 16) == (c // 16)).astype(np.float32)]
    for b in (16, 32, 64):
        ms.append((((r // (2 * b)) == (c // (2 * b))) & ((r // b) % 2 == 0) & ((c // b) % 2 == 1)).astype(np.float32))
    return np.ascontiguousarray(np.stack(ms, axis=1))


def make_in_maps(inp, nblk_own=8, nblk_pre=8):
    f = lambda a: np.ascontiguousarray(np.asarray(a, np.float32))
    x = f(inp['x'])
    c = f(inp['c'])
    w_ada = f(inp['w_ada'][0])
    b_ada = f(inp['b_ada'][0])
    w_adaf = f(inp['w_ada_final'])
    b_adaf = f(inp['b_ada_final'])
    bsl = lambda i: b_ada[i * D:(i + 1) * D]
    b_pp = np.ascontiguousarray(np.stack([_pp(bsl(0)), _pp(bsl(1)), _pp(bsl(3)), _pp(bsl(4))], axis=1))
    bb = np.stack([bsl(2), bsl(5), b_adaf[0:D], b_adaf[D:2 * D]], axis=0)
    b_bc = np.ascontiguousarray(np.broadcast_to(bb[None], (P, 4, D)))
    gfin_bc = np.ascontiguousarray(np.broadcast_to(f(inp['g_norm_final'])[None], (P, D)))
    g_pp = np.ascontiguousarray(np.stack([_pp(inp['g_norm_mix'][0]), _pp(inp['g_norm_ffn'][0])], axis=1))
    cwm = f(inp['conv_w_mix'][0])
    cw_mix = np.ascontiguousarray(cwm.T.reshape(4, 128, 3).transpose(1, 0, 2))
    cwq = f(inp['conv_w_qkv'][0])
    cw_qkv = np.ascontiguousarray(cwq.T.reshape(12, 128, 4).transpose(1, 0, 2))
    g_conv = np.ascontiguousarray(f(inp['g_conv_out'][0]).reshape(4, 128).T)
    g_dn = np.ascontiguousarray(f(inp['g_dn_out'][0]).reshape(128, 1))
    alog_bc = np.ascontiguousarray(np.broadcast_to(f(inp['a_log'][0])[None], (P, 4)))
    dtb_bc = np.ascontiguousarray(np.broadcast_to(f(inp['dt_bias'][0])[None], (P, 4)))
    shared = {
        "w_ada": w_ada, "w_ada_final": w_adaf, "b_pp": b_pp, "b_bc": b_bc, "gfin_bc": gfin_bc, "g_pp": g_pp,
        "w_in": f(inp['w_in'][0]), "w_out": f(inp['w_out'][0]), "w_gu": f(inp['w_gate_up'][0]),
        "w_down": f(inp['w_down'][0]), "cw_mix": cw_mix, "cw_qkv": cw_qkv, "g_conv": g_conv, "g_dn": g_dn,
        "alog_bc": alog_bc, "dtb_bc": dtb_bc, "consts": _consts(), "cmask": _cmask(),
    }
    n_own = nblk_own * TB
    n_pre = max(nblk_pre, 1) * TB
    S_ = x.shape[1]
    halfS = S_ // 2
    maps = []
    for core in range(8):
        b, half = core // 2, core % 2
        m = dict(shared)
        s0 = half * halfS
        m["x_own"] = np.ascontiguousarray(x[b, s0:s0 + n_own])
        p0 = s0 - n_pre if half == 1 else 0
        m["x_pre"] = np.ascontiguousarray(x[b, p0:p0 + n_pre])
        m["pmask"] = np.full((P, 1), float(half), np.float32)
        m["cT"] = _pp(c[b])
        maps.append(m)
    return maps


_NC_CACHE = {}


def kernel(**inputs):
    if 'nc' not in _NC_CACHE:
        _NC_CACHE['nc'] = build_nc()
    nc = _NC_CACHE['nc']
    maps = make_in_maps(inputs)
    res = run_bass_kernel_spmd(nc, maps, core_ids=list(range(8)))
    B, S_, _ = inputs['x'].shape
    out = np.empty((B, S_, D), np.float32)
    halfS = S_ // 2
    for core in range(8):
        b, half = core // 2, core % 2
        out[b, half * halfS:(half + 1) * halfS] = res.results[core]["out"]
    return out
```

```python
import numpy as np
from contextlib import ExitStack
import concourse.bass as bass
import concourse.mybir as mybir
from concourse.bass_utils import run_bass_kernel_spmd

F32 = mybir.dt.float32
BF = mybir.dt.bfloat16
AF = mybir.ActivationFunctionType
ALU = mybir.AluOpType
AX = mybir.AxisListType

P = 128
D = 1024
TB = 512
NT = 4
PIN = 3592
HID = 2816
NHC = 22
EPS = 1e-6
NSLOT = 4
MODEL_MARKS = False
EXPERIMENT = None
EXP_PRE = None
DK_SCALE = 128 ** -0.5


def PRIO(st, bl):
    return st - 0.03 * bl


class _Cap:
    def __init__(self):
        self.call = None

    def __getattr__(self, name):
        def f(*a, **k):
            self.call = (name, a, k)
            return None
        return f


def _free_elems(ap):
    n = 1
    for d in list(ap.shape)[1:]:
        n *= int(d)
    return n


class Sched:
    LAT = 0.25

    def __init__(self, nc, es):
        self.nc = nc
        self.es = es
        self.eng = {'pe': nc.tensor, 'dve': nc.vector, 'act': nc.scalar, 'pool': nc.gpsimd, 'sp': nc.sync}
        self.sem = {}
        self.cnt = {}
        for k in self.eng:
            self._newsem(k)
        self.seen = {e: {} for e in self.eng}
        self.buf = {}
        self.recs = []
        self.labels = []
        self.ticket = {}
        self.fin = {}
        self.free = {e: 0.0 for e in self.eng}
        self.n_emitted = 0
        self.crit = {}
        self.start = {}
        self.last_on = {}
        self.nwaits = 0
        self.nops = 0

    def _newsem(self, k):
        if k not in self.sem:
            self.sem[k] = self.es.enter_context(self.nc.semaphore("s_" + k))
            self.cnt[k] = 0

    def _dur(self, e, call, dsem):
        name, a, k = call
        try:
            if dsem is not None:
                out = k.get('out')
                nbytes = _free_elems(out) * int(out.shape[0]) * (2 if out.dtype == BF else 4)
                return 2.0 + nbytes / 150e3
            if e == 'pe':
                if name == 'transpose':
                    return 0.11
                rhs = k.get('rhs')
                cols = max(64, _free_elems(rhs))
                f = 4.0 if rhs.dtype == F32 else 1.0
                return max(0.1, cols * f / 1920.0)
            out = k.get('out', None)
            if out is None:
                out = k.get('ap', a[0] if a else None)
            n = _free_elems(out)
            if e == 'pool':
                return 0.2 + n / 450.0
            if e == 'dve' and name in ('tensor_tensor', 'scalar_tensor_tensor'):
                i0, i1 = k.get('in0'), k.get('in1')
                if i0 is not None and i1 is not None and 'PSum' not in type(i0.tensor).__name__ \
                        and 'PSum' not in type(i1.tensor).__name__ and i0.dtype == F32:
                    return 0.07 + 2 * n / 960.0
            return 0.07 + n / 960.0
        except Exception:
            return 0.3

    REN = {}
    REN2 = {}
    ALIAS = {}

    def op(self, e, fn, reads=(), writes=(), dsem=None):
        if self.REN2:
            reads = [self.REN2.get(b, b) for b in reads]
            writes = [self.REN2.get(b, b) for b in writes]
        if self.REN:
            reads = [self.REN.get(b, b) for b in reads]
            writes = [self.REN.get(b, b) for b in writes]
        if self.ALIAS:
            reads = list(reads) + [p_ for b in reads for p_ in self.ALIAS.get(b, ())]
            writes = list(writes) + [p_ for b in writes for p_ in self.ALIAS.get(b, ())]
        cap = _Cap()
        fn(cap)
        call = cap.call
        assert call is not None
        rid = len(self.recs)
        deps = set()
        recs = self.recs
        for b in reads:
            st = self.buf.get(b)
            if st:
                deps.update(st[0])
                if b.startswith('ps'):
                    for r in st[1]:
                        if recs[r][0] != e:
                            deps.add(r)
        for b in writes:
            st = self.buf.get(b)
            if st:
                deps.update(st[0])
                deps.update(st[1])
        ws = set(writes)
        for b in writes:
            st = self.buf.setdefault(b, [[], []])
            if st[1]:
                st[0] = [rid]
                st[1] = []
            else:
                if len(st[0]) > 64:
                    st[0] = st[0][-64:]
                st[0].append(rid)
        for b in reads:
            if b in ws:
                continue
            st = self.buf.setdefault(b, [[], []])
            if len(st[1]) > 256:
                keep = {}
                for r in st[1]:
                    keep[recs[r][0]] = r
                st[1] = sorted(keep.values())
            st[1].append(rid)
        if dsem is not None:
            self._newsem(dsem)
        self.recs.append((e, call, deps, dsem, self._dur(e, call, dsem)))
        self.labels.append((list(writes) + ['-'])[0])
        return None

    def flush(self):
        recs = self.recs
        i0 = self.n_emitted
        n = len(recs)
        if i0 >= n:
            return
        LAT = self.LAT
        pend = {}
        users = {}
        for rid in range(i0, n):
            c = 0
            for d in recs[rid][2]:
                if d >= i0:
                    c += 1
                    users.setdefault(d, []).append(rid)
            pend[rid] = c
        bl = {}
        for rid in range(n - 1, i0 - 1, -1):
            m = 0.0
            e = recs[rid][0]
            for u_ in users.get(rid, ()):
                v = bl[u_] + (LAT if recs[u_][0] != e else 0.03)
                if v > m:
                    m = v
            bl[rid] = recs[rid][4] + m
        fin = self.fin
        free = self.free

        def tdep_of(rid):
            e = recs[rid][0]
            t = 0.0
            cr = None
            for d in recs[rid][2]:
                fd = fin.get(d, 0.0) + (LAT if recs[d][0] != e else 0.03)
                if fd > t:
                    t = fd
                    cr = d
            return t, cr

        ready = {e: {} for e in self.eng}
        for rid in range(i0, n):
            if pend[rid] == 0:
                ready[recs[rid][0]][rid] = tdep_of(rid)
        left = n - i0
        while left:
            best = None
            for e, rd in ready.items():
                if not rd:
                    continue
                fe = free[e]
                cand = None
                for rid, (td, cr) in rd.items():
                    st = td if td > fe else fe
                    key = (PRIO(st, bl[rid]), -bl[rid], rid)
                    if cand is None or key < cand[0]:
                        cand = (key, rid, td, cr, st)
                if best is None or cand[0] < best[0]:
                    best = (cand[0], cand[1], e, cand[2], cand[3], cand[4])
            _, rid, e, td, cr, t = best
            del ready[e][rid]
            self.crit[rid] = ('dep', cr) if (cr is not None and td >= free[e]) else ('eng', self.last_on.get(e))
            self.start[rid] = t
            self.last_on[e] = rid
            self._emit(rid)
            dur = recs[rid][4]
            if recs[rid][3] is not None:
                free[e] = t + 0.15
                fin[rid] = t + dur
            else:
                free[e] = t + dur
                fin[rid] = t + dur
            for u_ in users.get(rid, ()):
                pend[u_] -= 1
                if pend[u_] == 0:
                    ready[recs[u_][0]][u_] = tdep_of(u_)
            left -= 1
        self.n_emitted = n

    def _emit(self, rid):
        e, call, deps, dsem, _ = self.recs[rid]
        eng = self.eng[e]
        need = {}
        for d in deps:
            k, v = self.ticket[d]
            if e == 'pe' and k == 'pe':
                continue
            if need.get(k, 0) < v:
                need[k] = v
        seen = self.seen[e]
        for k, v in need.items():
            if seen.get(k, 0) >= v:
                continue
            eng.wait_ge(self.sem[k], v)
            seen[k] = v
            self.nwaits += 1
        name, a, kw = call
        ins = getattr(eng, name)(*a, **kw)
        self.nops += 1
        if dsem is not None:
            self.cnt[dsem] += 16
            ins.then_inc(self.sem[dsem], 16)
            self.ticket[rid] = (dsem, self.cnt[dsem])
        else:
            self.cnt[e] += 1
            ins.then_inc(self.sem[e], 1)
            self.ticket[rid] = (e, self.cnt[e])

    def mark(self, label):
        self.flush()
        if not hasattr(self, 'marks'):
            self.marks = []
        self.marks.append((label, max(self.free.values()), dict(self.free)))

    def fence(self):
        self.flush()
        ce = ['pe', 'dve', 'act', 'pool']
        for e in ce:
            for k in ce:
                v = self.cnt[k]
                if v > 0 and self.seen[e].get(k, 0) < v:
                    self.eng[e].wait_ge(self.sem[k], v)
                    self.seen[e][k] = v
                    self.nwaits += 1
        t = max(self.free[e] for e in ce)
        t = max([t] + [self.fin.get(r, 0.0) for r in range(max(0, self.n_emitted - 400), self.n_emitted)
                       if self.recs[r][3] is None])
        for e in ce:
            self.free[e] = t

    def finish(self, e='sp'):
        self.flush()
        eng = self.eng[e]
        for k, v in self.cnt.items():
            if v > 0 and self.seen[e].get(k, 0) < v:
                eng.wait_ge(self.sem[k], v)
                self.seen[e][k] = v
        self.sim_time = max(self.free.values())
        self.busy = {}
        for (e, call, deps, dsem, dur) in self.recs:
            self.busy[e] = self.busy.get(e, 0.0) + (0.15 if dsem is not None else dur)


def bc(ap, n):
    return bass.AP(ap.tensor, ap.offset, [list(x) for x in ap.ap] + [[0, n]])


def build_nc(nblk_own=8, nblk_pre=8, dbg=(), stop=99):
    nc = bass.Bass("TRN2", target_bir_lowering=False)
    NTOK_OWN = nblk_own * TB
    NTOK_PRE = max(nblk_pre, 1) * TB

    def din(name, shape, dt=F32):
        return nc.dram_tensor(name, list(shape), dt, kind="ExternalInput").ap()

    x_own = din("x_own", [NTOK_OWN, D])
    x_pre = din("x_pre", [NTOK_PRE, D])
    pmask_d = din("pmask", [P, 1])
    cT_d = din("cT", [P, 8])
    w_ada_d = din("w_ada", [D, 6 * D])
    w_adaf_d = din("w_ada_final", [D, 2 * D])
    bpp_d = din("b_pp", [P, 4, 8])
    bbc_d = din("b_bc", [P, 4, D])
    gfin_d = din("gfin_bc", [P, D])
    gpp_d = din("g_pp", [P, 2, 8])
    w_in_d = din("w_in", [D, PIN])
    w_out_d = din("w_out", [D, D])
    w_gu_d = din("w_gu", [D, 2 * HID])
    w_dn_d = din("w_down", [HID, D])
    cwm_d = din("cw_mix", [P, 4, 3])
    cwq_d = din("cw_qkv", [P, 12, 4])
    gconv_d = din("g_conv", [P, 4])
    gdn_d = din("g_dn", [P, 1])
    alog_d = din("alog_bc", [P, 4])
    dtb_d = din("dtb_bc", [P, 4])
    cmask_d = din("cmask", [P, 4, P])
    cst_d = din("consts", [P, 5, P])
    out_d = nc.dram_tensor("out", [NTOK_OWN, D], F32, kind="ExternalOutput").ap()
    dbg_d = {}
    for name, shape in dbg:
        dbg_d[name] = nc.dram_tensor("dbg_" + name, list(shape), F32, kind="ExternalOutput").ap()

    win_bf = nc.dram_tensor("win_bf", [D, PIN], BF, kind="Internal").ap()
    wout_bf = nc.dram_tensor("wout_bf", [D, D], BF, kind="Internal").ap()
    wgu_bf = nc.dram_tensor("wgu_bf", [D, 2 * HID], BF, kind="Internal").ap()
    wdn_bf = nc.dram_tensor("wdn_bf", [HID, D], BF, kind="Internal").ap()

    with ExitStack() as es:
        S = Sched(nc, es)
        op = S.op

        def sb(name, shape, dt=F32):
            return es.enter_context(nc.sbuf_tensor(name, list(shape), dt))

        psb = [es.enter_context(nc.psum_tensor("ps%d" % i, [P, 512], F32)) for i in range(8)]
        psi = [0]

        class PS:
            def __init__(self, i):
                self.name = "ps%d" % i
                self.t = psb[i]
                self.b = psb[i].bitcast(BF)

            def v3(self, a=4):
                return self.t[:].rearrange("p (a b) -> p a b", a=a)

        def newps():
            i = psi[0]
            psi[0] = (i + 1) % 8
            return PS(i)

        cst = sb("cst", [P, 5, P])
        identf = cst[:, 0, :]
        onesf = cst[:, 1, :]
        Uf = cst[:, 2, :]
        identb_t = sb("identb", [P, P], BF)
        onesb_t = sb("onesb", [P, P], BF)
        blk1_t = sb("blk1", [P, P], BF)
        m_cs = sb("m_cs", [P, 4, P])
        m_sc = sb("m_sc", [P, 4, P])
        m_scn = sb("m_scn", [P, 4, P])
        diagq = sb("diagq", [P, 12, 4, P], BF)
        diagm = sb("diagm", [P, 4, 3, P], BF)
        prm = sb("prm", [P, 64])
        c_gsm, c_shm, c_gsf, c_shf = 0, 8, 16, 24
        c_gconv, c_gdn, c_negA, c_dtb, c_pm = 32, 36, 37, 41, 45
        modbc = sb("modbc", [P, 4, D])
        wab = sb("wab", [P, 8, 8], BF)
        cmb = sb("cmb", [P, 4, P], BF)

        X = sb("X", [P, NT, D])
        xsb = sb("xsb", [P, NT, D], BF)
        ssx = sb("ssx", [P, 8])
        hT = sb("hT", [P, 8, TB], BF)
        mixT = sb("mixT", [P, 8, TB], BF)
        ring = [sb("ring%d" % i, [P, 8, 512], BF) for i in range(NSLOT)]
        ucv = sb("ucv", [P, 4, TB + 8], BF)
        yaj = sb("yaj", [P, TB])
        yaj2 = sb("yaj2", [P, TB])
        sqb = sb("sqb", [P, TB], BF)
        rst = sb("rst", [P, TB])
        qkvp = sb("qkvp", [P, 12, TB + 8], BF)
        zs = sb("zs", [P, 4, TB], BF)
        S32 = sb("S32", [P, 4, P])
        Sbf = sb("Sbf", [P, 4, P], BF)
        absb = sb("absb", [P, NT, 8])
        sc = sb("sc", [P, 24, 16])
        ARENA_W = 15360
        arena = sb("arena", [P, ARENA_W])

        aoff = [0]

        def carve(nwords, dt=F32, shape3=None):
            a = arena[:, aoff[0]:aoff[0] + nwords]
            aoff[0] += nwords
            if dt == BF:
                a = a.bitcast(BF)
            if shape3 is not None:
                a = a.rearrange("p (a b) -> p a b", a=shape3)
            return a

        cs = carve(1536, F32, 3)
        e1t = carve(1024, F32, 2)
        sqj = e1t[:, 0, :].rearrange("p (a b) -> p a b", a=4)
        Erow = carve(512, F32, 4)
        dm0 = carve(512, F32, 4)
        dm1 = carve(512, F32, 4)
        dm2 = carve(512, F32, 4)
        qh = carve(512, F32, 4)
        kh = carve(512, F32, 4)
        u_sb = carve(512, F32, 4)
        oT = carve(2048, F32, 4)
        hcb = oT
        dm1b = carve(256, BF, 4)
        khT = carve(256, BF, 4)
        qhT = carve(256, BF, 4)
        Lt = [carve(256, BF, 4), carve(256, BF, 4)]
        Mt = [carve(256, BF, 4), carve(256, BF, 4)]
        Qb = carve(256, BF, 4)
        wT = carve(256, BF, 4)
        vnew = carve(256, BF, 4)
        Nf = [carve(256, BF, 4), carve(256, BF, 4)]
        Lbd = [carve(256, BF, 4), carve(256, BF, 4)]
        Mbd = [carve(256, BF, 4), carve(256, BF, 4)]
        kbg = [carve(256, BF, 4), carve(256, BF, 4)]
        kdec = [carve(256, BF, 4), carve(256, BF, 4)]
        vb = [carve(256, BF, 4), carve(256, BF, 4)]
        qdT = [carve(256, BF, 4), carve(256, BF, 4)]
        attnT = [carve(256, BF, 4), carve(256, BF, 4)]
        assert aoff[0] <= ARENA_W, aoff[0]
        actT = arena[:, 0:5632].bitcast(BF).rearrange("p (a b) -> p a b", a=NHC)
        _pg = {'cs0': (0, 1), 'cs1': (2, 3), 'cs2': (4, 5), 'e1t0': (6, 7), 'e1t1': (8, 9), 'Erow': (10, 11),
               'dm0': (12, 13), 'dm1': (14, 15), 'dm2': (16, 17), 'qh': (18, 19), 'kh': (20, 21)}
        S.ALIAS = {k_: ['aw%d' % p_ for p_ in v_] for k_, v_ in _pg.items()}
        for J_ in range(NHC):
            S.ALIAS['actT%d' % J_] = ['aw%d' % J_]

        def scv(i, w=4):
            return sc[:, i, 0:w]

        def dma(e, out, in_, reads, writes, dsem, **kw):
            return op(e, lambda en: en.dma_start(out=out, in_=in_, **kw), reads=reads, writes=writes, dsem=dsem)

        dma('sp', cst[:], cst_d, [], ['cst'], 'd_c0')
        dma('sp', prm[:, c_gconv:c_gconv + 4], gconv_d, [], ['prm'], 'd_c1')
        dma('sp', prm[:, c_gdn:c_gdn + 1], gdn_d, [], ['prm'], 'd_c1')
        dma('sp', prm[:, c_dtb:c_dtb + 4], dtb_d, [], ['prm'], 'd_c1')
        dma('sp', prm[:, c_pm:c_pm + 1], pmask_d, [], ['prm'], 'd_c1')
        dma('sp', prm[:, c_negA:c_negA + 4], alog_d, [], ['prm'], 'd_c1')
        cT = sb("cTs", [P, 8])
        bpp = sb("bpp", [P, 4, 8])
        gpp = sb("gpp", [P, 2, 8])
        cwm = sb("cwm", [P, 4, 3])
        cwq = sb("cwq", [P, 12, 4])
        dma('sp', cT[:], cT_d, [], ['cT'], 'd_cT')
        dma('sp', bpp[:], bpp_d, [], ['bpp'], 'd_bpp')
        dma('sp', gpp[:], gpp_d, [], ['gpp'], 'd_gpp')
        dma('sp', cwm[:], cwm_d, [], ['cwm'], 'd_cwm')
        dma('sp', cwq[:], cwq_d, [], ['cwq'], 'd_cwq')
        dma('sp', modbc[:], bbc_d, [], ['modbc'], 'd_c3')
        dma('pool', cmb[:], cmask_d, [], ['cmb'], 'd_cmb')

        def cast_rows(dst, src, nrows, c0, c1, name, sem):
            for r0 in range(0, nrows, 128):
                r1 = min(nrows, r0 + 128)
                for a in range(c0, c1, 2048):
                    b_ = min(c1, a + 2048)
                    dma('pool', dst[r0:r1, a:b_], src[r0:r1, a:b_], [], [name], sem)

        cast_rows(win_bf, w_in_d, D, 1536, PIN, 'win_bf_b', 'd_w0')
        cast_rows(win_bf, w_in_d, D, 0, 1536, 'win_bf_a', 'd_w1')

        op('dve', lambda e: e.tensor_copy(out=identb_t[:], in_=identf), ['cst'], ['identb'])
        op('dve', lambda e: e.tensor_copy(out=onesb_t[:], in_=onesf), ['cst'], ['onesb'])
        op('dve', lambda e: e.tensor_copy(out=blk1_t[:], in_=cst[:, 3, :]), ['cst'], ['blk1'])
        for h in range(4):
            op('dve', lambda e: e.tensor_copy(out=m_cs[:, h, :], in_=cst[:, 4, :]), ['cst'], ['m_cs'])
            op('dve', lambda e: e.tensor_copy(out=m_sc[:, h, :], in_=Uf), ['cst'], ['m_sc'])
            op('dve', lambda e: e.tensor_tensor(out=m_scn[:, h, :], in0=identf, in1=Uf, op=ALU.subtract),
               ['cst'], ['m_scn'])
            op('dve', lambda e: e.tensor_tensor(out=m_scn[:, h, :], in0=m_scn[:, h, :], in1=cmb[:, 0, :], op=ALU.mult),
               ['m_scn', 'cmb'], ['m_scn'])
        op('act', lambda e: e.activation(out=prm[:, c_negA:c_negA + 4], in_=prm[:, c_negA:c_negA + 4], func=AF.Exp),
           ['prm'], ['prm'])
        op('dve', lambda e: e.tensor_scalar(out=prm[:, c_negA:c_negA + 4], in0=prm[:, c_negA:c_negA + 4],
                                            scalar1=-1.0, scalar2=None, op0=ALU.mult), ['prm'], ['prm'])
        for cc in range(12):
            for j in range(4):
                eng = 'dve' if (cc + j) % 2 == 0 else 'pool'
                op(eng, lambda e: e.tensor_scalar(out=diagq[:, cc, j, :], in0=identf, scalar1=cwq[:, cc, j:j + 1],
                                                  scalar2=0.0, op0=ALU.mult, op1=ALU.add), ['cst', 'cwq'], ['diagq'])
        for jj in range(4):
            for j in range(3):
                eng = 'dve' if (jj + j) % 2 == 0 else 'pool'
                op(eng, lambda e: e.tensor_scalar(out=diagm[:, jj, j, :], in0=identf, scalar1=cwm[:, jj, j:j + 1],
                                                  scalar2=0.0, op0=ALU.mult, op1=ALU.add), ['cst', 'cwm'], ['diagm'])
        op('pool', lambda e: e.memset(S32[:], 0.0), [], ['S32'])
        op('pool', lambda e: e.memset(Sbf[:], 0.0), [], ['Sbf'])
        op('pool', lambda e: e.memset(qkvp[:], 0.0), [], ['qkvp%d' % i for i in range(12)])
        op('pool', lambda e: e.memset(ucv[:], 0.0), [], ['ucv'])

        cact = sb("cact", [P, 8])
        ctmp = sb("ctmp", [P, 8])
        op('act', lambda e: e.activation(out=ctmp[:], in_=cT[:], func=AF.Exp, scale=-1.0), ['cT'], ['ctmp'])
        op('act', lambda e: e.activation(out=ctmp[:], in_=ctmp[:], func=AF.Ln, bias=1.0), ['ctmp'], ['ctmp'])
        op('act', lambda e: e.activation(out=ctmp[:], in_=ctmp[:], func=AF.Exp, scale=-1.0), ['ctmp'], ['ctmp'])
        op('dve', lambda e: e.tensor_tensor(out=cact[:], in0=cT[:], in1=ctmp[:], op=ALU.mult), ['cT', 'ctmp'], ['cact'])
        MIXN = ['mixT%d' % i_ for i_ in range(8)]
        mixf = mixT[:].rearrange("p a b -> p (a b)").bitcast(F32)
        crep = mixf[:, 0:1024].rearrange("p (a b) -> p a b", a=8)
        gfin = mixf[:, 1024:2048]
        wst = arena[:, 1024:1024 + 8192].rearrange("p (a b) -> p a b", a=8)
        for kc in range(8):
            op('dve', lambda e: e.tensor_scalar(out=crep[:, kc, :], in0=onesf, scalar1=cact[:, kc:kc + 1],
                                                scalar2=None, op0=ALU.mult), ['cst', 'cact'], MIXN)
        dma('sp', gfin, gfin_d, [], MIXN, 'd_gfin')
        pp_cols = [0, 1, 3, 4]
        ppres = sb("ppres", [P, 4, 8])
        for vi, vcol in enumerate(pp_cols):
            dma('sp', wst, w_ada_d[:, vcol * D:(vcol + 1) * D].rearrange("(k p) c -> p k c", p=P),
                [], ['wst'], 'd_wst')
            ps = newps()
            for j in range(8):
                for kc in range(8):
                    op('pe', lambda e: e.matmul(ps.t[:, j:j + 1], lhsT=wst[:, kc, j * P:(j + 1) * P],
                                                rhs=cact[:, kc:kc + 1], start=(kc == 0), stop=(kc == 7)),
                       ['wst', 'cact'], [ps.name])
            op('dve', lambda e: e.tensor_tensor(out=ppres[:, vi, :], in0=ps.t[:, 0:8], in1=bpp[:, vi, :], op=ALU.add),
               [ps.name, 'bpp'], ['ppres'])
        for (vs, vh, gi, cg, csf) in ((1, 0, 0, c_gsm, c_shm), (3, 2, 1, c_gsf, c_shf)):
            op('dve', lambda e: e.scalar_tensor_tensor(out=prm[:, cg:cg + 8], in0=ppres[:, vs, :], scalar=1.0,
                                                       in1=gpp[:, gi, :], op0=ALU.add, op1=ALU.mult),
               ['ppres', 'gpp'], ['prm'])
            op('dve', lambda e: e.tensor_copy(out=prm[:, csf:csf + 8], in_=ppres[:, vh, :]), ['ppres'], ['prm'])
        bsrc = [(w_ada_d, 2), (w_ada_d, 5), (w_adaf_d, 0), (w_adaf_d, 1)]
        XN = ['X%d' % t_ for t_ in range(NT)]
        xst = X[:].rearrange("p a b -> p (a b)").rearrange("p (a b) -> p a b", a=8)

        def bc_job(jid):
            vi, half = jid // 2, jid % 2
            wd, vcol = bsrc[vi]
            dma('sp', xst, wd[:, vcol * D + half * 512:vcol * D + (half + 1) * 512].rearrange("(k p) c -> p k c", p=P),
                [], XN, 'd_x0')
            ps = newps()
            for kc in range(8):
                op('pe', lambda e: e.matmul(ps.t[:], lhsT=crep[:, kc, :], rhs=xst[:, kc, :],
                                            start=(kc == 0), stop=(kc == 7)), XN + MIXN, [ps.name])
            sl = modbc[:, vi, half * 512:(half + 1) * 512]
            op('dve', lambda e: e.tensor_tensor(out=sl, in0=ps.t[:], in1=sl, op=ALU.add), [ps.name, 'modbc'], ['modbc'])
            if vi == 3:
                op('dve', lambda e: e.scalar_tensor_tensor(out=sl, in0=sl, scalar=1.0,
                                                           in1=gfin[:, half * 512:(half + 1) * 512],
                                                           op0=ALU.add, op1=ALU.mult), ['modbc'] + MIXN, ['modbc'])
        bc_jobs = list(range(8))
        gate_m_bc = modbc[:, 0, :]
        gate_f_bc = modbc[:, 1, :]
        sh_o_bc = modbc[:, 2, :]
        gs_o_bc = modbc[:, 3, :]

        cast_rows(wout_bf, w_out_d, D, 0, D, 'wout_bf', 'd_w2')
        cast_rows(wgu_bf, w_gu_d, D, 0, 2 * HID, 'wgu_bf', 'd_w3')
        cast_rows(wdn_bf, w_dn_d, HID, 0, D, 'wdn_bf', 'd_w4')
        dma('sp', wab[:], win_bf[:, 3584:3592].rearrange("(k p) c -> p k c", p=P), ['win_bf_b'], ['wab'], 'd_c4')
        S.fence()
        if stop <= 1:
            nblk_pre = 0
            nblk_own = 0
        if 'modbc' in dbg_d:
            dma('pool', dbg_d['modbc'], modbc[:], ['modbc'], [], 'd_dbg')
            dma('pool', dbg_d['prm'], prm[:, 0:46], ['prm'], [], 'd_dbg')

        def win_src(g):
            return ('win_bf_a' if g < 3 else 'win_bf_b',
                    win_bf[:, g * 512:(g + 1) * 512].rearrange("(k p) c -> p k c", p=P), 8, 512)

        def wout_src(hf):
            return ('wout_bf', wout_bf[:, hf * 512:(hf + 1) * 512].rearrange("(k p) c -> p k c", p=P), 8, 512)

        def wgu_src(g, up):
            c0 = (HID if up else 0) + g * 512
            w = min(512, HID - g * 512)
            return ('wgu_bf', wgu_bf[:, c0:c0 + w].rearrange("(k p) c -> p k c", p=P), 8, w)

        def wdn_src(hf, G):
            r0 = G * 8 * P
            n = min(8, NHC - G * 8)
            return ('wdn_bf', wdn_bf[r0:r0 + n * P, hf * 512:(hf + 1) * 512].rearrange("(k p) c -> p k c", p=P), n, 512)

        plan = []
        for blk in range(nblk_pre):
            if blk == nblk_pre - 1:
                plan += [win_src(2), win_src(1), win_src(3)]
            plan += [win_src(4), win_src(5)]
        for blk in range(nblk_own):
            plan += [win_src(2), win_src(1), win_src(0), win_src(6), win_src(3), win_src(4), win_src(5)]
            plan += [wout_src(0), wout_src(1)]
            for g in range(6):
                plan += [wgu_src(g, False), wgu_src(g, True)]
            for hf in range(2):
                for G in range(3):
                    plan += [wdn_src(hf, G)]
        rstate = {'issued': 0, 'used': 0}

        def ring_issue(upto):
            while rstate['issued'] < min(upto, len(plan)):
                i = rstate['issued']
                name, src, nk, w = plan[i]
                slot = i % NSLOT
                dma('sp', ring[slot][:, 0:nk, 0:w], src, [name], ['ring%d' % slot], 'd_ring%d' % slot)
                rstate['issued'] += 1

        def ring_get_n(k):
            i = rstate['used']
            ring_issue(i + NSLOT)
            rstate['used'] += k
            res = []
            for q_ in range(k):
                slot = (i + q_) % NSLOT
                res += [ring[slot], 'ring%d' % slot]
            return res

        def ring_get():
            return ring_get_n(1)

        def load_x(src_blk_ap):
            for tt in range(NT):
                dma('sp', X[:, tt, :], src_blk_ap[:, tt, :], [], ['X%d' % tt], 'd_x%d' % tt)

        def norm_T(c_gs, c_sh):
            for tt in range(NT):
                op('act', lambda e: e.activation(out=xsb[:, tt, :], in_=X[:, tt, :], func=AF.Square,
                                                 accum_out=ssx[:, tt:tt + 1]), ['X%d' % tt], ['xsb%d' % tt, 'ssx'])
            op('act', lambda e: e.activation(out=ssx[:, 0:4], in_=ssx[:, 0:4], func=AF.Ln, scale=1.0 / D, bias=EPS),
               ['ssx'], ['ssx'])
            op('act', lambda e: e.activation(out=ssx[:, 0:4], in_=ssx[:, 0:4], func=AF.Exp, scale=-0.5),
               ['ssx'], ['ssx'])
            for tt in range(NT):
                if tt % 2 == 0:
                    op('dve', lambda e: e.tensor_scalar(out=xsb[:, tt, :], in0=X[:, tt, :], scalar1=ssx[:, tt:tt + 1],
                                                        scalar2=None, op0=ALU.mult), ['X%d' % tt, 'ssx'], ['xsb%d' % tt])
                else:
                    op('act', lambda e: e.activation(out=xsb[:, tt, :], in_=X[:, tt, :], func=AF.Copy,
                                                     scale=ssx[:, tt:tt + 1]), ['X%d' % tt, 'ssx'], ['xsb%d' % tt])
            for kp in range(4):
                ps = newps()
                for k2 in range(2):
                    kc = kp * 2 + k2
                    for tt in range(NT):
                        op('pe', lambda e: e.transpose(out=ps.b[:, k2 * 512 + tt * P:k2 * 512 + (tt + 1) * P],
                                                       in_=xsb[:, tt, kc * P:(kc + 1) * P], identity=identb_t[:]),
                           ['xsb%d' % tt, 'identb'], [ps.name])
                for k2 in range(2):
                    kc = kp * 2 + k2
                    src = ps.b[:, k2 * 512:(k2 + 1) * 512]
                    if kp % 2 == 0:
                        op('act', lambda e: e.activation(out=hT[:, kc, :], in_=src, func=AF.Identity,
                                                         scale=prm[:, c_gs + kc:c_gs + kc + 1],
                                                         bias=prm[:, c_sh + kc:c_sh + kc + 1]),
                           [ps.name, 'prm'], ['hT%d' % kc])
                    else:
                        op('dve', lambda e: e.tensor_scalar(out=hT[:, kc, :], in0=src,
                                                            scalar1=prm[:, c_gs + kc:c_gs + kc + 1],
                                                            scalar2=prm[:, c_sh + kc:c_sh + kc + 1],
                                                            op0=ALU.mult, op1=ALU.add), [ps.name, 'prm'], ['hT%d' % kc])

        def inproj(wt, wname, j):
            ps = newps()
            for kc in range(8):
                op('pe', lambda e: e.matmul(ps.t[:], lhsT=wt[:, kc, j * P:(j + 1) * P], rhs=hT[:, kc, :],
                                            start=(kc == 0), stop=(kc == 7)), [wname, 'hT%d' % kc], [ps.name])
            return ps

        def silu_from_ps(ps, out_ap, tmp_ap, rd, wr, wr_tmp, mul_eng='dve'):
            op('act', lambda e: e.activation(out=tmp_ap, in_=ps, func=AF.Exp, scale=-1.0), rd, wr_tmp)
            op('act', lambda e: e.activation(out=tmp_ap, in_=tmp_ap, func=AF.Ln, bias=1.0), wr_tmp, wr_tmp)
            op('act', lambda e: e.activation(out=tmp_ap, in_=tmp_ap, func=AF.Exp, scale=-1.0), wr_tmp, wr_tmp)
            op('dve', lambda e: e.tensor_tensor(out=out_ap, in0=ps, in1=tmp_ap, op=ALU.mult), rd + wr_tmp, wr)

        def conv_mixer(full, last_pre):
            wt, wn = ring_get()
            for j in range(4):
                ps = inproj(wt, wn, j)
                op('act', lambda e: e.activation(out=hcb[:, j, :], in_=ps.t[:], func=AF.Copy), [ps.name], ['oT%d' % j])
            wt, wn = ring_get()
            for j in range(4):
                ps = inproj(wt, wn, j)
                op('dve', lambda e: e.tensor_tensor(out=ucv[:, j, 8:8 + TB], in0=ps.t[:], in1=hcb[:, j, :], op=ALU.mult),
                   [ps.name, 'oT%d' % j], ['ucv'])
            if full:
                for j in range(4):
                    ps = newps()
                    for tap in range(3):
                        op('pe', lambda e: e.matmul(ps.t[:], lhsT=diagm[:, j, tap, :], rhs=ucv[:, j, 6 + tap:6 + tap + TB],
                                                    start=(tap == 0), stop=(tap == 2)), ['diagm', 'ucv'], [ps.name])
                    op('act', lambda e: e.activation(out=hcb[:, j, :], in_=ps.t[:], func=AF.Copy), [ps.name], ['oT%d' % j])
            if last_pre:
                op('pool', lambda e: e.tensor_scalar(out=ucv[:, :, 6:8], in0=ucv[:, :, TB + 6:TB + 8],
                                                     scalar1=prm[:, c_pm:c_pm + 1], scalar2=0.0, op0=ALU.mult, op1=ALU.add),
                   ['ucv', 'prm'], ['ucv'])
            else:
                op('pool', lambda e: e.tensor_copy(out=ucv[:, :, 6:8], in_=ucv[:, :, TB + 6:TB + 8]), ['ucv'], ['ucv'])
            if not full:
                return
            wt, wn = ring_get()
            for j in range(4):
                ps = inproj(wt, wn, j)
                op('dve', lambda e: e.tensor_tensor(out=yaj[:], in0=ps.t[:], in1=hcb[:, j, :], op=ALU.mult),
                   [ps.name, 'oT%d' % j], ['yaj'])
                op('act', lambda e: e.activation(out=sqb[:], in_=yaj[:], func=AF.Square), ['yaj'], ['sqb'])
                ps2 = newps()
                op('pe', lambda e: e.matmul(ps2.t[:], lhsT=blk1_t[:], rhs=sqb[:], start=True, stop=True),
                   ['blk1', 'sqb'], [ps2.name])
                op('act', lambda e: e.activation(out=rst[:], in_=ps2.t[:], func=AF.Ln, scale=1.0 / 64, bias=EPS),
                   [ps2.name], ['rst'])
                op('act', lambda e: e.activation(out=rst[:], in_=rst[:], func=AF.Exp, scale=-0.5), ['rst'], ['rst'])
                op('dve', lambda e: e.scalar_tensor_tensor(out=mixT[:, j, :], in0=yaj[:],
                                                           scalar=prm[:, c_gconv + j:c_gconv + j + 1], in1=rst[:],
                                                           op0=ALU.mult, op1=ALU.mult), ['yaj', 'rst', 'prm'], ['mixT%d' % j])

        def z_gate():
            wt, wn = ring_get()
            for h in range(4):
                ps = inproj(wt, wn, h)
                silu_from_ps(ps.t[:], zs[:, h, :], rst[:], [ps.name], ['zs'], ['rst'])

        def qkv_in(need_q):
            for t in range(3):
                if t == 0 and not need_q:
                    continue
                wt, wn = ring_get()
                for h in range(4):
                    cc = t * 4 + h
                    ps = inproj(wt, wn, h)
                    if h % 2 == 0:
                        op('act', lambda e: e.activation(out=qkvp[:, cc, 8:8 + TB], in_=ps.t[:], func=AF.Copy),
                           [ps.name], ['qkvp%d' % cc])
                    else:
                        op('dve', lambda e: e.tensor_copy(out=qkvp[:, cc, 8:8 + TB], in_=ps.t[:]),
                           [ps.name], ['qkvp%d' % cc])

        def ab_in():
            ps = newps()
            for tt in range(NT):
                for kc in range(8):
                    op('pe', lambda e: e.matmul(ps.t[:, tt * 8:(tt + 1) * 8], lhsT=hT[:, kc, tt * P:(tt + 1) * P],
                                                rhs=wab[:, kc, :], start=(kc == 0), stop=(kc == 7)),
                       ['hT%d' % kc, 'wab'], [ps.name])
            op('dve', lambda e: e.tensor_copy(out=absb[:].rearrange("p a b -> p (a b)"), in_=ps.t[:, 0:32]),
               [ps.name], ['absb'])

        (I_XA, I_ABS, I_E1, I_L1, I_G, I_E2, I_BETA, I_NBETA, I_GC, I_GL, I_EGL, I_ECOL, I_KDS) = range(13)
        J_SSQ, J_SSK, J_RQ, J_RK, J_SQ, J_SKBG, J_SKD = range(13, 20)

        def sl16(i):
            return sc[:, i, :]

        def sl4(i, c):
            return sc[:, i, c * 4:(c + 1) * 4]

        def bch(a2, n=4):
            return bass.AP(a2.tensor, a2.offset, [list(a2.ap[0]), [0, n], list(a2.ap[1])])

        PSA = [PS(i) for i in range(4)]
        psB_i = [0]

        def newpsB():
            i = psB_i[0]
            psB_i[0] = (i + 1) % 3
            return PS(4 + i)

        def dn_scalars():
            dv = lambda f, r, w: op('dve', f, r, w)
            ac = lambda f, r, w: op('act', f, r, w)
            v3 = lambda i: sc[:, i, :].rearrange("p (a b) -> p a b", a=4)
            dv(lambda e: e.tensor_tensor(out=v3(I_XA), in0=absb[:, :, 0:4], in1=bch(prm[:, c_dtb:c_dtb + 4]), op=ALU.add),
               ['absb', 'prm'], ['sc_xa'])
            ac(lambda e: e.activation(out=sl16(I_ABS), in_=sl16(I_XA), func=AF.Abs), ['sc_xa'], ['sc_abs'])
            ac(lambda e: e.activation(out=sl16(I_E1), in_=sl16(I_ABS), func=AF.Exp, scale=-1.0), ['sc_abs'], ['sc_e1'])
            ac(lambda e: e.activation(out=sl16(I_L1), in_=sl16(I_E1), func=AF.Ln, bias=1.0), ['sc_e1'], ['sc_l1'])
            dv(lambda e: e.scalar_tensor_tensor(out=sl16(I_G), in0=sl16(I_XA), scalar=0.0, in1=sl16(I_L1),
                                                op0=ALU.max, op1=ALU.add), ['sc_xa', 'sc_l1'], ['sc_g'])
            dv(lambda e: e.tensor_tensor(out=v3(I_G), in0=v3(I_G), in1=bch(prm[:, c_negA:c_negA + 4]), op=ALU.mult),
               ['sc_g', 'prm'], ['sc_g'])
            ac(lambda e: e.activation(out=v3(I_E2), in_=absb[:, :, 4:8], func=AF.Exp, scale=-1.0), ['absb'], ['sc_e2'])
            dv(lambda e: e.tensor_scalar(out=sl16(I_E2), in0=sl16(I_E2), scalar1=1.0, scalar2=None, op0=ALU.add),
               ['sc_e2'], ['sc_e2'])
            dv(lambda e: e.reciprocal(out=sl16(I_BETA), in_=sl16(I_E2)), ['sc_e2'], ['sc_beta'])
            dv(lambda e: e.tensor_scalar(out=sl16(I_NBETA), in0=sl16(I_BETA), scalar1=-1.0, scalar2=None, op0=ALU.mult),
               ['sc_beta'], ['sc_nbeta'])
            ps = newps()
            op('pe', lambda e: e.matmul(ps.t[:, 0:16], lhsT=Uf, rhs=sl16(I_G), start=True, stop=True),
               ['cst', 'sc_g'], [ps.name])
            op('pe', lambda e: e.matmul(ps.t[:, 16:32], lhsT=onesf, rhs=sl16(I_G), start=True, stop=True),
               ['cst', 'sc_g'], [ps.name])
            dv(lambda e: e.tensor_copy(out=sl16(I_GC), in_=ps.t[:, 0:16]), [ps.name], ['sc_gc'])
            dv(lambda e: e.tensor_copy(out=sl16(I_GL), in_=ps.t[:, 16:32]), [ps.name], ['sc_gl'])
            ac(lambda e: e.activation(out=sl16(I_EGL), in_=sl16(I_GL), func=AF.Exp), ['sc_gl'], ['sc_egl'])
            ac(lambda e: e.activation(out=sl16(I_ECOL), in_=sl16(I_GC), func=AF.Exp), ['sc_gc'], ['sc_ecol'])
            dv(lambda e: e.tensor_tensor(out=sl16(I_KDS), in0=sl16(I_GL), in1=sl16(I_GC), op=ALU.subtract),
               ['sc_gl', 'sc_gc'], ['sc_kds'])
            ac(lambda e: e.activation(out=sl16(I_KDS), in_=sl16(I_KDS), func=AF.Exp), ['sc_kds'], ['sc_kds'])

        def prep_gen(c, need_q):
            ob = c % 2
            sfx = '_%d' % ob
            dv = lambda f, r, w: op('dve', f, r, w)
            ac = lambda f, r, w: op('act', f, r, w)
            po = lambda f, r, w: op('pool', f, r, w)
            types = [0, 1, 2] if need_q else [1, 2]
            gcb = bc(sl4(I_GC, c), P)
            def colbc(slot, h):
                a1 = sc[:, slot, c * 4 + h:c * 4 + h + 1]
                return bass.AP(a1.tensor, a1.offset, [list(a1.ap[0]), [0, P]])
            ps_gr = PSA[0]
            for h in range(4):
                op('pe', lambda e: e.transpose(out=ps_gr.t[:, h * P:(h + 1) * P], in_=colbc(I_GC, h), identity=identf),
                   ['sc_gc', 'cst'], [ps_gr.name])
            gr3 = ps_gr.v3()
            yield
            if need_q:
                ac(lambda e: e.activation(out=Erow, in_=gr3, func=AF.Exp), [ps_gr.name], ['Erow'])
            dv(lambda e: e.tensor_tensor(out=dm0, in0=gr3, in1=gcb, op=ALU.subtract), [ps_gr.name, 'sc_gc'], ['dm0'])
            ps_br = PSA[1]
            for h in range(4):
                op('pe', lambda e: e.transpose(out=ps_br.t[:, h * P:(h + 1) * P], in_=colbc(I_BETA, h), identity=identf),
                   ['sc_beta', 'cst'], [ps_br.name])
            yield
            dv(lambda e: e.tensor_scalar(out=dm1, in0=dm0, scalar1=0.0, scalar2=None, op0=ALU.max), ['dm0'], ['dm1'])
            po(lambda e: e.tensor_scalar(out=dm2, in0=dm0, scalar1=0.0, scalar2=-3.0e38, op0=ALU.min, op1=ALU.max), ['dm0'], ['dm2'])
            ac(lambda e: e.activation(out=dm1, in_=dm1, func=AF.Exp, scale=-1.0), ['dm1'], ['dm1'])
            ac(lambda e: e.activation(out=dm2, in_=dm2, func=AF.Exp), ['dm2'], ['dm2'])
            yield
            f2 = lambda a: a.rearrange("p a b -> p (a b)")
            po(lambda e: e.tensor_tensor(out=f2(dm1), in0=f2(dm1), in1=f2(m_cs[:]), op=ALU.mult), ['dm1', 'm_cs'], ['dm1'])
            po(lambda e: e.tensor_tensor(out=dm1, in0=dm1, in1=bc(sl4(I_NBETA, c), P), op=ALU.mult), ['dm1', 'sc_nbeta'], ['dm1'])
            po(lambda e: e.tensor_tensor(out=dm1b, in0=dm1, in1=bch(cmb[:, 0, :]), op=ALU.mult), ['dm1', 'cmb'], ['dm1b'])
            po(lambda e: e.tensor_tensor(out=f2(dm2), in0=f2(dm2), in1=f2(m_sc[:]), op=ALU.mult), ['dm2', 'm_sc'], ['dm2'])
            dv(lambda e: e.tensor_tensor(out=f2(dm0), in0=ps_br.t[:], in1=f2(m_scn[:]), op=ALU.mult),
               [ps_br.name, 'm_scn'], ['dm0'])
            po(lambda e: e.tensor_tensor(out=f2(dm0), in0=f2(dm0), in1=f2(dm2), op=ALU.mult), ['dm0', 'dm2'], ['dm0'])
            yield
            Tps = {}
            for ti, t in enumerate(types):
                ps = PSA[t]
                for h in range(4):
                    cc = t * 4 + h
                    for tap in range(4):
                        op('pe', lambda e: e.matmul(ps.t[:, h * P:(h + 1) * P], lhsT=diagq[:, cc, tap, :],
                                                    rhs=qkvp[:, cc, 5 + tap + c * P:5 + tap + (c + 1) * P],
                                                    start=(tap == 0), stop=(tap == 3)),
                           ['diagq', 'qkvp%d' % cc], [ps.name])
                silu_from_ps(ps.t[:], cs[:, t, :], e1t[:, ti % 2, :], [ps.name], ['cs%d' % t], ['e1t%d' % (ti % 2)])
                yield
            for t in types:
                ps = PSA[t]
                Tps[t] = ps
                for h in range(4):
                    op('pe', lambda e: e.transpose(out=ps.t[:, h * P:(h + 1) * P], in_=cs[:, t, h * P:(h + 1) * P],
                                                   identity=identf), ['cs%d' % t, 'cst'], [ps.name])
            yield
            for t, islot, rslot in ((0, J_SSQ, J_RQ), (1, J_SSK, J_RK)):
                if t not in types:
                    continue
                ac(lambda e: e.activation(out=sqj, in_=Tps[t].v3(), func=AF.Square), [Tps[t].name], ['e1t0'])
                dv(lambda e: e.tensor_reduce(out=sl4(islot, ob), in_=sqj, axis=AX.X, op=ALU.add), ['e1t0'], ['sc_ss%d' % t + sfx])
                ac(lambda e: e.activation(out=sl4(rslot, ob), in_=sl4(islot, ob), func=AF.Ln, bias=EPS),
                   ['sc_ss%d' % t + sfx], ['sc_r%d' % t + sfx])
                ac(lambda e: e.activation(out=sl4(rslot, ob), in_=sl4(rslot, ob), func=AF.Exp, scale=-0.5),
                   ['sc_r%d' % t + sfx], ['sc_r%d' % t + sfx])
                yield
            dv(lambda e: e.tensor_tensor(out=sl4(J_SKBG, ob), in0=sl4(J_RK, ob), in1=sl4(I_BETA, c), op=ALU.mult),
               ['sc_r1' + sfx, 'sc_beta'], ['sc_skbg' + sfx])
            dv(lambda e: e.tensor_tensor(out=sl4(J_SKBG, ob), in0=sl4(J_SKBG, ob), in1=sl4(I_ECOL, c), op=ALU.mult),
               ['sc_skbg' + sfx, 'sc_ecol'], ['sc_skbg' + sfx])
            dv(lambda e: e.tensor_tensor(out=sl4(J_SKD, ob), in0=sl4(J_RK, ob), in1=sl4(I_KDS, c), op=ALU.mult),
               ['sc_r1' + sfx, 'sc_kds'], ['sc_skd' + sfx])
            Tk = Tps[1].v3()
            Tv = Tps[2].v3()
            dv(lambda e: e.tensor_tensor(out=kh, in0=Tk, in1=bc(sl4(J_RK, ob), P), op=ALU.mult),
               [Tps[1].name, 'sc_r1' + sfx], ['kh'])
            dv(lambda e: e.tensor_tensor(out=kbg[ob], in0=Tk, in1=bc(sl4(J_SKBG, ob), P), op=ALU.mult),
               [Tps[1].name, 'sc_skbg' + sfx], ['kbg' + sfx])
            yield
            dv(lambda e: e.tensor_tensor(out=kdec[ob], in0=Tk, in1=bc(sl4(J_SKD, ob), P), op=ALU.mult),
               [Tps[1].name, 'sc_skd' + sfx], ['kdec' + sfx])
            dv(lambda e: e.tensor_tensor(out=vb[ob], in0=Tv, in1=bc(sl4(I_BETA, c), P), op=ALU.mult),
               [Tps[2].name, 'sc_beta'], ['vb' + sfx])
            if need_q:
                dv(lambda e: e.tensor_scalar(out=sl4(J_SQ, ob), in0=sl4(J_RQ, ob), scalar1=DK_SCALE, scalar2=None, op0=ALU.mult),
                   ['sc_r0' + sfx], ['sc_sq' + sfx])
                dv(lambda e: e.tensor_tensor(out=qh, in0=Tps[0].v3(), in1=bc(sl4(J_SQ, ob), P), op=ALU.mult),
                   [Tps[0].name, 'sc_sq' + sfx], ['qh'])
            yield
            ps_k = PSA[3]
            for h in range(4):
                op('pe', lambda e: e.transpose(out=ps_k.t[:, h * P:(h + 1) * P], in_=kh[:, h, :], identity=identf),
                   ['kh', 'cst'], [ps_k.name])
            ac(lambda e: e.activation(out=khT, in_=ps_k.v3(), func=AF.Copy), [ps_k.name], ['khT'])
            if need_q:
                ps_q = PSA[0]
                for h in range(4):
                    op('pe', lambda e: e.transpose(out=ps_q.t[:, h * P:(h + 1) * P], in_=qh[:, h, :], identity=identf),
                       ['qh', 'cst'], [ps_q.name])
                ac(lambda e: e.activation(out=qhT, in_=ps_q.v3(), func=AF.Copy), [ps_q.name], ['qhT'])
                dv(lambda e: e.tensor_tensor(out=qdT[ob], in0=ps_q.v3(), in1=Erow, op=ALU.mult), [ps_q.name, 'Erow'], ['qdT' + sfx])
            yield
            ps_G = PSA[1]
            for h in range(4):
                op('pe', lambda e: e.matmul(ps_G.t[:, h * P:(h + 1) * P], lhsT=khT[:, h, :], rhs=khT[:, h, :],
                                            start=True, stop=True), ['khT'], [ps_G.name])
            if need_q:
                ps_A = PSA[2]
                for h in range(4):
                    op('pe', lambda e: e.matmul(ps_A.t[:, h * P:(h + 1) * P], lhsT=khT[:, h, :], rhs=qhT[:, h, :],
                                                start=True, stop=True), ['khT', 'qhT'], [ps_A.name])
            dv(lambda e: e.tensor_tensor(out=Lbd[ob], in0=ps_G.v3(), in1=dm1b, op=ALU.mult), [ps_G.name, 'dm1b'], ['Lbd' + sfx])
            dv(lambda e: e.tensor_tensor(out=Mbd[ob], in0=ps_G.v3(), in1=dm0, op=ALU.mult), [ps_G.name, 'dm0'], ['Mbd' + sfx])
            dv(lambda e: e.tensor_tensor(out=Nf[ob], in0=ps_G.v3(), in1=dm1, op=ALU.mult), [ps_G.name, 'dm1'], ['Nf' + sfx])
            if need_q:
                dv(lambda e: e.tensor_tensor(out=attnT[ob], in0=ps_A.v3(), in1=dm2, op=ALU.mult), [ps_A.name, 'dm2'], ['attnT' + sfx])
            yield

        def inv_gen(c, need_q):
            ob = c % 2
            sfx = '_%d' % ob
            dv = lambda f, r, w: op('dve', f, r, w)
            ac = lambda f, r, w: op('act', f, r, w)
            idb3 = bch(identb_t[:])
            dv(lambda e: e.tensor_tensor(out=Qb, in0=Mbd[ob], in1=idb3, op=ALU.add), ['Mbd' + sfx, 'identb'], ['Q'])
            Lc, Mc, Lcn, Mcn = Lbd[ob], Mbd[ob], 'Lbd' + sfx, 'Mbd' + sfx
            for ki, k in enumerate((1, 2, 4, 8)):
                nx = ki % 2
                if k > 1:
                    psQ = newpsB()
                    for h in range(4):
                        op('pe', lambda e: e.matmul(psQ.t[:, h * P:(h + 1) * P], lhsT=Lc[:, h, :], rhs=Qb[:, h, :],
                                                    start=True, stop=True), [Lcn, 'Q'], [psQ.name])
                if k < 8:
                    psA = newpsB()
                    psB = newpsB()
                    for h in range(4):
                        op('pe', lambda e: e.matmul(psA.t[:, h * P:(h + 1) * P], lhsT=Mc[:, h, :], rhs=Lc[:, h, :],
                                                    start=True, stop=True), [Lcn, Mcn], [psA.name])
                    for h in range(4):
                        op('pe', lambda e: e.matmul(psB.t[:, h * P:(h + 1) * P], lhsT=Lc[:, h, :], rhs=Mc[:, h, :],
                                                    start=True, stop=True), [Lcn, Mcn], [psB.name])
                yield
                if k > 1:
                    dv(lambda e: e.tensor_tensor(out=Qb, in0=psQ.v3(), in1=Qb, op=ALU.add), [psQ.name, 'Q'], ['Q'])
                if k < 8:
                    ac(lambda e: e.activation(out=Lt[nx], in_=psA.v3(), func=AF.Copy), [psA.name], ['Lt%d' % nx])
                    ac(lambda e: e.activation(out=Mt[nx], in_=psB.v3(), func=AF.Copy), [psB.name], ['Mt%d' % nx])
                    Lc, Mc, Lcn, Mcn = Lt[nx], Mt[nx], 'Lt%d' % nx, 'Mt%d' % nx
                yield
            Xn, Wp, Xnn, Wpn = Lt[1], Mt[1], 'Lt1', 'Mt1'
            for lvl in (1, 2, 3):
                psT = newpsB()
                for h in range(4):
                    op('pe', lambda e: e.transpose(out=psT.b[:, h * P:(h + 1) * P], in_=Qb[:, h, :], identity=identb_t[:]),
                       ['Q', 'identb'], [psT.name])
                psW = newpsB()
                for h in range(4):
                    op('pe', lambda e: e.matmul(psW.t[:, h * P:(h + 1) * P], lhsT=Nf[ob][:, h, :], rhs=Qb[:, h, :],
                                                start=True, stop=True), ['Nf' + sfx, 'Q'], [psW.name])
                yield
                ac(lambda e: e.activation(out=Xn, in_=psT.b[:, 0:512].rearrange("p (a b) -> p a b", a=4), func=AF.Copy),
                   [psT.name], [Xnn])
                dv(lambda e: e.tensor_tensor(out=Wp, in0=psW.v3(), in1=bch(cmb[:, lvl, :]), op=ALU.mult),
                   [psW.name, 'cmb'], [Wpn])
                psZ = newpsB()
                for h in range(4):
                    op('pe', lambda e: e.matmul(psZ.t[:, h * P:(h + 1) * P], lhsT=Xn[:, h, :], rhs=Wp[:, h, :],
                                                start=True, stop=True), [Xnn, Wpn], [psZ.name])
                yield
                dv(lambda e: e.tensor_tensor(out=Qb, in0=psZ.v3(), in1=Qb, op=ALU.add), [psZ.name, 'Q'], ['Q'])
            ps_w = newpsB()
            ps_u = newpsB()
            for h in range(4):
                op('pe', lambda e: e.matmul(ps_w.t[:, h * P:(h + 1) * P], lhsT=kbg[ob][:, h, :], rhs=Qb[:, h, :],
                                            start=True, stop=True), ['kbg' + sfx, 'Q'], [ps_w.name])
            for h in range(4):
                op('pe', lambda e: e.matmul(ps_u.t[:, h * P:(h + 1) * P], lhsT=Qb[:, h, :], rhs=vb[ob][:, h, :],
                                            start=True, stop=True), ['vb' + sfx, 'Q'], [ps_u.name])
            yield
            ac(lambda e: e.activation(out=wT, in_=ps_w.v3(), func=AF.Copy), [ps_w.name], ['wT'])
            ac(lambda e: e.activation(out=u_sb, in_=ps_u.v3(), func=AF.Copy), [ps_u.name], ['u'])
            ps_p = PS(7)
            for h in range(4):
                op('pe', lambda e: e.matmul(ps_p.t[:, h * P:(h + 1) * P], lhsT=wT[:, h, :], rhs=Sbf[:, h, :],
                                            start=True, stop=True), ['wT', 'Sbf'], [ps_p.name])
            yield
            dv(lambda e: e.tensor_tensor(out=vnew, in0=u_sb, in1=ps_p.v3(), op=ALU.subtract), ['u', ps_p.name], ['vnew'])
            if need_q:
                ps_o = PS(7)
                for h in range(4):
                    op('pe', lambda e: e.matmul(ps_o.t[:, h * P:(h + 1) * P], lhsT=Sbf[:, h, :], rhs=qdT[ob][:, h, :],
                                                start=True, stop=False), ['Sbf', 'qdT' + sfx], [ps_o.name])
                    op('pe', lambda e: e.matmul(ps_o.t[:, h * P:(h + 1) * P], lhsT=vnew[:, h, :], rhs=attnT[ob][:, h, :],
                                                start=False, stop=True), ['vnew', 'attnT' + sfx], [ps_o.name])
            ps_s = PS(3)
            for h in range(4):
                op('pe', lambda e: e.matmul(ps_s.t[:, h * P:(h + 1) * P], lhsT=kdec[ob][:, h, :], rhs=vnew[:, h, :],
                                            start=True, stop=True), ['kdec' + sfx, 'vnew'], [ps_s.name])
            yield
            if need_q:
                ac(lambda e: e.activation(out=oT[:, :, c * P:(c + 1) * P], in_=ps_o.v3(), func=AF.Copy), [ps_o.name], ['oT0', 'oT1', 'oT2', 'oT3'])
            for h in range(4):
                dv(lambda e: e.scalar_tensor_tensor(out=S32[:, h, :], in0=S32[:, h, :],
                                                    scalar=sc[:, I_EGL, c * 4 + h:c * 4 + h + 1],
                                                    in1=ps_s.t[:, h * P:(h + 1) * P], op0=ALU.mult, op1=ALU.add),
                   ['S32', 'sc_egl', ps_s.name], ['S32'])
            ac(lambda e: e.activation(out=Sbf[:], in_=S32[:], func=AF.Copy), ['S32'], ['Sbf'])
            yield

        def run_gens(*gens):
            gens = [g for g in gens if g is not None]
            while gens:
                for g in list(gens):
                    try:
                        next(g)
                    except StopIteration:
                        gens.remove(g)

        def dn_block(need_q):
            dn_scalars()
            run_gens(prep_gen(0, need_q))
            for c in range(4):
                run_gens(inv_gen(c, need_q), prep_gen(c + 1, need_q) if c < 3 else None)

        def halo_qkv(last_pre, need_q):
            for cc in range(12):
                if cc < 4 and not need_q and not last_pre:
                    continue
                nm = 'qkvp%d' % cc
                eng = 'pool' if cc % 2 == 0 else 'dve'
                if last_pre:
                    op(eng, lambda e: e.tensor_scalar(out=qkvp[:, cc, 5:8], in0=qkvp[:, cc, TB + 5:TB + 8],
                                                      scalar1=prm[:, c_pm:c_pm + 1], scalar2=0.0, op0=ALU.mult, op1=ALU.add),
                       [nm, 'prm'], [nm])
                else:
                    op(eng, lambda e: e.tensor_copy(out=qkvp[:, cc, 5:8], in_=qkvp[:, cc, TB + 5:TB + 8]), [nm], [nm])

        def dn_out():
            rtmp = [cs[:, 0, :], cs[:, 1, :], cs[:, 2, :], e1t[:, 0, :]]
            rnm = ['cs0', 'cs1', 'cs2', 'e1t0']
            stmp = [t_.rearrange("p a b -> p (a b)") for t_ in (Lt[0], Lt[1], Mt[0], Mt[1])]
            snm = ['Lt0', 'Lt1', 'Mt0', 'Mt1']
            for h in range(4):
                r_, rn, q_, qn = rtmp[h], rnm[h], stmp[h], snm[h]
                op('act', lambda e: e.activation(out=q_, in_=oT[:, h, :], func=AF.Square), ['oT%d' % h], [qn])
                ps = newps()
                op('pe', lambda e: e.matmul(ps.t[:], lhsT=onesb_t[:], rhs=q_, start=True, stop=True),
                   ['onesb', qn], [ps.name])
                op('act', lambda e: e.activation(out=r_, in_=ps.t[:], func=AF.Ln, scale=1.0 / P, bias=EPS),
                   [ps.name], [rn])
                op('act', lambda e: e.activation(out=r_, in_=r_, func=AF.Exp, scale=-0.5), [rn], [rn])
                op('dve', lambda e: e.scalar_tensor_tensor(out=r_, in0=oT[:, h, :],
                                                           scalar=prm[:, c_gdn:c_gdn + 1], in1=r_,
                                                           op0=ALU.mult, op1=ALU.mult), ['oT%d' % h, rn, 'prm'], [rn])
                op('pool', lambda e: e.tensor_tensor(out=mixT[:, 4 + h, :], in0=r_, in1=zs[:, h, :], op=ALU.mult),
                   [rn, 'zs'], ['mixT%d' % (4 + h)])

        def out_proj():
            w0, n0, w1, n1 = ring_get_n(2)
            for tt in range(NT):
                for hf, (wt, wn) in enumerate(((w0, n0), (w1, n1))):
                    ps = newps()
                    for kc in range(8):
                        op('pe', lambda e: e.matmul(ps.t[:], lhsT=mixT[:, kc, tt * P:(tt + 1) * P], rhs=wt[:, kc, :],
                                                    start=(kc == 0), stop=(kc == 7)), ['mixT%d' % kc, wn], [ps.name])
                    yt, yn = (yaj, 'yaj') if hf == 0 else (yaj2, 'yaj2')
                    op('dve', lambda e: e.tensor_tensor(out=yt[:], in0=ps.t[:], in1=gate_m_bc[:, hf * 512:(hf + 1) * 512],
                                                        op=ALU.mult), [ps.name, 'modbc'], [yn])
                    xs_ = X[:, tt, hf * 512:(hf + 1) * 512]
                    op('pool' if hf == 0 else 'dve', lambda e: e.tensor_tensor(out=xs_, in0=xs_, in1=yt[:], op=ALU.add),
                       ['X%d' % tt, yn], ['X%d' % tt])

        def ffn():
            for g in range(6):
                wg, ng, wu, nu = ring_get_n(2)
                nj = min(4, NHC - g * 4)
                for j in range(nj):
                    J = g * 4 + j
                    psg = newps()
                    psu = newps()
                    for kc in range(8):
                        op('pe', lambda e: e.matmul(psg.t[:], lhsT=wg[:, kc, j * P:(j + 1) * P], rhs=hT[:, kc, :],
                                                    start=(kc == 0), stop=(kc == 7)), [ng, 'hT%d' % kc], [psg.name])
                    for kc in range(8):
                        op('pe', lambda e: e.matmul(psu.t[:], lhsT=wu[:, kc, j * P:(j + 1) * P], rhs=hT[:, kc, :],
                                                    start=(kc == 0), stop=(kc == 7)), [nu, 'hT%d' % kc], [psu.name])
                    op('act', lambda e: e.activation(out=rst[:], in_=psg.t[:], func=AF.Silu), [psg.name], ['rst'])
                    op('dve', lambda e: e.tensor_tensor(out=actT[:, J, :], in0=psu.t[:], in1=rst[:], op=ALU.mult),
                       [psu.name, 'rst'], ['actT%d' % J])
            for hf in range(2):
                accs = [newps() for _ in range(NT)]
                for G in range(3):
                    wd, nd = ring_get()
                    n = min(8, NHC - G * 8)
                    for tt in range(NT):
                        for j in range(n):
                            J = G * 8 + j
                            op('pe', lambda e: e.matmul(accs[tt].t[:], lhsT=actT[:, J, tt * P:(tt + 1) * P], rhs=wd[:, j, :],
                                                        start=(J == 0), stop=(J == NHC - 1)), ['actT%d' % J, nd], [accs[tt].name])
                for tt in range(NT):
                    yt, yn = (yaj, 'yaj') if tt % 2 == 0 else (yaj2, 'yaj2')
                    op('dve', lambda e: e.tensor_tensor(out=yt[:], in0=accs[tt].t[:],
                                                        in1=gate_f_bc[:, hf * 512:(hf + 1) * 512], op=ALU.mult),
                       [accs[tt].name, 'modbc'], [yn])
                    xs_ = X[:, tt, hf * 512:(hf + 1) * 512]
                    op('pool' if tt % 2 == 0 else 'dve', lambda e: e.tensor_tensor(out=xs_, in0=xs_, in1=yt[:], op=ALU.add),
                       ['X%d' % tt, yn], ['X%d' % tt])

        def final_norm():
            for tt in range(NT):
                op('act', lambda e: e.activation(out=xsb[:, tt, :], in_=X[:, tt, :], func=AF.Square,
                                                 accum_out=ssx[:, 4 + tt:5 + tt]), ['X%d' % tt], ['xsb%d' % tt, 'ssxf'])
            op('act', lambda e: e.activation(out=ssx[:, 4:8], in_=ssx[:, 4:8], func=AF.Ln, scale=1.0 / D, bias=EPS),
               ['ssxf'], ['ssxf'])
            op('act', lambda e: e.activation(out=ssx[:, 4:8], in_=ssx[:, 4:8], func=AF.Exp, scale=-0.5), ['ssxf'], ['ssxf'])
            for tt in range(NT):
                op('dve', lambda e: e.scalar_tensor_tensor(out=X[:, tt, :], in0=X[:, tt, :], scalar=ssx[:, 4 + tt:5 + tt],
                                                           in1=gs_o_bc, op0=ALU.mult, op1=ALU.mult),
                   ['X%d' % tt, 'ssxf', 'modbc'], ['X%d' % tt])
                op('pool' if tt % 2 == 0 else 'dve', lambda e: e.tensor_tensor(out=X[:, tt, :], in0=X[:, tt, :], in1=sh_o_bc, op=ALU.add),
                   ['X%d' % tt, 'modbc'], ['X%d' % tt])

        def dump(name, ap_, rd):
            if name in dbg_d:
                dma('pool', dbg_d[name], ap_, rd, [], 'd_dbg')

        xpre_v = x_pre.rearrange("(n t p) d -> n p t d", t=NT, p=P)
        xown_v = x_own.rearrange("(n t p) d -> n p t d", t=NT, p=P)
        out_v = out_d.rearrange("(n t p) d -> n p t d", t=NT, p=P)

        S.mark('prologue')
        for blk in range(nblk_pre):
            last = (blk == nblk_pre - 1)
            if MODEL_MARKS:
                S.mark('pre%d' % blk)
            if EXP_PRE:
                S.REN = {n_: n_ + '_%d' % (blk % 2) for n_ in EXP_PRE}
            load_x(xpre_v[blk])
            norm_T(c_gsm, c_shm)
            njob = -(-len(bc_jobs) // (nblk_pre - blk))
            for _ in range(njob):
                bc_job(bc_jobs.pop(0))
            if last:
                conv_mixer(False, True)
            qkv_in(last)
            ab_in()
            dn_block(False)
            halo_qkv(last, False)
            if last:
                op('dve', lambda e: e.tensor_scalar(out=S32[:], in0=S32[:], scalar1=prm[:, c_pm:c_pm + 1], scalar2=None,
                                                    op0=ALU.mult), ['S32', 'prm'], ['S32'])
                op('act', lambda e: e.activation(out=Sbf[:], in_=S32[:], func=AF.Copy), ['S32'], ['Sbf'])

        while bc_jobs:
            bc_job(bc_jobs.pop(0))
        if stop <= 2:
            nblk_own = 0
        S.REN = {}
        for blk in range(nblk_own):
            if EXPERIMENT:
                S.REN = {n_: n_ + '_%d' % (blk % 2) for n_ in EXPERIMENT}
            if MODEL_MARKS:
                S.mark('own%d' % blk)
            load_x(xown_v[blk])
            norm_T(c_gsm, c_shm)
            conv_mixer(True, False)
            z_gate()
            qkv_in(True)
            ab_in()
            if MODEL_MARKS:
                S.mark('  inproj')
            if stop <= 3:
                break
            dn_block(True)
            halo_qkv(False, True)
            if MODEL_MARKS:
                S.mark('  dn')
            if stop <= 4:
                break
            dn_out()
            if blk == 0:
                dump('mixT', mixT[:], MIXN)
            out_proj()
            if blk == 0:
                dump('x1', X[:], XN)
            if MODEL_MARKS:
                S.mark('  outproj')
            if stop <= 5:
                break
            norm_T(c_gsf, c_shf)
            if stop <= 6:
                break
            ffn()
            if MODEL_MARKS:
                S.mark('  ffn')
            if stop <= 7:
                break
            final_norm()
            for tt in range(NT):
                dma('sp', out_v[blk][:, tt, :], X[:, tt, :], ['X%d' % tt], [], 'd_out%d' % tt)
        S.finish('sp')
        build_nc.stats = (S.nops, S.nwaits)
        build_nc.sim_time = S.sim_time
        build_nc.busy = S.busy
        build_nc.marks = getattr(S, 'marks', [])
        build_nc.S = S
    return nc


def _pp(v):
    return np.ascontiguousarray(np.asarray(v, np.float32).reshape(8, 128).T)


def _consts():
    p = np.arange(128)
    ident = np.eye(128, dtype=np.float32)
    ones = np.ones((128, 128), np.float32)
    U = (p[:, None] <= p[None, :]).astype(np.float32)
    blk = ((p[:, None] // 64) == (p[None, :] // 64)).astype(np.float32)
    strict = (p[:, None] > p[None, :]).astype(np.float32)
    return np.ascontiguousarray(np.stack([ident, ones, U, blk, strict], axis=1))


def _cmask():
    p = np.arange(128)
    r, c = p[:, None], p[None, :]
    ms = [((r // 16) == (c // 16)).astype(np.float32)]
    for b in (16, 32, 64):
        ms.append((((r // (2 * b)) == (c // (2 * b))) & ((r // b) % 2 == 0) & ((c // b) % 2 == 1)).astype(np.float32))
    return np.ascontiguousarray(np.stack(ms, axis=1))


def make_in_maps(inp, nblk_own=8, nblk_pre=8):
    f = lambda a: np.ascontiguousarray(np.asarray(a, np.float32))
    x = f(inp['x'])
    c = f(inp['c'])
    w_ada = f(inp['w_ada'][0])
    b_ada = f(inp['b_ada'][0])
    w_adaf = f(inp['w_ada_final'])
    b_adaf = f(inp['b_ada_final'])
    bsl = lambda i: b_ada[i * D:(i + 1) * D]
    b_pp = np.ascontiguousarray(np.stack([_pp(bsl(0)), _pp(bsl(1)), _pp(bsl(3)), _pp(bsl(4))], axis=1))
    bb = np.stack([bsl(2), bsl(5), b_adaf[0:D], b_adaf[D:2 * D]], axis=0)
    b_bc = np.ascontiguousarray(np.broadcast_to(bb[None], (P, 4, D)))
    gfin_bc = np.ascontiguousarray(np.broadcast_to(f(inp['g_norm_final'])[None], (P, D)))
    g_pp = np.ascontiguousarray(np.stack([_pp(inp['g_norm_mix'][0]), _pp(inp['g_norm_ffn'][0])], axis=1))
    cwm = f(inp['conv_w_mix'][0])
    cw_mix = np.ascontiguousarray(cwm.T.reshape(4, 128, 3).transpose(1, 0, 2))
    cwq = f(inp['conv_w_qkv'][0])
    cw_qkv = np.ascontiguousarray(cwq.T.reshape(12, 128, 4).transpose(1, 0, 2))
    g_conv = np.ascontiguousarray(f(inp['g_conv_out'][0]).reshape(4, 128).T)
    g_dn = np.ascontiguousarray(f(inp['g_dn_out'][0]).reshape(128, 1))
    alog_bc = np.ascontiguousarray(np.broadcast_to(f(inp['a_log'][0])[None], (P, 4)))
    dtb_bc = np.ascontiguousarray(np.broadcast_to(f(inp['dt_bias'][0])[None], (P, 4)))
    shared = {
        "w_ada": w_ada, "w_ada_final": w_adaf, "b_pp": b_pp, "b_bc": b_bc, "gfin_bc": gfin_bc, "g_pp": g_pp,
        "w_in": f(inp['w_in'][0]), "w_out": f(inp['w_out'][0]), "w_gu": f(inp['w_gate_up'][0]),
        "w_down": f(inp['w_down'][0]), "cw_mix": cw_mix, "cw_qkv": cw_qkv, "g_conv": g_conv, "g_dn": g_dn,
        "alog_bc": alog_bc, "dtb_bc": dtb_bc, "consts": _consts(), "cmask": _cmask(),
    }
    n_own = nblk_own * TB
    n_pre = max(nblk_pre, 1) * TB
    S_ = x.shape[1]
    halfS = S_ // 2
    maps = []
    for core in range(8):
        b, half = core // 2, core % 2
        m = dict(shared)
        s0 = half * halfS
        m["x_own"] = np.ascontiguousarray(x[b, s0:s0 + n_own])
        p0 = s0 - n_pre if half == 1 else 0
        m["x_pre"] = np.ascontiguousarray(x[b, p0:p0 + n_pre])
        m["pmask"] = np.full((P, 1), float(half), np.float32)
        m["cT"] = _pp(c[b])
        maps.append(m)
    return maps


_NC_CACHE = {}


def kernel(**inputs):
    if 'nc' not in _NC_CACHE:
        _NC_CACHE['nc'] = build_nc()
    nc = _NC_CACHE['nc']
    maps = make_in_maps(inputs)
    res = run_bass_kernel_spmd(nc, maps, core_ids=list(range(8)))
    B, S_, _ = inputs['x'].shape
    out = np.empty((B, S_, D), np.float32)
    halfS = S_ // 2
    for core in range(8):
        b, half = core // 2, core % 2
        out[b, half * halfS:(half + 1) * halfS] = res.results[core]["out"]
    return out
```

```python
import numpy as np
from contextlib import ExitStack
import concourse.bass as bass
import concourse.mybir as mybir
from concourse.bass_utils import run_bass_kernel_spmd

F32 = mybir.dt.float32
BF = mybir.dt.bfloat16
AF = mybir.ActivationFunctionType
ALU = mybir.AluOpType
AX = mybir.AxisListType

P = 128
D = 1024
TB = 512
NT = 4
PIN = 3592
HID = 2816
NHC = 22
EPS = 1e-6
NSLOT = 4
MODEL_MARKS = False
EXPERIMENT = None
EXP_PRE = None
DK_SCALE = 128 ** -0.5


def PRIO(st, bl):
    return st - 0.03 * bl


class _Cap:
    def __init__(self):
        self.call = None

    def __getattr__(self, name):
        def f(*a, **k):
            self.call = (name, a, k)
            return None
        return f


def _free_elems(ap):
    n = 1
    for d in list(ap.shape)[1:]:
        n *= int(d)
    return n


class Sched:
    LAT = 0.25

    def __init__(self, nc, es):
        self.nc = nc
        self.es = es
        self.eng = {'pe': nc.tensor, 'dve': nc.vector, 'act': nc.scalar, 'pool': nc.gpsimd, 'sp': nc.sync}
        self.sem = {}
        self.cnt = {}
        for k in self.eng:
            self._newsem(k)
        self.seen = {e: {} for e in self.eng}
        self.buf = {}
        self.recs = []
        self.labels = []
        self.ticket = {}
        self.fin = {}
        self.free = {e: 0.0 for e in self.eng}
        self.n_emitted = 0
        self.crit = {}
        self.start = {}
        self.last_on = {}
        self.nwaits = 0
        self.nops = 0

    def _newsem(self, k):
        if k not in self.sem:
            self.sem[k] = self.es.enter_context(self.nc.semaphore("s_" + k))
            self.cnt[k] = 0

    def _dur(self, e, call, dsem):
        name, a, k = call
        try:
            if dsem is not None:
                out = k.get('out')
                nbytes = _free_elems(out) * int(out.shape[0]) * (2 if out.dtype == BF else 4)
                return 2.0 + nbytes / 250e3
            if e == 'pe':
                if name == 'transpose':
                    return 0.11
                rhs = k.get('rhs')
                cols = max(64, _free_elems(rhs))
                f = 4.0 if rhs.dtype == F32 else 1.0
                return max(0.1, cols * f / 1920.0)
            out = k.get('out', None)
            if out is None:
                out = k.get('ap', a[0] if a else None)
            n = _free_elems(out)
            if e == 'pool':
                return 0.2 + n / 450.0
            return 0.07 + n / 960.0
        except Exception:
            return 0.3

    REN = {}
    REN2 = {}
    ALIAS = {}

    def op(self, e, fn, reads=(), writes=(), dsem=None):
        if self.REN2:
            reads = [self.REN2.get(b, b) for b in reads]
            writes = [self.REN2.get(b, b) for b in writes]
        if self.REN:
            reads = [self.REN.get(b, b) for b in reads]
            writes = [self.REN.get(b, b) for b in writes]
        if self.ALIAS:
            reads = list(reads) + [p_ for b in reads for p_ in self.ALIAS.get(b, ())]
            writes = list(writes) + [p_ for b in writes for p_ in self.ALIAS.get(b, ())]
        cap = _Cap()
        fn(cap)
        call = cap.call
        assert call is not None
        rid = len(self.recs)
        deps = set()
        recs = self.recs
        for b in reads:
            st = self.buf.get(b)
            if st:
                deps.update(st[0])
                if b.startswith('ps'):
                    for r in st[1]:
                        if recs[r][0] != e:
                            deps.add(r)
        for b in writes:
            st = self.buf.get(b)
            if st:
                deps.update(st[0])
                deps.update(st[1])
        ws = set(writes)
        for b in writes:
            st = self.buf.setdefault(b, [[], []])
            if st[1]:
                st[0] = [rid]
                st[1] = []
            else:
                if len(st[0]) > 64:
                    st[0] = st[0][-64:]
                st[0].append(rid)
        for b in reads:
            if b in ws:
                continue
            st = self.buf.setdefault(b, [[], []])
            if len(st[1]) > 256:
                keep = {}
                for r in st[1]:
                    keep[recs[r][0]] = r
                st[1] = sorted(keep.values())
            st[1].append(rid)
        if dsem is not None:
            self._newsem(dsem)
        self.recs.append((e, call, deps, dsem, self._dur(e, call, dsem)))
        self.labels.append((list(writes) + ['-'])[0])
        return None

    def flush(self):
        recs = self.recs
        i0 = self.n_emitted
        n = len(recs)
        if i0 >= n:
            return
        LAT = self.LAT
        pend = {}
        users = {}
        for rid in range(i0, n):
            c = 0
            for d in recs[rid][2]:
                if d >= i0:
                    c += 1
                    users.setdefault(d, []).append(rid)
            pend[rid] = c
        bl = {}
        for rid in range(n - 1, i0 - 1, -1):
            m = 0.0
            e = recs[rid][0]
            for u_ in users.get(rid, ()):
                v = bl[u_] + (LAT if recs[u_][0] != e else 0.03)
                if v > m:
                    m = v
            bl[rid] = recs[rid][4] + m
        fin = self.fin
        free = self.free

        def tdep_of(rid):
            e = recs[rid][0]
            t = 0.0
            cr = None
            for d in recs[rid][2]:
                fd = fin.get(d, 0.0) + (LAT if recs[d][0] != e else 0.03)
                if fd > t:
                    t = fd
                    cr = d
            return t, cr

        ready = {e: {} for e in self.eng}
        for rid in range(i0, n):
            if pend[rid] == 0:
                ready[recs[rid][0]][rid] = tdep_of(rid)
        left = n - i0
        while left:
            best = None
            for e, rd in ready.items():
                if not rd:
                    continue
                fe = free[e]
                cand = None
                for rid, (td, cr) in rd.items():
                    st = td if td > fe else fe
                    key = (PRIO(st, bl[rid]), -bl[rid], rid)
                    if cand is None or key < cand[0]:
                        cand = (key, rid, td, cr, st)
                if best is None or cand[0] < best[0]:
                    best = (cand[0], cand[1], e, cand[2], cand[3], cand[4])
            _, rid, e, td, cr, t = best
            del ready[e][rid]
            self.crit[rid] = ('dep', cr) if (cr is not None and td >= free[e]) else ('eng', self.last_on.get(e))
            self.start[rid] = t
            self.last_on[e] = rid
            self._emit(rid)
            dur = recs[rid][4]
            if recs[rid][3] is not None:
                free[e] = t + 0.15
                fin[rid] = t + dur
            else:
                free[e] = t + dur
                fin[rid] = t + dur
            for u_ in users.get(rid, ()):
                pend[u_] -= 1
                if pend[u_] == 0:
                    ready[recs[u_][0]][u_] = tdep_of(u_)
            left -= 1
        self.n_emitted = n

    def _emit(self, rid):
        e, call, deps, dsem, _ = self.recs[rid]
        eng = self.eng[e]
        need = {}
        for d in deps:
            k, v = self.ticket[d]
            if e == 'pe' and k == 'pe':
                continue
            if need.get(k, 0) < v:
                need[k] = v
        seen = self.seen[e]
        for k, v in need.items():
            if seen.get(k, 0) >= v:
                continue
            eng.wait_ge(self.sem[k], v)
            seen[k] = v
            self.nwaits += 1
        name, a, kw = call
        ins = getattr(eng, name)(*a, **kw)
        self.nops += 1
        if dsem is not None:
            self.cnt[dsem] += 16
            ins.then_inc(self.sem[dsem], 16)
            self.ticket[rid] = (dsem, self.cnt[dsem])
        else:
            self.cnt[e] += 1
            ins.then_inc(self.sem[e], 1)
            self.ticket[rid] = (e, self.cnt[e])

    def mark(self, label):
        self.flush()
        if not hasattr(self, 'marks'):
            self.marks = []
        self.marks.append((label, max(self.free.values()), dict(self.free)))

    def fence(self):
        self.flush()
        ce = ['pe', 'dve', 'act', 'pool']
        for e in ce:
            for k in ce:
                v = self.cnt[k]
                if v > 0 and self.seen[e].get(k, 0) < v:
                    self.eng[e].wait_ge(self.sem[k], v)
                    self.seen[e][k] = v
                    self.nwaits += 1
        t = max(self.free[e] for e in ce)
        t = max([t] + [self.fin.get(r, 0.0) for r in range(max(0, self.n_emitted - 400), self.n_emitted)
                       if self.recs[r][3] is None])
        for e in ce:
            self.free[e] = t

    def finish(self, e='sp'):
        self.flush()
        eng = self.eng[e]
        for k, v in self.cnt.items():
            if v > 0 and self.seen[e].get(k, 0) < v:
                eng.wait_ge(self.sem[k], v)
                self.seen[e][k] = v
        self.sim_time = max(self.free.values())
        self.busy = {}
        for (e, call, deps, dsem, dur) in self.recs:
            self.busy[e] = self.busy.get(e, 0.0) + (0.15 if dsem is not None else dur)


def bc(ap, n):
    return bass.AP(ap.tensor, ap.offset, [list(x) for x in ap.ap] + [[0, n]])


def build_nc(nblk_own=8, nblk_pre=8, dbg=(), stop=99):
    nc = bass.Bass("TRN2", target_bir_lowering=False)
    NTOK_OWN = nblk_own * TB
    NTOK_PRE = max(nblk_pre, 1) * TB

    def din(name, shape, dt=F32):
        return nc.dram_tensor(name, list(shape), dt, kind="ExternalInput").ap()

    x_own = din("x_own", [NTOK_OWN, D])
    x_pre = din("x_pre", [NTOK_PRE, D])
    pmask_d = din("pmask", [P, 1])
    cT_d = din("cT", [P, 8])
    w_ada_d = din("w_ada", [D, 6 * D])
    w_adaf_d = din("w_ada_final", [D, 2 * D])
    bpp_d = din("b_pp", [P, 4, 8])
    bbc_d = din("b_bc", [P, 4, D])
    gfin_d = din("gfin_bc", [P, D])
    gpp_d = din("g_pp", [P, 2, 8])
    w_in_d = din("w_in", [D, PIN])
    w_out_d = din("w_out", [D, D])
    w_gu_d = din("w_gu", [D, 2 * HID])
    w_dn_d = din("w_down", [HID, D])
    cwm_d = din("cw_mix", [P, 4, 3])
    cwq_d = din("cw_qkv", [P, 12, 4])
    gconv_d = din("g_conv", [P, 4])
    gdn_d = din("g_dn", [P, 1])
    alog_d = din("alog_bc", [P, 4])
    dtb_d = din("dtb_bc", [P, 4])
    cmask_d = din("cmask", [P, 4, P])
    cst_d = din("consts", [P, 5, P])
    out_d = nc.dram_tensor("out", [NTOK_OWN, D], F32, kind="ExternalOutput").ap()
    dbg_d = {}
    for name, shape in dbg:
        dbg_d[name] = nc.dram_tensor("dbg_" + name, list(shape), F32, kind="ExternalOutput").ap()

    win_bf = nc.dram_tensor("win_bf", [D, PIN], BF, kind="Internal").ap()
    wout_bf = nc.dram_tensor("wout_bf", [D, D], BF, kind="Internal").ap()
    wgu_bf = nc.dram_tensor("wgu_bf", [D, 2 * HID], BF, kind="Internal").ap()
    wdn_bf = nc.dram_tensor("wdn_bf", [HID, D], BF, kind="Internal").ap()

    with ExitStack() as es:
        S = Sched(nc, es)
        op = S.op

        def sb(name, shape, dt=F32):
            return es.enter_context(nc.sbuf_tensor(name, list(shape), dt))

        psb = [es.enter_context(nc.psum_tensor("ps%d" % i, [P, 512], F32)) for i in range(8)]
        psi = [0]

        class PS:
            def __init__(self, i):
                self.name = "ps%d" % i
                self.t = psb[i]
                self.b = psb[i].bitcast(BF)

            def v3(self, a=4):
                return self.t[:].rearrange("p (a b) -> p a b", a=a)

        def newps():
            i = psi[0]
            psi[0] = (i + 1) % 8
            return PS(i)

        cst = sb("cst", [P, 5, P])
        identf = cst[:, 0, :]
        onesf = cst[:, 1, :]
        Uf = cst[:, 2, :]
        identb_t = sb("identb", [P, P], BF)
        onesb_t = sb("onesb", [P, P], BF)
        blk1_t = sb("blk1", [P, P], BF)
        m_cs = sb("m_cs", [P, 4, P])
        m_sc = sb("m_sc", [P, 4, P])
        m_scn = sb("m_scn", [P, 4, P])
        diagq = sb("diagq", [P, 12, 4, P], BF)
        diagm = sb("diagm", [P, 4, 3, P], BF)
        prm = sb("prm", [P, 64])
        c_gsm, c_shm, c_gsf, c_shf = 0, 8, 16, 24
        c_gconv, c_gdn, c_negA, c_dtb, c_pm = 32, 36, 37, 41, 45
        modbc = sb("modbc", [P, 4, D])
        wab = sb("wab", [P, 8, 8], BF)
        cmb = sb("cmb", [P, 4, P], BF)

        X = sb("X", [P, NT, D])
        xsb = sb("xsb", [P, NT, D], BF)
        ssx = sb("ssx", [P, 8])
        hT = sb("hT", [P, 8, TB], BF)
        mixT = sb("mixT", [P, 8, TB], BF)
        ring = [sb("ring%d" % i, [P, 8, 512], BF) for i in range(NSLOT)]
        ucv = sb("ucv", [P, 4, TB + 8], BF)
        yaj = sb("yaj", [P, TB])
        yaj2 = sb("yaj2", [P, TB])
        sqb = sb("sqb", [P, TB], BF)
        rst = sb("rst", [P, TB])
        qkvp = sb("qkvp", [P, 12, TB + 8], BF)
        zs = sb("zs", [P, 4, TB], BF)
        S32 = sb("S32", [P, 4, P])
        Sbf = sb("Sbf", [P, 4, P], BF)
        absb = sb("absb", [P, NT, 8])
        sc = sb("sc", [P, 24, 16])
        ARENA_W = 15360
        arena = sb("arena", [P, ARENA_W])

        aoff = [0]

        def carve(nwords, dt=F32, shape3=None):
            a = arena[:, aoff[0]:aoff[0] + nwords]
            aoff[0] += nwords
            if dt == BF:
                a = a.bitcast(BF)
            if shape3 is not None:
                a = a.rearrange("p (a b) -> p a b", a=shape3)
            return a

        cs = carve(1536, F32, 3)
        e1t = carve(1024, F32, 2)
        sqj = e1t[:, 0, :].rearrange("p (a b) -> p a b", a=4)
        Erow = carve(512, F32, 4)
        dm0 = carve(512, F32, 4)
        dm1 = carve(512, F32, 4)
        dm2 = carve(512, F32, 4)
        qh = carve(512, F32, 4)
        kh = carve(512, F32, 4)
        u_sb = carve(512, F32, 4)
        oT = carve(2048, F32, 4)
        hcb = oT
        dm1b = carve(256, BF, 4)
        khT = carve(256, BF, 4)
        qhT = carve(256, BF, 4)
        Lt = [carve(256, BF, 4), carve(256, BF, 4)]
        Mt = [carve(256, BF, 4), carve(256, BF, 4)]
        Qb = carve(256, BF, 4)
        wT = carve(256, BF, 4)
        vnew = carve(256, BF, 4)
        Nf = [carve(256, BF, 4), carve(256, BF, 4)]
        Lbd = [carve(256, BF, 4), carve(256, BF, 4)]
        Mbd = [carve(256, BF, 4), carve(256, BF, 4)]
        kbg = [carve(256, BF, 4), carve(256, BF, 4)]
        kdec = [carve(256, BF, 4), carve(256, BF, 4)]
        vb = [carve(256, BF, 4), carve(256, BF, 4)]
        qdT = [carve(256, BF, 4), carve(256, BF, 4)]
        attnT = [carve(256, BF, 4), carve(256, BF, 4)]
        assert aoff[0] <= ARENA_W, aoff[0]
        actT = arena[:, 0:5632].bitcast(BF).rearrange("p (a b) -> p a b", a=NHC)
        _pg = {'cs0': (0, 1), 'cs1': (2, 3), 'cs2': (4, 5), 'e1t0': (6, 7), 'e1t1': (8, 9), 'Erow': (10, 11),
               'dm0': (12, 13), 'dm1': (14, 15), 'dm2': (16, 17), 'qh': (18, 19), 'kh': (20, 21)}
        S.ALIAS = {k_: ['aw%d' % p_ for p_ in v_] for k_, v_ in _pg.items()}
        for J_ in range(NHC):
            S.ALIAS['actT%d' % J_] = ['aw%d' % J_]

        def scv(i, w=4):
            return sc[:, i, 0:w]

        def dma(e, out, in_, reads, writes, dsem, **kw):
            return op(e, lambda en: en.dma_start(out=out, in_=in_, **kw), reads=reads, writes=writes, dsem=dsem)

        dma('sp', cst[:], cst_d, [], ['cst'], 'd_c0')
        dma('sp', prm[:, c_gconv:c_gconv + 4], gconv_d, [], ['prm'], 'd_c1')
        dma('sp', prm[:, c_gdn:c_gdn + 1], gdn_d, [], ['prm'], 'd_c1')
        dma('sp', prm[:, c_dtb:c_dtb + 4], dtb_d, [], ['prm'], 'd_c1')
        dma('sp', prm[:, c_pm:c_pm + 1], pmask_d, [], ['prm'], 'd_c1')
        dma('sp', prm[:, c_negA:c_negA + 4], alog_d, [], ['prm'], 'd_c1')
        cT = sb("cTs", [P, 8])
        bpp = sb("bpp", [P, 4, 8])
        gpp = sb("gpp", [P, 2, 8])
        cwm = sb("cwm", [P, 4, 3])
        cwq = sb("cwq", [P, 12, 4])
        dma('sp', cT[:], cT_d, [], ['cT'], 'd_cT')
        dma('sp', bpp[:], bpp_d, [], ['bpp'], 'd_bpp')
        dma('sp', gpp[:], gpp_d, [], ['gpp'], 'd_gpp')
        dma('sp', cwm[:], cwm_d, [], ['cwm'], 'd_cwm')
        dma('sp', cwq[:], cwq_d, [], ['cwq'], 'd_cwq')
        dma('sp', modbc[:], bbc_d, [], ['modbc'], 'd_c3')
        dma('pool', cmb[:], cmask_d, [], ['cmb'], 'd_cmb')

        def cast_rows(dst, src, nrows, c0, c1, name, sem):
            for r0 in range(0, nrows, 128):
                r1 = min(nrows, r0 + 128)
                for a in range(c0, c1, 2048):
                    b_ = min(c1, a + 2048)
                    dma('pool', dst[r0:r1, a:b_], src[r0:r1, a:b_], [], [name], sem)

        cast_rows(win_bf, w_in_d, D, 1536, PIN, 'win_bf_b', 'd_w0')
        cast_rows(win_bf, w_in_d, D, 0, 1536, 'win_bf_a', 'd_w1')

        op('dve', lambda e: e.tensor_copy(out=identb_t[:], in_=identf), ['cst'], ['identb'])
        op('dve', lambda e: e.tensor_copy(out=onesb_t[:], in_=onesf), ['cst'], ['onesb'])
        op('dve', lambda e: e.tensor_copy(out=blk1_t[:], in_=cst[:, 3, :]), ['cst'], ['blk1'])
        for h in range(4):
            op('dve', lambda e: e.tensor_copy(out=m_cs[:, h, :], in_=cst[:, 4, :]), ['cst'], ['m_cs'])
            op('dve', lambda e: e.tensor_copy(out=m_sc[:, h, :], in_=Uf), ['cst'], ['m_sc'])
            op('dve', lambda e: e.tensor_tensor(out=m_scn[:, h, :], in0=identf, in1=Uf, op=ALU.subtract),
               ['cst'], ['m_scn'])
            op('dve', lambda e: e.tensor_tensor(out=m_scn[:, h, :], in0=m_scn[:, h, :], in1=cmb[:, 0, :], op=ALU.mult),
               ['m_scn', 'cmb'], ['m_scn'])
        op('act', lambda e: e.activation(out=prm[:, c_negA:c_negA + 4], in_=prm[:, c_negA:c_negA + 4], func=AF.Exp),
           ['prm'], ['prm'])
        op('dve', lambda e: e.tensor_scalar(out=prm[:, c_negA:c_negA + 4], in0=prm[:, c_negA:c_negA + 4],
                                            scalar1=-1.0, scalar2=None, op0=ALU.mult), ['prm'], ['prm'])
        for cc in range(12):
            for j in range(4):
                eng = 'dve' if (cc + j) % 2 == 0 else 'pool'
                op(eng, lambda e: e.tensor_scalar(out=diagq[:, cc, j, :], in0=identf, scalar1=cwq[:, cc, j:j + 1],
                                                  scalar2=0.0, op0=ALU.mult, op1=ALU.add), ['cst', 'cwq'], ['diagq'])
        for jj in range(4):
            for j in range(3):
                eng = 'dve' if (jj + j) % 2 == 0 else 'pool'
                op(eng, lambda e: e.tensor_scalar(out=diagm[:, jj, j, :], in0=identf, scalar1=cwm[:, jj, j:j + 1],
                                                  scalar2=0.0, op0=ALU.mult, op1=ALU.add), ['cst', 'cwm'], ['diagm'])
        op('pool', lambda e: e.memset(S32[:], 0.0), [], ['S32'])
        op('pool', lambda e: e.memset(Sbf[:], 0.0), [], ['Sbf'])
        op('pool', lambda e: e.memset(qkvp[:], 0.0), [], ['qkvp%d' % i for i in range(12)])
        op('pool', lambda e: e.memset(ucv[:], 0.0), [], ['ucv'])

        cact = sb("cact", [P, 8])
        ctmp = sb("ctmp", [P, 8])
        op('act', lambda e: e.activation(out=ctmp[:], in_=cT[:], func=AF.Exp, scale=-1.0), ['cT'], ['ctmp'])
        op('act', lambda e: e.activation(out=ctmp[:], in_=ctmp[:], func=AF.Ln, bias=1.0), ['ctmp'], ['ctmp'])
        op('act', lambda e: e.activation(out=ctmp[:], in_=ctmp[:], func=AF.Exp, scale=-1.0), ['ctmp'], ['ctmp'])
        op('dve', lambda e: e.tensor_tensor(out=cact[:], in0=cT[:], in1=ctmp[:], op=ALU.mult), ['cT', 'ctmp'], ['cact'])
        MIXN = ['mixT%d' % i_ for i_ in range(8)]
        mixf = mixT[:].rearrange("p a b -> p (a b)").bitcast(F32)
        crep = mixf[:, 0:1024].rearrange("p (a b) -> p a b", a=8)
        gfin = mixf[:, 1024:2048]
        wst = arena[:, 1024:1024 + 8192].rearrange("p (a b) -> p a b", a=8)
        for kc in range(8):
            op('dve', lambda e: e.tensor_scalar(out=crep[:, kc, :], in0=onesf, scalar1=cact[:, kc:kc + 1],
                                                scalar2=None, op0=ALU.mult), ['cst', 'cact'], MIXN)
        dma('sp', gfin, gfin_d, [], MIXN, 'd_gfin')
        pp_cols = [0, 1, 3, 4]
        ppres = sb("ppres", [P, 4, 8])
        prmf = sb("prmf", [P, 16])
        for vi, vcol in list(enumerate(pp_cols))[:2]:
            dma('sp', wst, w_ada_d[:, vcol * D:(vcol + 1) * D].rearrange("(k p) c -> p k c", p=P),
                [], ['wst'], 'd_wst')
            ps = newps()
            for j in range(8):
                for kc in range(8):
                    op('pe', lambda e: e.matmul(ps.t[:, j:j + 1], lhsT=wst[:, kc, j * P:(j + 1) * P],
                                                rhs=cact[:, kc:kc + 1], start=(kc == 0), stop=(kc == 7)),
                       ['wst', 'cact'], [ps.name])
            op('dve', lambda e: e.tensor_tensor(out=ppres[:, vi, :], in0=ps.t[:, 0:8], in1=bpp[:, vi, :], op=ALU.add),
               [ps.name, 'bpp'], ['ppres'])
        op('dve', lambda e: e.scalar_tensor_tensor(out=prm[:, c_gsm:c_gsm + 8], in0=ppres[:, 1, :], scalar=1.0,
                                                   in1=gpp[:, 0, :], op0=ALU.add, op1=ALU.mult),
           ['ppres', 'gpp'], ['prm'])
        op('dve', lambda e: e.tensor_copy(out=prm[:, c_shm:c_shm + 8], in_=ppres[:, 0, :]), ['ppres'], ['prm'])
        bsrc = [(w_ada_d, 2), (w_ada_d, 5), (w_adaf_d, 0), (w_adaf_d, 1)]
        XN = ['X%d' % t_ for t_ in range(NT)]
        xst = X[:].rearrange("p a b -> p (a b)").rearrange("p (a b) -> p a b", a=8)

        def bc_job(jid):
            vi, half = jid // 2, jid % 2
            wd, vcol = bsrc[vi]
            dma('sp', xst, wd[:, vcol * D + half * 512:vcol * D + (half + 1) * 512].rearrange("(k p) c -> p k c", p=P),
                [], XN, 'd_x0')
            ps = newps()
            for kc in range(8):
                op('pe', lambda e: e.matmul(ps.t[:], lhsT=crep[:, kc, :], rhs=xst[:, kc, :],
                                            start=(kc == 0), stop=(kc == 7)), XN + MIXN, [ps.name])
            sl = modbc[:, vi, half * 512:(half + 1) * 512]
            op('dve', lambda e: e.tensor_tensor(out=sl, in0=ps.t[:], in1=sl, op=ALU.add), [ps.name, 'modbc'], ['modbc'])
            if vi == 3:
                op('dve', lambda e: e.scalar_tensor_tensor(out=sl, in0=sl, scalar=1.0,
                                                           in1=gfin[:, half * 512:(half + 1) * 512],
                                                           op0=ALU.add, op1=ALU.mult), ['modbc'] + MIXN, ['modbc'])
        def pp_job(jid):
            vi, half = 2 + jid // 2, jid % 2
            vcol = pp_cols[vi]
            dma('sp', xst, w_ada_d[:, vcol * D + half * 512:vcol * D + (half + 1) * 512].rearrange("(k p) c -> p k c", p=P),
                [], XN, 'd_x0')
            ps = newps()
            for jj in range(4):
                for kc in range(8):
                    op('pe', lambda e: e.matmul(ps.t[:, jj:jj + 1], lhsT=xst[:, kc, jj * P:(jj + 1) * P],
                                                rhs=cact[:, kc:kc + 1], start=(kc == 0), stop=(kc == 7)),
                       XN + ['cact'], [ps.name])
            pn = 'ppres%d' % vi
            op('dve', lambda e: e.tensor_tensor(out=ppres[:, vi, half * 4:(half + 1) * 4], in0=ps.t[:, 0:4],
                                                in1=bpp[:, vi, half * 4:(half + 1) * 4], op=ALU.add),
               [ps.name, 'bpp'], [pn])
            if jid == 3:
                op('dve', lambda e: e.scalar_tensor_tensor(out=prmf[:, 0:8], in0=ppres[:, 3, :], scalar=1.0,
                                                           in1=gpp[:, 1, :], op0=ALU.add, op1=ALU.mult),
                   ['ppres3', 'gpp'], ['prmf'])
                op('dve', lambda e: e.tensor_copy(out=prmf[:, 8:16], in_=ppres[:, 2, :]), ['ppres2'], ['prmf'])
        bc_jobs = [('pp', j_) for j_ in range(4)] + [('bc', j_) for j_ in range(8)]

        def run_job(job):
            (pp_job if job[0] == 'pp' else bc_job)(job[1])
        gate_m_bc = modbc[:, 0, :]
        gate_f_bc = modbc[:, 1, :]
        sh_o_bc = modbc[:, 2, :]
        gs_o_bc = modbc[:, 3, :]

        cast_rows(wout_bf, w_out_d, D, 0, D, 'wout_bf', 'd_w2')
        cast_rows(wgu_bf, w_gu_d, D, 0, 2 * HID, 'wgu_bf', 'd_w3')
        cast_rows(wdn_bf, w_dn_d, HID, 0, D, 'wdn_bf', 'd_w4')
        dma('sp', wab[:], win_bf[:, 3584:3592].rearrange("(k p) c -> p k c", p=P), ['win_bf_b'], ['wab'], 'd_c4')
        S.fence()
        if stop <= 1:
            nblk_pre = 0
            nblk_own = 0
        if 'modbc' in dbg_d:
            dma('pool', dbg_d['modbc'], modbc[:], ['modbc'], [], 'd_dbg')
            dma('pool', dbg_d['prm'], prm[:, 0:46], ['prm'], [], 'd_dbg')

        def win_src(g):
            return ('win_bf_a' if g < 3 else 'win_bf_b',
                    win_bf[:, g * 512:(g + 1) * 512].rearrange("(k p) c -> p k c", p=P), 8, 512)

        def wout_src(hf):
            return ('wout_bf', wout_bf[:, hf * 512:(hf + 1) * 512].rearrange("(k p) c -> p k c", p=P), 8, 512)

        def wgu_src(g, up):
            c0 = (HID if up else 0) + g * 512
            w = min(512, HID - g * 512)
            return ('wgu_bf', wgu_bf[:, c0:c0 + w].rearrange("(k p) c -> p k c", p=P), 8, w)

        def wdn_src(hf, G):
            r0 = G * 8 * P
            n = min(8, NHC - G * 8)
            return ('wdn_bf', wdn_bf[r0:r0 + n * P, hf * 512:(hf + 1) * 512].rearrange("(k p) c -> p k c", p=P), n, 512)

        plan = []
        for blk in range(nblk_pre):
            if blk == nblk_pre - 1:
                plan += [win_src(2), win_src(1), win_src(3)]
            plan += [win_src(4), win_src(5)]
        for blk in range(nblk_own):
            plan += [win_src(2), win_src(1), win_src(0), win_src(6), win_src(3), win_src(4), win_src(5)]
            plan += [wout_src(0), wout_src(1)]
            for g in range(6):
                plan += [wgu_src(g, False), wgu_src(g, True)]
            for hf in range(2):
                for G in range(3):
                    plan += [wdn_src(hf, G)]
        rstate = {'issued': 0, 'used': 0}

        def ring_issue(upto):
            while rstate['issued'] < min(upto, len(plan)):
                i = rstate['issued']
                name, src, nk, w = plan[i]
                slot = i % NSLOT
                dma('sp', ring[slot][:, 0:nk, 0:w], src, [name], ['ring%d' % slot], 'd_ring%d' % slot)
                rstate['issued'] += 1

        def ring_get_n(k):
            i = rstate['used']
            ring_issue(i + NSLOT)
            rstate['used'] += k
            res = []
            for q_ in range(k):
                slot = (i + q_) % NSLOT
                res += [ring[slot], 'ring%d' % slot]
            return res

        def ring_get():
            return ring_get_n(1)

        def load_x(src_blk_ap):
            for tt in range(NT):
                dma('sp', X[:, tt, :], src_blk_ap[:, tt, :], [], ['X%d' % tt], 'd_x%d' % tt)

        def norm_T(c_gs, c_sh, pt=None, pn='prm'):
            pt = prm if pt is None else pt
            for tt in range(NT):
                op('act', lambda e: e.activation(out=xsb[:, tt, :], in_=X[:, tt, :], func=AF.Square,
                                                 accum_out=ssx[:, tt:tt + 1]), ['X%d' % tt], ['xsb%d' % tt, 'ssx'])
            op('act', lambda e: e.activation(out=ssx[:, 0:4], in_=ssx[:, 0:4], func=AF.Ln, scale=1.0 / D, bias=EPS),
               ['ssx'], ['ssx'])
            op('act', lambda e: e.activation(out=ssx[:, 0:4], in_=ssx[:, 0:4], func=AF.Exp, scale=-0.5),
               ['ssx'], ['ssx'])
            for tt in range(NT):
                if tt % 2 == 0:
                    op('dve', lambda e: e.tensor_scalar(out=xsb[:, tt, :], in0=X[:, tt, :], scalar1=ssx[:, tt:tt + 1],
                                                        scalar2=None, op0=ALU.mult), ['X%d' % tt, 'ssx'], ['xsb%d' % tt])
                else:
                    op('act', lambda e: e.activation(out=xsb[:, tt, :], in_=X[:, tt, :], func=AF.Copy,
                                                     scale=ssx[:, tt:tt + 1]), ['X%d' % tt, 'ssx'], ['xsb%d' % tt])
            for kp in range(4):
                ps = newps()
                for k2 in range(2):
                    kc = kp * 2 + k2
                    for tt in range(NT):
                        op('pe', lambda e: e.transpose(out=ps.b[:, k2 * 512 + tt * P:k2 * 512 + (tt + 1) * P],
                                                       in_=xsb[:, tt, kc * P:(kc + 1) * P], identity=identb_t[:]),
                           ['xsb%d' % tt, 'identb'], [ps.name])
                for k2 in range(2):
                    kc = kp * 2 + k2
                    src = ps.b[:, k2 * 512:(k2 + 1) * 512]
                    if kp % 2 == 0:
                        op('act', lambda e: e.activation(out=hT[:, kc, :], in_=src, func=AF.Identity,
                                                         scale=pt[:, c_gs + kc:c_gs + kc + 1],
                                                         bias=pt[:, c_sh + kc:c_sh + kc + 1]),
                           [ps.name, pn], ['hT%d' % kc])
                    else:
                        op('dve', lambda e: e.tensor_scalar(out=hT[:, kc, :], in0=src,
                                                            scalar1=pt[:, c_gs + kc:c_gs + kc + 1],
                                                            scalar2=pt[:, c_sh + kc:c_sh + kc + 1],
                                                            op0=ALU.mult, op1=ALU.add), [ps.name, pn], ['hT%d' % kc])

        def inproj(wt, wname, j):
            ps = newps()
            for kc in range(8):
                op('pe', lambda e: e.matmul(ps.t[:], lhsT=wt[:, kc, j * P:(j + 1) * P], rhs=hT[:, kc, :],
                                            start=(kc == 0), stop=(kc == 7)), [wname, 'hT%d' % kc], [ps.name])
            return ps

        def silu_from_ps(ps, out_ap, tmp_ap, rd, wr, wr_tmp, mul_eng='dve'):
            op('act', lambda e: e.activation(out=tmp_ap, in_=ps, func=AF.Exp, scale=-1.0), rd, wr_tmp)
            op('act', lambda e: e.activation(out=tmp_ap, in_=tmp_ap, func=AF.Ln, bias=1.0), wr_tmp, wr_tmp)
            op('act', lambda e: e.activation(out=tmp_ap, in_=tmp_ap, func=AF.Exp, scale=-1.0), wr_tmp, wr_tmp)
            op('dve', lambda e: e.tensor_tensor(out=out_ap, in0=ps, in1=tmp_ap, op=ALU.mult), rd + wr_tmp, wr)

        def conv_mixer(full, last_pre):
            wt, wn = ring_get()
            for j in range(4):
                ps = inproj(wt, wn, j)
                op('act', lambda e: e.activation(out=hcb[:, j, :], in_=ps.t[:], func=AF.Copy), [ps.name], ['oT%d' % j])
            wt, wn = ring_get()
            for j in range(4):
                ps = inproj(wt, wn, j)
                op('dve', lambda e: e.tensor_tensor(out=ucv[:, j, 8:8 + TB], in0=ps.t[:], in1=hcb[:, j, :], op=ALU.mult),
                   [ps.name, 'oT%d' % j], ['ucv'])
            if full:
                for j in range(4):
                    ps = newps()
                    for tap in range(3):
                        op('pe', lambda e: e.matmul(ps.t[:], lhsT=diagm[:, j, tap, :], rhs=ucv[:, j, 6 + tap:6 + tap + TB],
                                                    start=(tap == 0), stop=(tap == 2)), ['diagm', 'ucv'], [ps.name])
                    op('act', lambda e: e.activation(out=hcb[:, j, :], in_=ps.t[:], func=AF.Copy), [ps.name], ['oT%d' % j])
            if last_pre:
                op('pool', lambda e: e.tensor_scalar(out=ucv[:, :, 6:8], in0=ucv[:, :, TB + 6:TB + 8],
                                                     scalar1=prm[:, c_pm:c_pm + 1], scalar2=0.0, op0=ALU.mult, op1=ALU.add),
                   ['ucv', 'prm'], ['ucv'])
            else:
                op('pool', lambda e: e.tensor_copy(out=ucv[:, :, 6:8], in_=ucv[:, :, TB + 6:TB + 8]), ['ucv'], ['ucv'])
            if not full:
                return
            wt, wn = ring_get()
            for j in range(4):
                ps = inproj(wt, wn, j)
                op('dve', lambda e: e.tensor_tensor(out=yaj[:], in0=ps.t[:], in1=hcb[:, j, :], op=ALU.mult),
                   [ps.name, 'oT%d' % j], ['yaj'])
                op('act', lambda e: e.activation(out=sqb[:], in_=yaj[:], func=AF.Square), ['yaj'], ['sqb'])
                ps2 = newps()
                op('pe', lambda e: e.matmul(ps2.t[:], lhsT=blk1_t[:], rhs=sqb[:], start=True, stop=True),
                   ['blk1', 'sqb'], [ps2.name])
                op('act', lambda e: e.activation(out=rst[:], in_=ps2.t[:], func=AF.Ln, scale=1.0 / 64, bias=EPS),
                   [ps2.name], ['rst'])
                op('act', lambda e: e.activation(out=rst[:], in_=rst[:], func=AF.Exp, scale=-0.5), ['rst'], ['rst'])
                op('dve', lambda e: e.scalar_tensor_tensor(out=mixT[:, j, :], in0=yaj[:],
                                                           scalar=prm[:, c_gconv + j:c_gconv + j + 1], in1=rst[:],
                                                           op0=ALU.mult, op1=ALU.mult), ['yaj', 'rst', 'prm'], ['mixT%d' % j])

        def z_gate():
            wt, wn = ring_get()
            for h in range(4):
                ps = inproj(wt, wn, h)
                silu_from_ps(ps.t[:], zs[:, h, :], rst[:], [ps.name], ['zs'], ['rst'])

        def qkv_in(need_q):
            for t in range(3):
                if t == 0 and not need_q:
                    continue
                wt, wn = ring_get()
                for h in range(4):
                    cc = t * 4 + h
                    ps = inproj(wt, wn, h)
                    if h % 2 == 0:
                        op('act', lambda e: e.activation(out=qkvp[:, cc, 8:8 + TB], in_=ps.t[:], func=AF.Copy),
                           [ps.name], ['qkvp%d' % cc])
                    else:
                        op('dve', lambda e: e.tensor_copy(out=qkvp[:, cc, 8:8 + TB], in_=ps.t[:]),
                           [ps.name], ['qkvp%d' % cc])

        def ab_in():
            ps = newps()
            for tt in range(NT):
                for kc in range(8):
                    op('pe', lambda e: e.matmul(ps.t[:, tt * 8:(tt + 1) * 8], lhsT=hT[:, kc, tt * P:(tt + 1) * P],
                                                rhs=wab[:, kc, :], start=(kc == 0), stop=(kc == 7)),
                       ['hT%d' % kc, 'wab'], [ps.name])
            op('dve', lambda e: e.tensor_copy(out=absb[:].rearrange("p a b -> p (a b)"), in_=ps.t[:, 0:32]),
               [ps.name], ['absb'])

        (I_XA, I_ABS, I_E1, I_L1, I_G, I_E2, I_BETA, I_NBETA, I_GC, I_GL, I_EGL, I_ECOL, I_KDS) = range(13)
        J_SSQ, J_SSK, J_RQ, J_RK, J_SQ, J_SKBG, J_SKD = range(13, 20)

        def sl16(i):
            return sc[:, i, :]

        def sl4(i, c):
            return sc[:, i, c * 4:(c + 1) * 4]

        def bch(a2, n=4):
            return bass.AP(a2.tensor, a2.offset, [list(a2.ap[0]), [0, n], list(a2.ap[1])])

        PSA = [PS(i) for i in range(4)]
        psB_i = [0]

        def newpsB():
            i = psB_i[0]
            psB_i[0] = (i + 1) % 3
            return PS(4 + i)

        def dn_scalars():
            dv = lambda f, r, w: op('dve', f, r, w)
            ac = lambda f, r, w: op('act', f, r, w)
            v3 = lambda i: sc[:, i, :].rearrange("p (a b) -> p a b", a=4)
            dv(lambda e: e.tensor_tensor(out=v3(I_XA), in0=absb[:, :, 0:4], in1=bch(prm[:, c_dtb:c_dtb + 4]), op=ALU.add),
               ['absb', 'prm'], ['sc_xa'])
            ac(lambda e: e.activation(out=sl16(I_ABS), in_=sl16(I_XA), func=AF.Abs), ['sc_xa'], ['sc_abs'])
            ac(lambda e: e.activation(out=sl16(I_E1), in_=sl16(I_ABS), func=AF.Exp, scale=-1.0), ['sc_abs'], ['sc_e1'])
            ac(lambda e: e.activation(out=sl16(I_L1), in_=sl16(I_E1), func=AF.Ln, bias=1.0), ['sc_e1'], ['sc_l1'])
            dv(lambda e: e.scalar_tensor_tensor(out=sl16(I_G), in0=sl16(I_XA), scalar=0.0, in1=sl16(I_L1),
                                                op0=ALU.max, op1=ALU.add), ['sc_xa', 'sc_l1'], ['sc_g'])
            dv(lambda e: e.tensor_tensor(out=v3(I_G), in0=v3(I_G), in1=bch(prm[:, c_negA:c_negA + 4]), op=ALU.mult),
               ['sc_g', 'prm'], ['sc_g'])
            ac(lambda e: e.activation(out=v3(I_E2), in_=absb[:, :, 4:8], func=AF.Exp, scale=-1.0), ['absb'], ['sc_e2'])
            dv(lambda e: e.tensor_scalar(out=sl16(I_E2), in0=sl16(I_E2), scalar1=1.0, scalar2=None, op0=ALU.add),
               ['sc_e2'], ['sc_e2'])
            dv(lambda e: e.reciprocal(out=sl16(I_BETA), in_=sl16(I_E2)), ['sc_e2'], ['sc_beta'])
            dv(lambda e: e.tensor_scalar(out=sl16(I_NBETA), in0=sl16(I_BETA), scalar1=-1.0, scalar2=None, op0=ALU.mult),
               ['sc_beta'], ['sc_nbeta'])
            ps = newps()
            op('pe', lambda e: e.matmul(ps.t[:, 0:16], lhsT=Uf, rhs=sl16(I_G), start=True, stop=True),
               ['cst', 'sc_g'], [ps.name])
            op('pe', lambda e: e.matmul(ps.t[:, 16:32], lhsT=onesf, rhs=sl16(I_G), start=True, stop=True),
               ['cst', 'sc_g'], [ps.name])
            dv(lambda e: e.tensor_copy(out=sl16(I_GC), in_=ps.t[:, 0:16]), [ps.name], ['sc_gc'])
            dv(lambda e: e.tensor_copy(out=sl16(I_GL), in_=ps.t[:, 16:32]), [ps.name], ['sc_gl'])
            ac(lambda e: e.activation(out=sl16(I_EGL), in_=sl16(I_GL), func=AF.Exp), ['sc_gl'], ['sc_egl'])
            ac(lambda e: e.activation(out=sl16(I_ECOL), in_=sl16(I_GC), func=AF.Exp), ['sc_gc'], ['sc_ecol'])
            dv(lambda e: e.tensor_tensor(out=sl16(I_KDS), in0=sl16(I_GL), in1=sl16(I_GC), op=ALU.subtract),
               ['sc_gl', 'sc_gc'], ['sc_kds'])
            ac(lambda e: e.activation(out=sl16(I_KDS), in_=sl16(I_KDS), func=AF.Exp), ['sc_kds'], ['sc_kds'])

        def prep_gen(c, need_q):
            ob = c % 2
            sfx = '_%d' % ob
            dv = lambda f, r, w: op('dve', f, r, w)
            ac = lambda f, r, w: op('act', f, r, w)
            po = lambda f, r, w: op('pool', f, r, w)
            types = [0, 1, 2] if need_q else [1, 2]
            gcb = bc(sl4(I_GC, c), P)
            def colbc(slot, h):
                a1 = sc[:, slot, c * 4 + h:c * 4 + h + 1]
                return bass.AP(a1.tensor, a1.offset, [list(a1.ap[0]), [0, P]])
            ps_gr = PSA[0]
            for h in range(4):
                op('pe', lambda e: e.transpose(out=ps_gr.t[:, h * P:(h + 1) * P], in_=colbc(I_GC, h), identity=identf),
                   ['sc_gc', 'cst'], [ps_gr.name])
            gr3 = ps_gr.v3()
            yield
            if need_q:
                ac(lambda e: e.activation(out=Erow, in_=gr3, func=AF.Exp), [ps_gr.name], ['Erow'])
            dv(lambda e: e.tensor_tensor(out=dm0, in0=gr3, in1=gcb, op=ALU.subtract), [ps_gr.name, 'sc_gc'], ['dm0'])
            ps_br = PSA[1]
            for h in range(4):
                op('pe', lambda e: e.transpose(out=ps_br.t[:, h * P:(h + 1) * P], in_=colbc(I_BETA, h), identity=identf),
                   ['sc_beta', 'cst'], [ps_br.name])
            yield
            dv(lambda e: e.tensor_scalar(out=dm1, in0=dm0, scalar1=0.0, scalar2=None, op0=ALU.max), ['dm0'], ['dm1'])
            po(lambda e: e.tensor_scalar(out=dm2, in0=dm0, scalar1=0.0, scalar2=-3.0e38, op0=ALU.min, op1=ALU.max), ['dm0'], ['dm2'])
            ac(lambda e: e.activation(out=dm1, in_=dm1, func=AF.Exp, scale=-1.0), ['dm1'], ['dm1'])
            ac(lambda e: e.activation(out=dm2, in_=dm2, func=AF.Exp), ['dm2'], ['dm2'])
            yield
            f2 = lambda a: a.rearrange("p a b -> p (a b)")
            po(lambda e: e.tensor_tensor(out=f2(dm1), in0=f2(dm1), in1=f2(m_cs[:]), op=ALU.mult), ['dm1', 'm_cs'], ['dm1'])
            po(lambda e: e.tensor_tensor(out=dm1, in0=dm1, in1=bc(sl4(I_NBETA, c), P), op=ALU.mult), ['dm1', 'sc_nbeta'], ['dm1'])
            po(lambda e: e.tensor_tensor(out=dm1b, in0=dm1, in1=bch(cmb[:, 0, :]), op=ALU.mult), ['dm1', 'cmb'], ['dm1b'])
            po(lambda e: e.tensor_tensor(out=f2(dm2), in0=f2(dm2), in1=f2(m_sc[:]), op=ALU.mult), ['dm2', 'm_sc'], ['dm2'])
            dv(lambda e: e.tensor_tensor(out=f2(dm0), in0=ps_br.t[:], in1=f2(m_scn[:]), op=ALU.mult),
               [ps_br.name, 'm_scn'], ['dm0'])
            po(lambda e: e.tensor_tensor(out=f2(dm0), in0=f2(dm0), in1=f2(dm2), op=ALU.mult), ['dm0', 'dm2'], ['dm0'])
            yield
            Tps = {}
            for ti, t in enumerate(types):
                ps = PSA[t]
                for h in range(4):
                    cc = t * 4 + h
                    for tap in range(4):
                        op('pe', lambda e: e.matmul(ps.t[:, h * P:(h + 1) * P], lhsT=diagq[:, cc, tap, :],
                                                    rhs=qkvp[:, cc, 5 + tap + c * P:5 + tap + (c + 1) * P],
                                                    start=(tap == 0), stop=(tap == 3)),
                           ['diagq', 'qkvp%d' % cc], [ps.name])
                silu_from_ps(ps.t[:], cs[:, t, :], e1t[:, ti % 2, :], [ps.name], ['cs%d' % t], ['e1t%d' % (ti % 2)])
                yield
            for t in types:
                ps = PSA[t]
                Tps[t] = ps
                for h in range(4):
                    op('pe', lambda e: e.transpose(out=ps.t[:, h * P:(h + 1) * P], in_=cs[:, t, h * P:(h + 1) * P],
                                                   identity=identf), ['cs%d' % t, 'cst'], [ps.name])
            yield
            for t, islot, rslot in ((0, J_SSQ, J_RQ), (1, J_SSK, J_RK)):
                if t not in types:
                    continue
                ac(lambda e: e.activation(out=sqj, in_=Tps[t].v3(), func=AF.Square), [Tps[t].name], ['e1t0'])
                dv(lambda e: e.tensor_reduce(out=sl4(islot, ob), in_=sqj, axis=AX.X, op=ALU.add), ['e1t0'], ['sc_ss%d' % t + sfx])
                ac(lambda e: e.activation(out=sl4(rslot, ob), in_=sl4(islot, ob), func=AF.Ln, bias=EPS),
                   ['sc_ss%d' % t + sfx], ['sc_r%d' % t + sfx])
                ac(lambda e: e.activation(out=sl4(rslot, ob), in_=sl4(rslot, ob), func=AF.Exp, scale=-0.5),
                   ['sc_r%d' % t + sfx], ['sc_r%d' % t + sfx])
                yield
            dv(lambda e: e.tensor_tensor(out=sl4(J_SKBG, ob), in0=sl4(J_RK, ob), in1=sl4(I_BETA, c), op=ALU.mult),
               ['sc_r1' + sfx, 'sc_beta'], ['sc_skbg' + sfx])
            dv(lambda e: e.tensor_tensor(out=sl4(J_SKBG, ob), in0=sl4(J_SKBG, ob), in1=sl4(I_ECOL, c), op=ALU.mult),
               ['sc_skbg' + sfx, 'sc_ecol'], ['sc_skbg' + sfx])
            dv(lambda e: e.tensor_tensor(out=sl4(J_SKD, ob), in0=sl4(J_RK, ob), in1=sl4(I_KDS, c), op=ALU.mult),
               ['sc_r1' + sfx, 'sc_kds'], ['sc_skd' + sfx])
            Tk = Tps[1].v3()
            Tv = Tps[2].v3()
            dv(lambda e: e.tensor_tensor(out=kh, in0=Tk, in1=bc(sl4(J_RK, ob), P), op=ALU.mult),
               [Tps[1].name, 'sc_r1' + sfx], ['kh'])
            dv(lambda e: e.tensor_tensor(out=kbg[ob], in0=Tk, in1=bc(sl4(J_SKBG, ob), P), op=ALU.mult),
               [Tps[1].name, 'sc_skbg' + sfx], ['kbg' + sfx])
            yield
            dv(lambda e: e.tensor_tensor(out=kdec[ob], in0=Tk, in1=bc(sl4(J_SKD, ob), P), op=ALU.mult),
               [Tps[1].name, 'sc_skd' + sfx], ['kdec' + sfx])
            dv(lambda e: e.tensor_tensor(out=vb[ob], in0=Tv, in1=bc(sl4(I_BETA, c), P), op=ALU.mult),
               [Tps[2].name, 'sc_beta'], ['vb' + sfx])
            if need_q:
                dv(lambda e: e.tensor_scalar(out=sl4(J_SQ, ob), in0=sl4(J_RQ, ob), scalar1=DK_SCALE, scalar2=None, op0=ALU.mult),
                   ['sc_r0' + sfx], ['sc_sq' + sfx])
                dv(lambda e: e.tensor_tensor(out=qh, in0=Tps[0].v3(), in1=bc(sl4(J_SQ, ob), P), op=ALU.mult),
                   [Tps[0].name, 'sc_sq' + sfx], ['qh'])
            yield
            ps_k = PSA[3]
            for h in range(4):
                op('pe', lambda e: e.transpose(out=ps_k.t[:, h * P:(h + 1) * P], in_=kh[:, h, :], identity=identf),
                   ['kh', 'cst'], [ps_k.name])
            ac(lambda e: e.activation(out=khT, in_=ps_k.v3(), func=AF.Copy), [ps_k.name], ['khT'])
            if need_q:
                ps_q = PSA[0]
                for h in range(4):
                    op('pe', lambda e: e.transpose(out=ps_q.t[:, h * P:(h + 1) * P], in_=qh[:, h, :], identity=identf),
                       ['qh', 'cst'], [ps_q.name])
                ac(lambda e: e.activation(out=qhT, in_=ps_q.v3(), func=AF.Copy), [ps_q.name], ['qhT'])
                dv(lambda e: e.tensor_tensor(out=qdT[ob], in0=ps_q.v3(), in1=Erow, op=ALU.mult), [ps_q.name, 'Erow'], ['qdT' + sfx])
            yield
            ps_G = PSA[1]
            for h in range(4):
                op('pe', lambda e: e.matmul(ps_G.t[:, h * P:(h + 1) * P], lhsT=khT[:, h, :], rhs=khT[:, h, :],
                                            start=True, stop=True), ['khT'], [ps_G.name])
            if need_q:
                ps_A = PSA[2]
                for h in range(4):
                    op('pe', lambda e: e.matmul(ps_A.t[:, h * P:(h + 1) * P], lhsT=khT[:, h, :], rhs=qhT[:, h, :],
                                                start=True, stop=True), ['khT', 'qhT'], [ps_A.name])
            dv(lambda e: e.tensor_tensor(out=Lbd[ob], in0=ps_G.v3(), in1=dm1b, op=ALU.mult), [ps_G.name, 'dm1b'], ['Lbd' + sfx])
            dv(lambda e: e.tensor_tensor(out=Mbd[ob], in0=ps_G.v3(), in1=dm0, op=ALU.mult), [ps_G.name, 'dm0'], ['Mbd' + sfx])
            dv(lambda e: e.tensor_tensor(out=Nf[ob], in0=ps_G.v3(), in1=dm1, op=ALU.mult), [ps_G.name, 'dm1'], ['Nf' + sfx])
            if need_q:
                dv(lambda e: e.tensor_tensor(out=attnT[ob], in0=ps_A.v3(), in1=dm2, op=ALU.mult), [ps_A.name, 'dm2'], ['attnT' + sfx])
            yield

        def inv_gen(c, need_q):
            ob = c % 2
            sfx = '_%d' % ob
            dv = lambda f, r, w: op('dve', f, r, w)
            ac = lambda f, r, w: op('act', f, r, w)
            idb3 = bch(identb_t[:])
            dv(lambda e: e.tensor_tensor(out=Qb, in0=Mbd[ob], in1=idb3, op=ALU.add), ['Mbd' + sfx, 'identb'], ['Q'])
            Lc, Mc, Lcn, Mcn = Lbd[ob], Mbd[ob], 'Lbd' + sfx, 'Mbd' + sfx
            for ki, k in enumerate((1, 2, 4, 8)):
                nx = ki % 2
                if k > 1:
                    psQ = newpsB()
                    for h in range(4):
                        op('pe', lambda e: e.matmul(psQ.t[:, h * P:(h + 1) * P], lhsT=Lc[:, h, :], rhs=Qb[:, h, :],
                                                    start=True, stop=True), [Lcn, 'Q'], [psQ.name])
                if k < 8:
                    psA = newpsB()
                    psB = newpsB()
                    for h in range(4):
                        op('pe', lambda e: e.matmul(psA.t[:, h * P:(h + 1) * P], lhsT=Mc[:, h, :], rhs=Lc[:, h, :],
                                                    start=True, stop=True), [Lcn, Mcn], [psA.name])
                    for h in range(4):
                        op('pe', lambda e: e.matmul(psB.t[:, h * P:(h + 1) * P], lhsT=Lc[:, h, :], rhs=Mc[:, h, :],
                                                    start=True, stop=True), [Lcn, Mcn], [psB.name])
                yield
                if k > 1:
                    dv(lambda e: e.tensor_tensor(out=Qb, in0=psQ.v3(), in1=Qb, op=ALU.add), [psQ.name, 'Q'], ['Q'])
                if k < 8:
                    ac(lambda e: e.activation(out=Lt[nx], in_=psA.v3(), func=AF.Copy), [psA.name], ['Lt%d' % nx])
                    ac(lambda e: e.activation(out=Mt[nx], in_=psB.v3(), func=AF.Copy), [psB.name], ['Mt%d' % nx])
                    Lc, Mc, Lcn, Mcn = Lt[nx], Mt[nx], 'Lt%d' % nx, 'Mt%d' % nx
                yield
            Xn, Wp, Xnn, Wpn = Lt[1], Mt[1], 'Lt1', 'Mt1'
            for lvl in (1, 2, 3):
                psT = newpsB()
                for h in range(4):
                    op('pe', lambda e: e.transpose(out=psT.b[:, h * P:(h + 1) * P], in_=Qb[:, h, :], identity=identb_t[:]),
                       ['Q', 'identb'], [psT.name])
                psW = newpsB()
                for h in range(4):
                    op('pe', lambda e: e.matmul(psW.t[:, h * P:(h + 1) * P], lhsT=Nf[ob][:, h, :], rhs=Qb[:, h, :],
                                                start=True, stop=True), ['Nf' + sfx, 'Q'], [psW.name])
                yield
                ac(lambda e: e.activation(out=Xn, in_=psT.b[:, 0:512].rearrange("p (a b) -> p a b", a=4), func=AF.Copy),
                   [psT.name], [Xnn])
                dv(lambda e: e.tensor_tensor(out=Wp, in0=psW.v3(), in1=bch(cmb[:, lvl, :]), op=ALU.mult),
                   [psW.name, 'cmb'], [Wpn])
                psZ = newpsB()
                for h in range(4):
                    op('pe', lambda e: e.matmul(psZ.t[:, h * P:(h + 1) * P], lhsT=Xn[:, h, :], rhs=Wp[:, h, :],
                                                start=True, stop=True), [Xnn, Wpn], [psZ.name])
                yield
                dv(lambda e: e.tensor_tensor(out=Qb, in0=psZ.v3(), in1=Qb, op=ALU.add), [psZ.name, 'Q'], ['Q'])
            ps_w = newpsB()
            ps_u = newpsB()
            for h in range(4):
                op('pe', lambda e: e.matmul(ps_w.t[:, h * P:(h + 1) * P], lhsT=kbg[ob][:, h, :], rhs=Qb[:, h, :],
                                            start=True, stop=True), ['kbg' + sfx, 'Q'], [ps_w.name])
            for h in range(4):
                op('pe', lambda e: e.matmul(ps_u.t[:, h * P:(h + 1) * P], lhsT=Qb[:, h, :], rhs=vb[ob][:, h, :],
                                            start=True, stop=True), ['vb' + sfx, 'Q'], [ps_u.name])
            yield
            ac(lambda e: e.activation(out=wT, in_=ps_w.v3(), func=AF.Copy), [ps_w.name], ['wT'])
            ac(lambda e: e.activation(out=u_sb, in_=ps_u.v3(), func=AF.Copy), [ps_u.name], ['u'])
            ps_p = PS(7)
            for h in range(4):
                op('pe', lambda e: e.matmul(ps_p.t[:, h * P:(h + 1) * P], lhsT=wT[:, h, :], rhs=Sbf[:, h, :],
                                            start=True, stop=True), ['wT', 'Sbf'], [ps_p.name])
            yield
            dv(lambda e: e.tensor_tensor(out=vnew, in0=u_sb, in1=ps_p.v3(), op=ALU.subtract), ['u', ps_p.name], ['vnew'])
            if need_q:
                ps_o = PS(3)
                for h in range(4):
                    op('pe', lambda e: e.matmul(ps_o.t[:, h * P:(h + 1) * P], lhsT=Sbf[:, h, :], rhs=qdT[ob][:, h, :],
                                                start=True, stop=False), ['Sbf', 'qdT' + sfx], [ps_o.name])
                    op('pe', lambda e: e.matmul(ps_o.t[:, h * P:(h + 1) * P], lhsT=vnew[:, h, :], rhs=attnT[ob][:, h, :],
                                                start=False, stop=True), ['vnew', 'attnT' + sfx], [ps_o.name])
            ps_s = PS(7)
            for h in range(4):
                op('pe', lambda e: e.matmul(ps_s.t[:, h * P:(h + 1) * P], lhsT=kdec[ob][:, h, :], rhs=vnew[:, h, :],
                                            start=True, stop=True), ['kdec' + sfx, 'vnew'], [ps_s.name])
            yield
            if need_q:
                ac(lambda e: e.activation(out=oT[:, :, c * P:(c + 1) * P], in_=ps_o.v3(), func=AF.Copy), [ps_o.name], ['oT0', 'oT1', 'oT2', 'oT3'])
            for h in range(4):
                dv(lambda e: e.scalar_tensor_tensor(out=S32[:, h, :], in0=S32[:, h, :],
                                                    scalar=sc[:, I_EGL, c * 4 + h:c * 4 + h + 1],
                                                    in1=ps_s.t[:, h * P:(h + 1) * P], op0=ALU.mult, op1=ALU.add),
                   ['S32', 'sc_egl', ps_s.name], ['S32'])
            ac(lambda e: e.activation(out=Sbf[:], in_=S32[:], func=AF.Copy), ['S32'], ['Sbf'])
            yield

        def run_gens(*gens):
            gens = [g for g in gens if g is not None]
            while gens:
                for g in list(gens):
                    try:
                        next(g)
                    except StopIteration:
                        gens.remove(g)

        def dn_block(need_q):
            dn_scalars()
            run_gens(prep_gen(0, need_q))
            for c in range(4):
                run_gens(inv_gen(c, need_q), prep_gen(c + 1, need_q) if c < 3 else None)

        def halo_qkv(last_pre, need_q):
            for cc in range(12):
                if cc < 4 and not need_q and not last_pre:
                    continue
                nm = 'qkvp%d' % cc
                eng = 'pool' if cc % 2 == 0 else 'dve'
                if last_pre:
                    op(eng, lambda e: e.tensor_scalar(out=qkvp[:, cc, 5:8], in0=qkvp[:, cc, TB + 5:TB + 8],
                                                      scalar1=prm[:, c_pm:c_pm + 1], scalar2=0.0, op0=ALU.mult, op1=ALU.add),
                       [nm, 'prm'], [nm])
                else:
                    op(eng, lambda e: e.tensor_copy(out=qkvp[:, cc, 5:8], in_=qkvp[:, cc, TB + 5:TB + 8]), [nm], [nm])

        def dn_out():
            rtmp = [cs[:, 0, :], cs[:, 1, :], cs[:, 2, :], e1t[:, 0, :]]
            rnm = ['cs0', 'cs1', 'cs2', 'e1t0']
            stmp = [t_.rearrange("p a b -> p (a b)") for t_ in (Lt[0], Lt[1], Mt[0], Mt[1])]
            snm = ['Lt0', 'Lt1', 'Mt0', 'Mt1']
            for h in range(4):
                r_, rn, q_, qn = rtmp[h], rnm[h], stmp[h], snm[h]
                op('act', lambda e: e.activation(out=q_, in_=oT[:, h, :], func=AF.Square), ['oT%d' % h], [qn])
                ps = newps()
                op('pe', lambda e: e.matmul(ps.t[:], lhsT=onesb_t[:], rhs=q_, start=True, stop=True),
                   ['onesb', qn], [ps.name])
                op('act', lambda e: e.activation(out=r_, in_=ps.t[:], func=AF.Ln, scale=1.0 / P, bias=EPS),
                   [ps.name], [rn])
                op('act', lambda e: e.activation(out=r_, in_=r_, func=AF.Exp, scale=-0.5), [rn], [rn])
                op('dve', lambda e: e.scalar_tensor_tensor(out=r_, in0=oT[:, h, :],
                                                           scalar=prm[:, c_gdn:c_gdn + 1], in1=r_,
                                                           op0=ALU.mult, op1=ALU.mult), ['oT%d' % h, rn, 'prm'], [rn])
                op('pool', lambda e: e.tensor_tensor(out=mixT[:, 4 + h, :], in0=r_, in1=zs[:, h, :], op=ALU.mult),
                   [rn, 'zs'], ['mixT%d' % (4 + h)])

        def out_proj():
            w0, n0, w1, n1 = ring_get_n(2)
            for tt in range(NT):
                for hf, (wt, wn) in enumerate(((w0, n0), (w1, n1))):
                    ps = newps()
                    for kc in range(8):
                        op('pe', lambda e: e.matmul(ps.t[:], lhsT=mixT[:, kc, tt * P:(tt + 1) * P], rhs=wt[:, kc, :],
                                                    start=(kc == 0), stop=(kc == 7)), ['mixT%d' % kc, wn], [ps.name])
                    yt, yn = (yaj, 'yaj') if hf == 0 else (yaj2, 'yaj2')
                    op('dve', lambda e: e.tensor_tensor(out=yt[:], in0=ps.t[:], in1=gate_m_bc[:, hf * 512:(hf + 1) * 512],
                                                        op=ALU.mult), [ps.name, 'modbc'], [yn])
                    xs_ = X[:, tt, hf * 512:(hf + 1) * 512]
                    op('pool' if hf == 0 else 'dve', lambda e: e.tensor_tensor(out=xs_, in0=xs_, in1=yt[:], op=ALU.add),
                       ['X%d' % tt, yn], ['X%d' % tt])

        def ffn():
            for g in range(6):
                wg, ng, wu, nu = ring_get_n(2)
                nj = min(4, NHC - g * 4)
                for j in range(nj):
                    J = g * 4 + j
                    psg = newps()
                    psu = newps()
                    for kc in range(8):
                        op('pe', lambda e: e.matmul(psg.t[:], lhsT=wg[:, kc, j * P:(j + 1) * P], rhs=hT[:, kc, :],
                                                    start=(kc == 0), stop=(kc == 7)), [ng, 'hT%d' % kc], [psg.name])
                    for kc in range(8):
                        op('pe', lambda e: e.matmul(psu.t[:], lhsT=wu[:, kc, j * P:(j + 1) * P], rhs=hT[:, kc, :],
                                                    start=(kc == 0), stop=(kc == 7)), [nu, 'hT%d' % kc], [psu.name])
                    op('act', lambda e: e.activation(out=rst[:], in_=psg.t[:], func=AF.Silu), [psg.name], ['rst'])
                    op('dve', lambda e: e.tensor_tensor(out=actT[:, J, :], in0=psu.t[:], in1=rst[:], op=ALU.mult),
                       [psu.name, 'rst'], ['actT%d' % J])
            for hf in range(2):
                accs = [newps() for _ in range(NT)]
                for G in range(3):
                    wd, nd = ring_get()
                    n = min(8, NHC - G * 8)
                    for tt in range(NT):
                        for j in range(n):
                            J = G * 8 + j
                            op('pe', lambda e: e.matmul(accs[tt].t[:], lhsT=actT[:, J, tt * P:(tt + 1) * P], rhs=wd[:, j, :],
                                                        start=(J == 0), stop=(J == NHC - 1)), ['actT%d' % J, nd], [accs[tt].name])
                for tt in range(NT):
                    yt, yn = (yaj, 'yaj') if tt % 2 == 0 else (yaj2, 'yaj2')
                    op('dve', lambda e: e.tensor_tensor(out=yt[:], in0=accs[tt].t[:],
                                                        in1=gate_f_bc[:, hf * 512:(hf + 1) * 512], op=ALU.mult),
                       [accs[tt].name, 'modbc'], [yn])
                    xs_ = X[:, tt, hf * 512:(hf + 1) * 512]
                    op('pool' if tt % 2 == 0 else 'dve', lambda e: e.tensor_tensor(out=xs_, in0=xs_, in1=yt[:], op=ALU.add),
                       ['X%d' % tt, yn], ['X%d' % tt])

        def final_norm():
            for tt in range(NT):
                op('act', lambda e: e.activation(out=xsb[:, tt, :], in_=X[:, tt, :], func=AF.Square,
                                                 accum_out=ssx[:, 4 + tt:5 + tt]), ['X%d' % tt], ['xsb%d' % tt, 'ssxf'])
            op('act', lambda e: e.activation(out=ssx[:, 4:8], in_=ssx[:, 4:8], func=AF.Ln, scale=1.0 / D, bias=EPS),
               ['ssxf'], ['ssxf'])
            op('act', lambda e: e.activation(out=ssx[:, 4:8], in_=ssx[:, 4:8], func=AF.Exp, scale=-0.5), ['ssxf'], ['ssxf'])
            for tt in range(NT):
                op('dve', lambda e: e.scalar_tensor_tensor(out=X[:, tt, :], in0=X[:, tt, :], scalar=ssx[:, 4 + tt:5 + tt],
                                                           in1=gs_o_bc, op0=ALU.mult, op1=ALU.mult),
                   ['X%d' % tt, 'ssxf', 'modbc'], ['X%d' % tt])
                op('pool' if tt % 2 == 0 else 'dve', lambda e: e.tensor_tensor(out=X[:, tt, :], in0=X[:, tt, :], in1=sh_o_bc, op=ALU.add),
                   ['X%d' % tt, 'modbc'], ['X%d' % tt])

        def dump(name, ap_, rd):
            if name in dbg_d:
                dma('pool', dbg_d[name], ap_, rd, [], 'd_dbg')

        xpre_v = x_pre.rearrange("(n t p) d -> n p t d", t=NT, p=P)
        xown_v = x_own.rearrange("(n t p) d -> n p t d", t=NT, p=P)
        out_v = out_d.rearrange("(n t p) d -> n p t d", t=NT, p=P)

        S.mark('prologue')
        for blk in range(nblk_pre):
            last = (blk == nblk_pre - 1)
            if MODEL_MARKS:
                S.mark('pre%d' % blk)
            if EXP_PRE:
                S.REN = {n_: n_ + '_%d' % (blk % 2) for n_ in EXP_PRE}
            load_x(xpre_v[blk])
            norm_T(c_gsm, c_shm)
            njob = -(-len(bc_jobs) // (nblk_pre - blk))
            for _ in range(njob):
                run_job(bc_jobs.pop(0))
            if last:
                conv_mixer(False, True)
            qkv_in(last)
            ab_in()
            dn_block(False)
            halo_qkv(last, False)
            if last:
                op('dve', lambda e: e.tensor_scalar(out=S32[:], in0=S32[:], scalar1=prm[:, c_pm:c_pm + 1], scalar2=None,
                                                    op0=ALU.mult), ['S32', 'prm'], ['S32'])
                op('act', lambda e: e.activation(out=Sbf[:], in_=S32[:], func=AF.Copy), ['S32'], ['Sbf'])

        while bc_jobs:
            run_job(bc_jobs.pop(0))
        if stop <= 2:
            nblk_own = 0
        S.REN = {}
        for blk in range(nblk_own):
            if EXPERIMENT:
                S.REN = {n_: n_ + '_%d' % (blk % 2) for n_ in EXPERIMENT}
            if MODEL_MARKS:
                S.mark('own%d' % blk)
            load_x(xown_v[blk])
            norm_T(c_gsm, c_shm)
            conv_mixer(True, False)
            z_gate()
            qkv_in(True)
            ab_in()
            if MODEL_MARKS:
                S.mark('  inproj')
            if stop <= 3:
                break
            dn_block(True)
            halo_qkv(False, True)
            if MODEL_MARKS:
                S.mark('  dn')
            if stop <= 4:
                break
            dn_out()
            if blk == 0:
                dump('mixT', mixT[:], MIXN)
            out_proj()
            if blk == 0:
                dump('x1', X[:], XN)
            if MODEL_MARKS:
                S.mark('  outproj')
            if stop <= 5:
                break
            norm_T(0, 8, prmf, 'prmf')
            if stop <= 6:
                break
            ffn()
            if MODEL_MARKS:
                S.mark('  ffn')
            if stop <= 7:
                break
            final_norm()
            for tt in range(NT):
                dma('sp', out_v[blk][:, tt, :], X[:, tt, :], ['X%d' % tt], [], 'd_out%d' % tt)
        S.finish('sp')
        build_nc.stats = (S.nops, S.nwaits)
        build_nc.sim_time = S.sim_time
        build_nc.busy = S.busy
        build_nc.marks = getattr(S, 'marks', [])
        build_nc.S = S
    return nc


def _pp(v):
    return np.ascontiguousarray(np.asarray(v, np.float32).reshape(8, 128).T)


def _consts():
    p = np.arange(128)
    ident = np.eye(128, dtype=np.float32)
    ones = np.ones((128, 128), np.float32)
    U = (p[:, None] <= p[None, :]).astype(np.float32)
    blk = ((p[:, None] // 64) == (p[None, :] // 64)).astype(np.float32)
    strict = (p[:, None] > p[None, :]).astype(np.float32)
    return np.ascontiguousarray(np.stack([ident, ones, U, blk, strict], axis=1))


def _cmask():
    p = np.arange(128)
    r, c = p[:, None], p[None, :]
    ms = [((r // 16) == (c // 16)).astype(np.float32)]
    for b in (16, 32, 64):
        ms.append((((r // (2 * b)) == (c // (2 * b))) & ((r // b) % 2 == 0) & ((c // b) % 2 == 1)).astype(np.float32))
    return np.ascontiguousarray(np.stack(ms, axis=1))


def make_in_maps(inp, nblk_own=8, nblk_pre=8):
    f = lambda a: np.ascontiguousarray(np.asarray(a, np.float32))
    x = f(inp['x'])
    c = f(inp['c'])
    w_ada = f(inp['w_ada'][0])
    b_ada = f(inp['b_ada'][0])
    w_adaf = f(inp['w_ada_final'])
    b_adaf = f(inp['b_ada_final'])
    bsl = lambda i: b_ada[i * D:(i + 1) * D]
    b_pp = np.ascontiguousarray(np.stack([_pp(bsl(0)), _pp(bsl(1)), _pp(bsl(3)), _pp(bsl(4))], axis=1))
    bb = np.stack([bsl(2), bsl(5), b_adaf[0:D], b_adaf[D:2 * D]], axis=0)
    b_bc = np.ascontiguousarray(np.broadcast_to(bb[None], (P, 4, D)))
    gfin_bc = np.ascontiguousarray(np.broadcast_to(f(inp['g_norm_final'])[None], (P, D)))
    g_pp = np.ascontiguousarray(np.stack([_pp(inp['g_norm_mix'][0]), _pp(inp['g_norm_ffn'][0])], axis=1))
    cwm = f(inp['conv_w_mix'][0])
    cw_mix = np.ascontiguousarray(cwm.T.reshape(4, 128, 3).transpose(1, 0, 2))
    cwq = f(inp['conv_w_qkv'][0])
    cw_qkv = np.ascontiguousarray(cwq.T.reshape(12, 128, 4).transpose(1, 0, 2))
    g_conv = np.ascontiguousarray(f(inp['g_conv_out'][0]).reshape(4, 128).T)
    g_dn = np.ascontiguousarray(f(inp['g_dn_out'][0]).reshape(128, 1))
    alog_bc = np.ascontiguousarray(np.broadcast_to(f(inp['a_log'][0])[None], (P, 4)))
    dtb_bc = np.ascontiguousarray(np.broadcast_to(f(inp['dt_bias'][0])[None], (P, 4)))
    shared = {
        "w_ada": w_ada, "w_ada_final": w_adaf, "b_pp": b_pp, "b_bc": b_bc, "gfin_bc": gfin_bc, "g_pp": g_pp,
        "w_in": f(inp['w_in'][0]), "w_out": f(inp['w_out'][0]), "w_gu": f(inp['w_gate_up'][0]),
        "w_down": f(inp['w_down'][0]), "cw_mix": cw_mix, "cw_qkv": cw_qkv, "g_conv": g_conv, "g_dn": g_dn,
        "alog_bc": alog_bc, "dtb_bc": dtb_bc, "consts": _consts(), "cmask": _cmask(),
    }
    n_own = nblk_own * TB
    n_pre = max(nblk_pre, 1) * TB
    S_ = x.shape[1]
    halfS = S_ // 2
    maps = []
    for core in range(8):
        b, half = core // 2, core % 2
        m = dict(shared)
        s0 = half * halfS
        m["x_own"] = np.ascontiguousarray(x[b, s0:s0 + n_own])
        p0 = s0 - n_pre if half == 1 else 0
        m["x_pre"] = np.ascontiguousarray(x[b, p0:p0 + n_pre])
        m["pmask"] = np.full((P, 1), float(half), np.float32)
        m["cT"] = _pp(c[b])
        maps.append(m)
    return maps


_NC_CACHE = {}


def kernel(**inputs):
    if 'nc' not in _NC_CACHE:
        _NC_CACHE['nc'] = build_nc()
    nc = _NC_CACHE['nc']
    maps = make_in_maps(inputs)
    res = run_bass_kernel_spmd(nc, maps, core_ids=list(range(8)))
    B, S_, _ = inputs['x'].shape
    out = np.empty((B, S_, D), np.float32)
    halfS = S_ // 2
    for core in range(8):
        b, half = core // 2, core % 2
        out[b, half * halfS:(half + 1) * halfS] = res.results[core]["out"]
    return out
```
